# Optimizing a Trainium2 kernel written in Bass

```python
import math
import jax
import jax.numpy as jnp
from jax import lax
import numpy as np

D_MODEL = 1024
BATCH = 4
SEQ = 8192
DEPTH = 2
DEC_BATCH = 32
DEC_SEQ = 1
PAST_LEN = 16384
PAGE_SIZE = 128

WINDOWS = (128, 512, 2048)
DILATIONS = (1, 4, 16)
N_GROUPS = 3
H_G = 8
DH_A = 64
A_GROUP = H_G * DH_A
A_QKV = N_GROUPS * A_GROUP
Q_BLOCK = 128
NH_B = 4
DK_B = 256
B_WIDTH = NH_B * DK_B
CONV_B = 4
MLSTM_CHUNK = 64
D_FF = 2816
CONV_F = 3
NUM_BUCKETS = 32
MAX_DISTANCE = 2048
RMS_EPS = 1e-6
IN_SIZES = (A_QKV, A_QKV, A_QKV, B_WIDTH, B_WIDTH, B_WIDTH, B_WIDTH, NH_B, NH_B, D_MODEL, D_MODEL)
IN_COLS = 3 * A_QKV + 4 * B_WIDTH + 2 * NH_B + 2 * D_MODEL

kernel_name = 'dilated_attn_mlstm_gated_hybrid_step'


def _rms_norm(x, g):
    xf = x.astype(jnp.float32)
    y = xf * lax.rsqrt(jnp.mean(xf * xf, axis=-1, keepdims=True) + RMS_EPS)
    return (y * g.astype(jnp.float32)).astype(x.dtype)


def _t5_bucket(dist):
    max_exact = NUM_BUCKETS // 2
    df = jnp.maximum(dist, 1).astype(jnp.float32)
    large = max_exact + (jnp.log(df / max_exact) / math.log(MAX_DISTANCE / max_exact)
                         * (NUM_BUCKETS - max_exact)).astype(jnp.int32)
    large = jnp.minimum(large, NUM_BUCKETS - 1)
    return jnp.where(dist < max_exact, dist, large)


def _causal_dwconv(u, buf, w, b):
    K = w.shape[0]
    T = u.shape[1]
    full = jnp.concatenate([buf.astype(u.dtype), u], axis=1)
    y = b + sum(full[:, i:i + T] * w[i] for i in range(K))
    return y, full[:, T:]


def _dilated_attn_prompt(q, k, v, bias_g, window, dil):
    B, S, H, Dh = q.shape
    span = window // dil
    U = S // dil
    nb = -(-U // Q_BLOCK)
    Up = nb * Q_BLOCK

    def to_blocks(t):
        t = t.reshape(B, U, dil, H, Dh).transpose(0, 2, 1, 3, 4)
        t = jnp.pad(t, ((0, 0), (0, 0), (0, Up - U), (0, 0), (0, 0)))
        return t.reshape(B, dil, nb, Q_BLOCK, H, Dh)

    def with_prev(t):
        prev = jnp.pad(t, ((0, 0), (0, 0), (1, 0), (0, 0), (0, 0), (0, 0)))[:, :, :-1]
        return jnp.concatenate([prev, t], axis=3)

    qb = to_blocks(q)
    kb = with_prev(to_blocks(k))
    vb = with_prev(to_blocks(v))
    qi = jnp.arange(Q_BLOCK)[:, None]
    ki = jnp.arange(2 * Q_BLOCK)[None, :]
    rel = qi + Q_BLOCK - ki
    band = (rel >= 0) & (rel <= span)
    has_prev = (jnp.arange(nb) > 0)[:, None, None]
    mask = band[None] & (has_prev | (ki >= Q_BLOCK)[None])
    bias = bias_g[_t5_bucket(jnp.maximum(rel, 0) * dil)].transpose(2, 0, 1)
    s = jnp.einsum('brnqhc,brnkhc->brnhqk', qb, kb).astype(jnp.float32) * (Dh ** -0.5) + bias
    s = jnp.where(mask[None, None, :, None], s, -jnp.inf)
    lse = jax.nn.logsumexp(s, axis=-1)
    p = jnp.exp(s - lse[..., None]).astype(v.dtype)
    o = jnp.einsum('brnhqk,brnkhc->brnqhc', p, vb)
    o = o.reshape(B, dil, Up, H, Dh)[:, :, :U].transpose(0, 2, 1, 3, 4).reshape(B, S, H, Dh)
    lse = lse.transpose(0, 1, 2, 4, 3).reshape(B, dil, Up, H)[:, :, :U]
    lse = lse.transpose(0, 2, 1, 3).reshape(B, S, H)
    return o, lse


def _dilated_attn_sample(q, k_all, v_all, bias_g, window, dil, n_buf):
    T = q.shape[1]
    Dh = q.shape[-1]
    span = window // dil
    j = jnp.arange(span + 1)
    idx = (n_buf + jnp.arange(T))[:, None] - j[None, :] * dil
    valid = idx >= 0
    idxc = jnp.maximum(idx, 0)
    kg = k_all[:, idxc]
    vg = v_all[:, idxc]
    bias = bias_g[_t5_bucket(j * dil)].T
    s = jnp.einsum('bthc,btjhc->bthj', q, kg).astype(jnp.float32) * (Dh ** -0.5) + bias
    s = jnp.where(valid[None, :, None, :], s, -jnp.inf)
    lse = jax.nn.logsumexp(s, axis=-1)
    p = jnp.exp(s - lse[..., None]).astype(vg.dtype)
    o = jnp.einsum('bthj,btjhc->bthc', p, vg)
    return o, lse


def _combine_groups(outs, lses):
    alpha = jax.nn.softmax(jnp.stack(lses, 0), axis=0)
    o = jnp.stack(outs, 0).astype(jnp.float32)
    return jnp.sum(alpha[..., None] * o, axis=0)


def _mlstm_chunk(carry, inp):
    C, n, m = carry
    q, k, v, ig, lf = inp
    L = q.shape[2]
    b = jnp.cumsum(lf, axis=-1)
    causal = jnp.tril(jnp.ones((L, L), dtype=bool))
    logd = jnp.where(causal, b[..., :, None] - b[..., None, :] + ig[..., None, :], -jnp.inf)
    inter = b + m[..., None]
    m_t = jnp.maximum(inter, jnp.max(logd, axis=-1))
    dw = jnp.exp(logd - m_t[..., None])
    iw = jnp.exp(inter - m_t)
    qk = jnp.einsum('bhtc,bhsc->bhts', q, k) * dw
    num = iw[..., None] * jnp.einsum('bhtc,bhcv->bhtv', q, C) + jnp.einsum('bhts,bhsv->bhtv', qk, v)
    den = iw * jnp.einsum('bhtc,bhc->bht', q, n) + jnp.sum(qk, axis=-1)
    h = num / jnp.maximum(jnp.abs(den), jnp.exp(-m_t))[..., None]
    m_new = m_t[..., -1]
    wc = jnp.exp(b[..., -1:] - b + ig - m_new[..., None])
    decay = jnp.exp(b[..., -1] + m - m_new)
    C_new = decay[..., None, None] * C + jnp.einsum('bhs,bhsc,bhsv->bhcv', wc, k, v)
    n_new = decay[..., None] * n + jnp.einsum('bhs,bhsc->bhc', wc, k)
    return (C_new, n_new, m_new), h


def _mlstm(q, k, v, ig, lf, C, n, m, chunk):
    B, T = q.shape[:2]
    nc = T // chunk

    def to_chunks(t):
        t = t.reshape((B, nc, chunk) + t.shape[2:])
        return jnp.moveaxis(jnp.moveaxis(t, 3, 2), 1, 0)

    xs = (to_chunks(q), to_chunks(k), to_chunks(v), to_chunks(ig), to_chunks(lf))
    (C, n, m), h = lax.scan(_mlstm_chunk, (C, n, m), xs)
    h = jnp.moveaxis(jnp.moveaxis(h, 0, 1), 2, 3).reshape(B, T, NH_B, DK_B)
    return h, C, n, m


def _layer(x, prm, rel_bias, past):
    B, T, _ = x.shape
    f32 = jnp.float32
    h = _rms_norm(x, prm['norm1_g'])
    proj = h @ prm['w_in']
    pts = np.cumsum(IN_SIZES)[:-1].tolist()
    q_a, k_a, v_a, q_b, k_b, v_b, o_b, i_b, f_b, g_a, g_b = jnp.split(proj, pts, axis=-1)

    qa = q_a.reshape(B, T, N_GROUPS, H_G, DH_A)
    ka = k_a.reshape(B, T, N_GROUPS, H_G, DH_A)
    va = v_a.reshape(B, T, N_GROUPS, H_G, DH_A)
    outs, lses, new_kv = [], [], []
    for g in range(N_GROUPS):
        window, dil = WINDOWS[g], DILATIONS[g]
        bias_g = rel_bias[:, g * H_G:(g + 1) * H_G]
        qg, kg, vg = qa[:, :, g], ka[:, :, g], va[:, :, g]
        if past is None:
            o, lse = _dilated_attn_prompt(qg, kg, vg, bias_g, window, dil)
            keep = min(window, T)
            new_kv.append(jnp.stack([kg[:, T - keep:], vg[:, T - keep:]], axis=2))
        else:
            cache = past['kv'][g]
            k_all = jnp.concatenate([cache[:, :, 0].astype(kg.dtype), kg], axis=1)
            v_all = jnp.concatenate([cache[:, :, 1].astype(vg.dtype), vg], axis=1)
            o, lse = _dilated_attn_sample(qg, k_all, v_all, bias_g, window, dil, cache.shape[1])
            new_kv.append(jnp.stack([kg, vg], axis=2))
        outs.append(o)
        lses.append(lse)
    a_out = _combine_groups(outs, lses).reshape(B, T, A_GROUP).astype(x.dtype)

    qk_pre = jnp.concatenate([q_b, k_b], axis=-1)
    if past is None:
        mbuf = jnp.zeros((B, CONV_B - 1, 2 * B_WIDTH), x.dtype)
    else:
        mbuf = past['mconv']
    qk_c, new_mconv = _causal_dwconv(qk_pre, mbuf, prm['mconv_w'], prm['mconv_b'])
    qk_c = jax.nn.silu(qk_c)
    qm = qk_c[..., :B_WIDTH].reshape(B, T, NH_B, DK_B).astype(f32)
    km = qk_c[..., B_WIDTH:].reshape(B, T, NH_B, DK_B).astype(f32) * (DK_B ** -0.5)
    vm = v_b.reshape(B, T, NH_B, DK_B).astype(f32)
    ig = i_b.astype(f32) + prm['mgate_b'][0]
    lf = jax.nn.log_sigmoid(f_b.astype(f32) + prm['mgate_b'][1])
    if past is None:
        C0 = jnp.zeros((B, NH_B, DK_B, DK_B), f32)
        n0 = jnp.zeros((B, NH_B, DK_B), f32)
        m0 = jnp.zeros((B, NH_B), f32)
        chunk = MLSTM_CHUNK if T % MLSTM_CHUNK == 0 else T
    else:
        C0 = past['C'].astype(f32)
        n0 = past['n'].astype(f32)
        m0 = past['m'].astype(f32)
        chunk = T
    hm, C1, n1, m1 = _mlstm(qm, km, vm, ig, lf, C0, n0, m0, chunk)
    o_gate = jax.nn.sigmoid(o_b.astype(f32)).reshape(B, T, NH_B, DK_B)
    b_out = (o_gate * hm).reshape(B, T, B_WIDTH).astype(x.dtype)

    merged = jax.nn.sigmoid(g_a) * (a_out @ prm['w_pa']) + jax.nn.sigmoid(g_b) * (b_out @ prm['w_pb'])
    x = x + merged @ prm['w_o']

    h2 = _rms_norm(x, prm['norm2_g'])
    u = h2 @ prm['w_up']
    if past is None:
        fbuf = jnp.zeros((B, CONV_F - 1, 2 * D_FF), x.dtype)
    else:
        fbuf = past['fconv']
    u, new_fconv = _causal_dwconv(u, fbuf, prm['fconv_w'], prm['fconv_b'])
    x = x + (jax.nn.gelu(u[..., :D_FF]) * u[..., D_FF:]) @ prm['w_down']
    return x, (new_kv[0], new_kv[1], new_kv[2], new_mconv, C1, n1, m1, new_fconv)


def setup_inputs(seed: int = 0) -> dict:
    key = jax.random.key(seed)
    ks = jax.random.split(key, 32)
    f32 = jnp.float32

    def nrm(i, shape, scale):
        return scale * jax.random.normal(ks[i], shape, f32)

    lens = [min(w, PAST_LEN) for w in WINDOWS]
    i_bias = nrm(20, (DEPTH, NH_B), 0.1)
    f_bias = jnp.linspace(3.0, 6.0, NH_B, dtype=f32)[None, :] + nrm(21, (DEPTH, NH_B), 0.1)
    return {
        'x_prompt': nrm(0, (BATCH, SEQ, D_MODEL), 1.0),
        'x_sample': nrm(1, (DEC_BATCH, DEC_SEQ, D_MODEL), 1.0),
        'cache_kv_w128': nrm(2, (DEPTH, DEC_BATCH, lens[0], 2, H_G, DH_A), 1.0),
        'cache_kv_w512': nrm(3, (DEPTH, DEC_BATCH, lens[1], 2, H_G, DH_A), 1.0),
        'cache_kv_w2048': nrm(4, (DEPTH, DEC_BATCH, lens[2], 2, H_G, DH_A), 1.0),
        'state_mlstm_conv': nrm(5, (DEPTH, DEC_BATCH, CONV_B - 1, 2 * B_WIDTH), 1.0),
        'state_mlstm_C': nrm(6, (DEPTH, DEC_BATCH, NH_B, DK_B, DK_B), 0.1),
        'state_mlstm_n': nrm(7, (DEPTH, DEC_BATCH, NH_B, DK_B), 0.1),
        'state_mlstm_m': nrm(8, (DEPTH, DEC_BATCH, NH_B), 1.0),
        'state_ffn_conv': nrm(9, (DEPTH, DEC_BATCH, CONV_F - 1, 2 * D_FF), 1.0),
        'rel_bias': nrm(10, (NUM_BUCKETS, N_GROUPS * H_G), 0.2),
        'norm1_g': 1.0 + nrm(11, (DEPTH, D_MODEL), 0.02),
        'w_in': nrm(12, (DEPTH, D_MODEL, IN_COLS), D_MODEL ** -0.5),
        'mconv_w': nrm(13, (DEPTH, CONV_B, 2 * B_WIDTH), CONV_B ** -0.5),
        'mconv_b': nrm(14, (DEPTH, 2 * B_WIDTH), 0.02),
        'mgate_b': jnp.stack([i_bias, f_bias], axis=1),
        'w_pa': nrm(15, (DEPTH, A_GROUP, D_MODEL), A_GROUP ** -0.5),
        'w_pb': nrm(16, (DEPTH, B_WIDTH, D_MODEL), B_WIDTH ** -0.5),
        'w_o': nrm(17, (DEPTH, D_MODEL, D_MODEL), D_MODEL ** -0.5),
        'norm2_g': 1.0 + nrm(18, (DEPTH, D_MODEL), 0.02),
        'w_up': nrm(19, (DEPTH, D_MODEL, 2 * D_FF), D_MODEL ** -0.5),
        'fconv_w': nrm(22, (DEPTH, CONV_F, 2 * D_FF), CONV_F ** -0.5),
        'fconv_b': nrm(23, (DEPTH, 2 * D_FF), 0.02),
        'w_down': nrm(24, (DEPTH, D_FF, D_MODEL), D_FF ** -0.5),
        'final_norm_g': 1.0 + nrm(25, (D_MODEL,), 0.02),
    }


def reference(x_prompt, x_sample, cache_kv_w128, cache_kv_w512, cache_kv_w2048, state_mlstm_conv,
              state_mlstm_C, state_mlstm_n, state_mlstm_m, state_ffn_conv, rel_bias, norm1_g, w_in,
              mconv_w, mconv_b, mgate_b, w_pa, w_pb, w_o, norm2_g, w_up, fconv_w, fconv_b, w_down,
              final_norm_g):
    yp, ys = x_prompt, x_sample
    p_st = [[] for _ in range(8)]
    s_st = [[] for _ in range(8)]
    for l in range(DEPTH):
        prm = {'norm1_g': norm1_g[l], 'w_in': w_in[l], 'mconv_w': mconv_w[l], 'mconv_b': mconv_b[l],
               'mgate_b': mgate_b[l], 'w_pa': w_pa[l], 'w_pb': w_pb[l], 'w_o': w_o[l],
               'norm2_g': norm2_g[l], 'w_up': w_up[l], 'fconv_w': fconv_w[l], 'fconv_b': fconv_b[l],
               'w_down': w_down[l]}
        past = {'kv': (cache_kv_w128[l], cache_kv_w512[l], cache_kv_w2048[l]),
                'mconv': state_mlstm_conv[l], 'C': state_mlstm_C[l], 'n': state_mlstm_n[l],
                'm': state_mlstm_m[l], 'fconv': state_ffn_conv[l]}
        yp, st_p = _layer(yp, prm, rel_bias, None)
        ys, st_s = _layer(ys, prm, rel_bias, past)
        for i in range(8):
            p_st[i].append(st_p[i])
            s_st[i].append(st_s[i])
    y_prompt = _rms_norm(yp, final_norm_g)
    y_sample = _rms_norm(ys, final_norm_g)
    kv_w128_prompt = jnp.stack(p_st[0], 0)
    kv_w128_sample = jnp.stack(s_st[0], 0)
    kv_w512_prompt = jnp.stack(p_st[1], 0)
    kv_w512_sample = jnp.stack(s_st[1], 0)
    kv_w2048_prompt = jnp.stack(p_st[2], 0)
    kv_w2048_sample = jnp.stack(s_st[2], 0)
    mlstm_conv_prompt = jnp.stack(p_st[3], 0)
    mlstm_conv_sample = jnp.stack(s_st[3], 0)
    mlstm_C_prompt = jnp.stack(p_st[4], 0)
    mlstm_C_sample = jnp.stack(s_st[4], 0)
    mlstm_n_prompt = jnp.stack(p_st[5], 0)
    mlstm_n_sample = jnp.stack(s_st[5], 0)
    mlstm_m_prompt = jnp.stack(p_st[6], 0)
    mlstm_m_sample = jnp.stack(s_st[6], 0)
    ffn_conv_prompt = jnp.stack(p_st[7], 0)
    ffn_conv_sample = jnp.stack(s_st[7], 0)
    return (y_prompt, y_sample, kv_w128_prompt, kv_w128_sample, kv_w512_prompt, kv_w512_sample,
            kv_w2048_prompt, kv_w2048_sample, mlstm_conv_prompt, mlstm_conv_sample,
            mlstm_C_prompt, mlstm_C_sample, mlstm_n_prompt, mlstm_n_sample,
            mlstm_m_prompt, mlstm_m_sample, ffn_conv_prompt, ffn_conv_sample)
```

```python
import contextlib
import math
import numpy as np
import concourse.bass as bass
import concourse.mybir as mybir
from concourse.bass_utils import run_bass_kernel_spmd

F32 = mybir.dt.float32
BF16 = mybir.dt.bfloat16
AF = mybir.ActivationFunctionType
ALU = mybir.AluOpType
AX = mybir.AxisListType

D = 1024
SEQ = 8192
DEPTH = 2
TT = 512
NT_FULL = SEQ // TT
DIL = (1, 4, 16)
WIN = (128, 512, 2048)
DFF = 2816
INC = 10760
SLOT = 4096
DMA_ALL_SP = False
NEG = -30000.0
EPS = 1e-6

O_QA, O_KA, O_VA, O_QB, O_KB, O_VB, O_OB, O_I, O_GA, O_GB = 0, 1536, 3072, 4608, 5632, 6656, 7680, 8704, 8712, 9736


def slab_list():
    L = []
    for i in range(3):
        L.append(("qa", i, "w_in", 8, [(O_QA + 512 * i, 512)], 1))
    for i in range(3):
        L.append(("ka", i, "w_in", 8, [(O_KA + 512 * i, 512)], 1))
    for i in range(3):
        L.append(("va", i, "w_in", 8, [(O_VA + 512 * i, 512)], 1))
    for i in range(2):
        L.append(("qb", i, "w_in", 8, [(O_QB + 512 * i, 512)], 1))
    for i in range(2):
        L.append(("kb", i, "w_in", 8, [(O_KB + 512 * i, 512)], 1))
    for i in range(2):
        L.append(("vb", i, "w_in", 8, [(O_VB + 512 * i, 512)], 1))
    for i in range(2):
        L.append(("ob", i, "w_in", 8, [(O_OB + 512 * i, 512)], 1))
    L.append(("gt", 0, "w_in", 8, [(O_I, 8)], 1))
    for i in range(2):
        L.append(("ga", i, "w_in", 8, [(O_GA + 512 * i, 512)], 1))
    for i in range(2):
        L.append(("gb", i, "w_in", 8, [(O_GB + 512 * i, 512)], 1))
    L.append(("pa", 0, "w_pa", 4, [(0, 1024)], 0))
    for i in range(2):
        L.append(("pb", i, "w_pb", 8, [(512 * i, 512)], 0))
    for i in range(2):
        L.append(("wo", i, "w_o", 8, [(512 * i, 512)], 0))
    for i in range(11):
        L.append(("up", i, "w_up", 8, [(256 * i, 256), (DFF + 256 * i, 256)], 2))
    for i in range(8):
        L.append(("dn", i, "w_down", 22, [(128 * i, 128)], 0))
    return L


SLABS = slab_list()
NSLAB = len(SLABS)


class Trk:
    NDS = 48

    def __init__(self, nc, es):
        self.nc = nc
        self.E = {"pe": nc.tensor, "act": nc.scalar, "dve": nc.vector, "pool": nc.gpsimd, "sp": nc.sync}
        self.sem = {e: es.enter_context(nc.semaphore("sem_" + e)) for e in self.E}
        self.cnt = {e: 0 for e in self.E}
        self.seen = {e: {} for e in self.E}
        self.res = {}
        self.dsems = [es.enter_context(nc.semaphore("dsem%d" % i)) for i in range(self.NDS)]
        self.dcnt = [0] * self.NDS
        self.dnext = 0

    def semh(self, key):
        return self.sem[key] if isinstance(key, str) else self.dsems[key[1]]

    def _wait(self, e, key, val):
        if e == "pe" and key == "pe":
            return
        if self.seen[e].get(key, 0) >= val:
            return
        self.seen[e][key] = val
        self.E[e].wait_ge(self.semh(key), val)

    def deps(self, e, reads, writes):
        toks = {}

        def add(k, v):
            if toks.get(k, 0) < v:
                toks[k] = v
        for r in reads:
            st = self.res.get(r)
            if st and st["w"]:
                add(*st["w"])
        for w in writes:
            st = self.res.get(w)
            if st:
                if st["w"]:
                    add(*st["w"])
                for k, v in st["r"].items():
                    add(k, v)
        for k, v in toks.items():
            self._wait(e, k, v)

    def commit(self, tok, reads, writes):
        for r in reads:
            st = self.res.setdefault(r, {"w": None, "r": {}})
            if st["r"].get(tok[0], 0) < tok[1]:
                st["r"][tok[0]] = tok[1]
        for w in writes:
            self.res[w] = {"w": tok, "r": {}}

    def op(self, e, fn, reads=(), writes=()):
        self.deps(e, reads, writes)
        ins = fn(self.E[e])
        self.cnt[e] += 1
        ins.then_inc(self.sem[e], 1)
        self.commit((e, self.cnt[e]), reads, writes)

    def dma(self, q, out, in_, reads=(), writes=(), slow=False):
        if DMA_ALL_SP:
            q = "sp"
        self.deps(q, reads, writes)
        i = self.dnext
        self.dnext = (i + 1) % self.NDS
        if self.dcnt[i] > 0:
            self._wait(q, ("d", i), 16 * self.dcnt[i])
        self.dcnt[i] += 1
        if slow:
            ins = self.E[q].dma_start(out=out, in_=in_, allow_slow_non_contiguous=True)
        else:
            ins = self.E[q].dma_start(out=out, in_=in_)
        ins.then_inc(self.dsems[i], 16)
        self.commit((("d", i), 16 * self.dcnt[i]), reads, writes)

    def alias(self, olds, news):
        for n in news:
            st = self.res.setdefault(n, {"w": None, "r": {}})
            for o in olds:
                so = self.res.get(o)
                if not so:
                    continue
                if so["w"] and st["r"].get(so["w"][0], 0) < so["w"][1]:
                    st["r"][so["w"][0]] = so["w"][1]
                for k, v in so["r"].items():
                    if st["r"].get(k, 0) < v:
                        st["r"][k] = v

    def barrier(self):
        for e in self.E:
            for f in self.E:
                if f != e and self.cnt[f] > 0:
                    self._wait(e, f, self.cnt[f])
            for i in range(self.NDS):
                if self.dcnt[i] > 0:
                    self._wait(e, ("d", i), 16 * self.dcnt[i])
        self.res = {}


def t5_bucket(dist):
    dist = np.asarray(dist, dtype=np.int64)
    df = np.maximum(dist, 1).astype(np.float32)
    large = 16 + (np.log(df / np.float32(16)) / np.float32(math.log(2048 / 16)) * np.float32(16)).astype(np.int32)
    large = np.minimum(large, 31)
    return np.where(dist < 16, dist, large).astype(np.int64)


C_ID, C_TRI, C_ONE, C_SEL63, C_CNEG, C_EPS, C_END = 0, 128, 192, 320, 448, 704, 708


def make_consts():
    c = np.zeros((128, C_END), np.float32)
    c[:, C_ID:C_ID + 128] = np.eye(128, dtype=np.float32)
    s = np.arange(64)
    c[:64, C_TRI:C_TRI + 64] = (s[:, None] <= s[None, :]).astype(np.float32)
    c[:, C_ONE:C_ONE + 128] = 1.0
    c[63, C_SEL63:C_SEL63 + 128] = 1.0
    cn = np.where(s[None, :] <= s[:, None], 0.0, -1e30).astype(np.float32)
    c[:64, C_CNEG:C_CNEG + 256] = np.tile(cn, (1, 4))
    c[:, C_EPS] = EPS
    return c


class StopBuild(Exception):
    pass


def build(NT=NT_FULL, do_sample=True, stage=99):
    def chk(n):
        if abs(stage) == n:
            raise StopBuild()

    nc = bass.Bass("TRN2", target_bir_lowering=False)
    es = contextlib.ExitStack()

    def din(name, shape):
        return nc.dram_tensor(name, list(shape), F32, kind="ExternalInput").ap()

    def dout(name, shape):
        return nc.dram_tensor(name, list(shape), F32, kind="ExternalOutput").ap()

    xp = din("xp", [SEQ, D])
    xs = din("xs", [4, D])
    cache = [din("cache%d" % g, [2, 4, 128, 1024]) for g in range(3)]
    st_mconv = din("st_mconv", [2, 4, 3, 2048])
    st_C = din("st_C", [2, 4, 4, 256, 256])
    st_n = din("st_n", [2, 4, 4, 256])
    st_m = din("st_m", [2, 4, 4])
    st_fconv = din("st_fconv", [2, 4, 2, 2 * DFF])
    bias_g = din("bias_g", [128, 24 * 2 * 128])
    bias_m = din("bias_m", [128, 24 * 2 * 128])
    bias_s = din("bias_s", [128, 24])
    bias_0 = din("bias_0", [4, 24])
    consts = din("consts", [128, C_END])
    consts2 = din("consts2", [128, 528])
    W = {
        "w_in": din("w_in", [2, D, INC]), "w_pa": din("w_pa", [2, 512, D]), "w_pb": din("w_pb", [2, D, D]),
        "w_o": din("w_o", [2, D, D]), "w_up": din("w_up", [2, D, 2 * DFF]), "w_down": din("w_down", [2, DFF, D]),
    }
    norm1_g = din("norm1_g", [2, D])
    norm2_g = din("norm2_g", [2, D])
    mconv_w = din("mconv_w", [2, 4, 2048])
    mconv_b = din("mconv_b", [2, 2048])
    mgate_b = din("mgate_b", [2, 8])
    fconv_w = din("fconv_w", [2, 3, 2 * DFF])
    fconv_b = din("fconv_b", [2, 2 * DFF])
    fin_g = din("fin_g", [1, D])
    cw_t = din("cw_t", [2, 128, 64])
    cb_t = din("cb_t", [2, 128, 16])
    fw_t = din("fw_t", [2, 128, 132])
    fb_t = din("fb_t", [2, 128, 44])
    gn_t = din("gn_t", [128, 32])

    o_yp = dout("o_yp", [SEQ, D])
    o_ys = dout("o_ys", [4, D])
    o_kvp = [dout("o_kvp%d" % g, [2, WIN[g], 1024]) for g in range(3)]
    o_kvs = [dout("o_kvs%d" % g, [2, 4, 1024]) for g in range(3)]
    o_mcp = dout("o_mcp", [2, 128, 48])
    o_mcs = dout("o_mcs", [2, 4, 3, 2048])
    o_Cp = dout("o_Cp", [2, 4, 256, 256])
    o_Cs = dout("o_Cs", [2, 4, 4, 256, 256])
    o_np = dout("o_np", [2, 128, 8])
    o_ns = dout("o_ns", [2, 4, 4, 256])
    o_mp = dout("o_mp", [2, 4])
    o_ms = dout("o_ms", [2, 4, 4])
    o_fcp = dout("o_fcp", [2, 128, 88])
    o_fcs = dout("o_fcs", [2, 4, 2, 2 * DFF])

    wbf = nc.dram_tensor("wbf", [2, NSLAB, 128, SLOT], BF16, kind="Internal").ap()
    xscr = nc.dram_tensor("xscr", [SEQ, D], F32, kind="Internal").ap()
    k3d = nc.dram_tensor("k3d", [4, 128, 16, 2, 128], BF16, kind="Internal").ap()
    v3d = nc.dram_tensor("v3d", [4, 128, 16, 2, 128], BF16, kind="Internal").ap()

    T = Trk(nc, es)

    def sb(name, shape, dt=F32):
        return es.enter_context(nc.sbuf_tensor(name, list(shape), dt))

    cst = sb("cst", [128, C_END])
    cstb = sb("cstb", [128, 320], BF16)
    T.dma("pool", cst[:], consts[:, :], writes=["cst"])
    T.op("dve", lambda e: e.tensor_copy(out=cstb[:, 0:320], in_=cst[:, 0:320]), reads=["cst"], writes=["cstb"])
    identb = cstb[:, 0:128]
    onesb = cstb[:, 192:320]
    identf = cst[:, C_ID:C_ID + 128]

    psum = [es.enter_context(nc.psum_tensor("ps%d" % i, [128, 512], F32)) for i in range(6)]
    psb = [es.enter_context(nc.psum_tensor("psb%d" % i, [128, 1024], BF16)) for i in range(2)]
    pctr = [0, 0]

    def nps():
        i = pctr[0] % 4
        pctr[0] += 1
        return psum[i], "ps%d" % i

    def npb():
        i = pctr[1] % 2
        pctr[1] += 1
        return psb[i], "psb%d" % i

    wsl = sb("wsl", [128, 3, SLOT], BF16)

    class WS:
        def __init__(self):
            self.seq = []
            self.i = 0
            self.issued = 0

        def extend(self, items):
            self.seq.extend(items)

        def _issue(self):
            if self.issued < len(self.seq):
                l, si = self.seq[self.issued]
                k = self.issued % 3
                kind, idx, src, nk, cols, gain = SLABS[si]
                n = nk * sum(c[1] for c in cols)
                T.dma("sp", wsl[:, k, 0:n], wbf[l, si, :, 0:n], reads=["wbf%d_%d" % (l, si)], writes=["wsl%d" % k])
                self.issued += 1

        def get(self, l, si):
            assert self.seq[self.i] == (l, si), (self.seq[self.i], l, si)
            while self.issued < min(self.i + 3, len(self.seq)):
                self._issue()
            k = self.i % 3
            self.i += 1
            kind, idx, src, nk, cols, gain = SLABS[si]
            nc_ = sum(c[1] for c in cols)
            return wsl[:, k, 0:nk * nc_].rearrange("p (k c) -> p k c", k=nk), "wsl%d" % k

    ws = WS()

    with contextlib.ExitStack() as es0:
        stg = [es0.enter_context(nc.sbuf_tensor("stg%d" % i, [128, SLOT], F32)) for i in range(2)]
        stb = [es0.enter_context(nc.sbuf_tensor("stb%d" % i, [128, SLOT], BF16)) for i in range(2)]
        gn = es0.enter_context(nc.sbuf_tensor("gn", [128, 2, 2, 8], F32))
        T.dma("pool", gn[:].rearrange("p a b c -> p (a b c)"), gn_t[:, :], writes=["gn"])
        j = 0
        for l in range(2):
            for si, (kind, idx, src, nk, cols, gain) in enumerate(SLABS):
                if stage < 0:
                    continue
                k = j % 2
                j += 1
                nc_ = sum(c[1] for c in cols)
                n = nk * nc_
                sview = stg[k][:, 0:n].rearrange("p (k c) -> p k c", k=nk)
                bview = stb[k][:, 0:n].rearrange("p (k c) -> p k c", k=nk)
                wsrc = W[src][l].rearrange("(k p) c -> p k c", p=128)
                off = 0
                for (c0, cn) in cols:
                    T.dma("sp" if (j % 2) else "pool", sview[:, :, off:off + cn], wsrc[:, :, c0:c0 + cn],
                          writes=["stg%d" % k], slow=(cn < 128))
                    off += cn
                if gain:
                    for kc in range(nk):
                        eng = "dve" if kc % 2 == 0 else "act"
                        if eng == "dve":
                            T.op("dve", lambda e, kc=kc: e.tensor_scalar(out=bview[:, kc, :], in0=sview[:, kc, :],
                                 scalar1=gn[:, gain - 1, l, kc:kc + 1], scalar2=None, op0=ALU.mult),
                                 reads=["stg%d" % k, "gn"], writes=["stb%d" % k])
                        else:
                            T.op("act", lambda e, kc=kc: e.activation(out=bview[:, kc, :], in_=sview[:, kc, :],
                                 func=AF.Copy, scale=gn[:, gain - 1, l, kc:kc + 1]),
                                 reads=["stg%d" % k, "gn"], writes=["stb%d" % k])
                else:
                    h = n // 2
                    T.op("dve", lambda e: e.tensor_copy(out=stb[k][:, 0:h], in_=stg[k][:, 0:h]),
                         reads=["stg%d" % k], writes=["stb%d" % k])
                    T.op("act", lambda e: e.activation(out=stb[k][:, h:n], in_=stg[k][:, h:n], func=AF.Copy),
                         reads=["stg%d" % k], writes=["stb%d" % k])
                T.dma("sp" if (j % 2) else "pool", wbf[l, si, :, 0:n], stb[k][:, 0:n], reads=["stb%d" % k],
                      writes=["wbf%d_%d" % (l, si)])
        T.barrier()

    try:
      with contextlib.ExitStack() as es1:
        if stage == 0:
            raise StopBuild()
        def sb1(name, shape, dt=F32):
            return es1.enter_context(nc.sbuf_tensor(name, list(shape), dt))

        xt = sb1("xt", [128, 4, D])
        hb = sb1("hb", [128, D], BF16)
        junk = hb
        hT = sb1("hT", [128, 8, TT], BF16)
        ss = sb1("ss", [128, 4])
        rs = sb1("rs", [128, 4])
        U = sb1("U", [128, 32, TT], BF16)
        QT, qcT, kcT = U[:, 0:12, :], U[:, 12:20, :], U[:, 20:28, :]
        boT, sgT, mT, actT = U[:, 0:8, :], U[:, 8:24, :], U[:, 24:32, :], U[:, 0:22, :]
        KT1 = sb1("KT1", [128, 4, 640], BF16)
        KT2 = sb1("KT2", [128, 4, 4, 2, 128], BF16)
        KT3 = sb1("KT3", [128, 16, 2, 128], BF16)
        k3n = sb1("k3n", [128, 16, 32], BF16)
        v3n = sb1("v3n", [128, 2, 512], BF16)
        V1 = sb1("V1", [128, 5, 512], BF16)
        V2 = sb1("V2", [128, 4, 2, 512], BF16)
        V3 = sb1("V3", [128, 16, 2, 128], BF16)
        btab = sb1("btab", [128, 24, 2, 128], BF16)
        pre = sb1("pre", [128, 2, 516])
        acc = sb1("acc", [128, 2, 512])
        fpre, fc = pre, acc
        cw = sb1("cw", [128, 16, 4])
        cb = sb1("cb", [128, 16])
        carry = sb1("carry", [128, 16, 3])
        vaug = sb1("vaug", [64, 8, 4, 258], BF16)
        ogT = sb1("ogT", [128, 8, TT], BF16)
        graw = sb1("graw", [64, 8, 8])
        gbias = sb1("gbias", [64, 8])
        igt = sb1("igt", [64, 8, 4])
        lft = sb1("lft", [64, 8, 4])
        aT = sb1("aT", [128, 4, TT], BF16)
        Cf = sb1("Cf", [128, 2, 4, 257])
        Cb = sb1("Cb", [128, 2, 4, 258], BF16)
        mprev = sb1("mprev", [128, 4])
        sm = sb1("sm", [128, 512], BF16)
        mgA = sm
        stmp = sb1("stmp", [128, 256])
        rden = sb1("rden", [128, 512])
        stf = rden
        ft = rden
        g_sb = sb1("g_sb", [64, 4])
        b_sb = sb1("b_sb", [64, 4])
        dg = sb1("dg", [64, 256])
        Lm = sb1("Lm", [64, 256])
        dw = Lm
        stat = sb1("stat", [64, 12])
        small = sb1("small", [64, 16])
        bc = sb1("bc", [128, 12])
        wc = sb1("wc", [64, 4])
        Pbf = sb1("Pbf", [64, 256], BF16)
        PTs = sb1("PTs", [64, 256], BF16)
        ktok = sb1("ktok", [64, D], BF16)
        intra = sb1("intra", [64, 257])
        nd = intra
        bo = sb1("bo", [64, D], BF16)
        fw = sb1("fw", [128, 2, 22, 3])
        fb = sb1("fb", [128, 2, 22])
        fcar = sb1("fcar", [128, 2, 22, 2])

        T.op("dve", lambda e: e.memset(vaug[:], 1.0), writes=["vaug"])
        for nm, t in (("KT1", KT1), ("KT2", KT2), ("KT3", KT3), ("V1", V1), ("V2", V2), ("V3", V3)):
            T.op("dve", lambda e, t=t: e.memset(t[:], 0.0), writes=[nm])
        btf = acc[:, 0, :]
        btm = acc[:, 1, :]
        for q in range(12):
            T.dma("pool", btf, bias_g[:, q * 512:(q + 1) * 512], writes=["acc0"])
            T.dma("pool", btm, bias_m[:, q * 512:(q + 1) * 512], writes=["acc1"])
            T.op("dve", lambda e, q=q: e.tensor_tensor(out=btab[:, 2 * q:2 * q + 2, :, :].rearrange("p a b c -> p (a b c)"),
                 in0=btf, in1=btm, op=ALU.add), reads=["acc0", "acc1"], writes=["btab"])

        chk(1)

        def norm_T(srcs):
            T.op("dve", lambda e: e.memset(ss[:], 0.0), writes=["ss"])
            for s in range(4):
                T.op("act", lambda e: e.activation(out=junk[:], in_=xt[:, s, :], func=AF.Square, accum_out=ss[:, s:s + 1]),
                     reads=["xt%d" % s, "ss"], writes=["hb", "ss"])
            T.op("act", lambda e: e.activation(out=rs[:], in_=ss[:], func=AF.Ln, scale=1.0 / D, bias=cst[:, C_EPS:C_EPS + 1]), reads=["ss", "cst"], writes=["rs"])
            T.op("act", lambda e: e.activation(out=rs[:], in_=rs[:], func=AF.Exp, scale=-0.5), reads=["rs"], writes=["rs"])
            for s in range(4):
                T.op("dve", lambda e: e.tensor_scalar(out=hb[:], in0=xt[:, s, :], scalar1=rs[:, s:s + 1], scalar2=None, op0=ALU.mult),
                     reads=["xt%d" % s, "rs"], writes=["hb"])
                pb_, pn = npb()
                for kc in range(8):
                    T.op("pe", lambda e: e.transpose(out=pb_[:, kc * 128:(kc + 1) * 128], in_=hb[:, kc * 128:(kc + 1) * 128], identity=identb),
                         reads=["hb", "cstb"], writes=[pn])
                T.op("act", lambda e: e.activation(out=hT[:, :, s * 128:(s + 1) * 128], in_=pb_[:, :].rearrange("p (k t) -> p k t", k=8), func=AF.Copy),
                     reads=[pn], writes=["hT"])

        def mm(out, lhsT, rhs, start, stop, reads, writes, **kw):
            T.op("pe", lambda e: e.matmul(out, lhsT, rhs, start=start, stop=stop, skip_group_check=True, **kw), reads=reads, writes=writes)

        def fm_chunk(wv, wn, c, rhsT, rname, nk=8):
            p, pn = nps()
            for kc in range(nk):
                mm(p[:, :], wv[:, kc, c * 128:(c + 1) * 128], rhsT[:, kc, :], kc == 0, kc == nk - 1, [wn, rname], [pn])
            return p, pn

        for l in range(DEPTH):
            ws.extend([(l, si) for _ in range(NT) for si in range(NSLAB)])
        SIDX = {}
        for si, sdef in enumerate(SLABS):
            SIDX[(sdef[0], sdef[1])] = si

        for l in range(DEPTH):
            xsrc = xp if l == 0 else xscr
            T.dma("pool", cw[:].rearrange("p c k -> p (c k)"), cw_t[l], writes=["cw"])
            T.dma("pool", cb[:], cb_t[l], writes=["cb"])
            T.dma("pool", fw[:].rearrange("p h j k -> p (h j k)"), fw_t[l], writes=["fw"])
            T.dma("pool", fb[:].rearrange("p h j -> p (h j)"), fb_t[l], writes=["fb"])
            T.dma("pool", gbias[:], mgate_b[l:l + 1, :].broadcast_to([64, 8]), writes=["gbias"])
            T.op("dve", lambda e: e.memset(carry[:], 0.0), writes=["carry"])
            T.op("dve", lambda e: e.memset(fcar[:], 0.0), writes=["fcar"])
            T.op("dve", lambda e: e.memset(Cf[:], 0.0), writes=["Cf"])
            T.op("dve", lambda e: e.memset(Cb[:], 0.0), writes=["Cb"])
            T.op("dve", lambda e: e.memset(mprev[:], 0.0), writes=["mprev"])
            T.op("dve", lambda e: e.memset(KT3[:], 0.0), writes=["KT3"])
            for c in range(4):
                T.dma("pool", k3d[c], KT3[:], reads=["KT3"], writes=["k3d%d" % c])
                T.dma("pool", v3d[c], KT3[:], reads=["KT3"], writes=["v3d%d" % c])

            for ti in range(NT):
                t0 = ti * TT
                last = (ti == NT - 1)
                for s in range(4):
                    T.dma("pool", xt[:, s, :], xsrc[t0 + s * 128:t0 + (s + 1) * 128, :], reads=["xscr"] if l else [], writes=["xt%d" % s])
                chk(2)
                norm_T(None)
                chk(3)
                T.alias(["actT", "mT", "sgT", "boT"], ["QT", "qcT", "kcT"])
                par2 = ti % 2
                v3 = ti % 4
                par3 = (ti // 4) % 2

                for g in range(3):
                    wv, wn = ws.get(l, SIDX[("qa", g)])
                    d = DIL[g]
                    for c in range(4):
                        p, pn = fm_chunk(wv, wn, c, hT, "hT")
                        if d == 1:
                            T.op("act", lambda e: e.activation(out=QT[:, c, :], in_=p[:, :], func=AF.Copy), reads=[pn], writes=["QT"])
                        else:
                            T.op("act", lambda e: e.activation(out=QT[:, 4 * g + c, :].rearrange("p (r u) -> p r u", r=d),
                                 in_=p[:, :].rearrange("p (u r) -> p r u", r=d), func=AF.Copy), reads=[pn], writes=["QT"])
                chk(4)
                for g in range(3):
                    wv, wn = ws.get(l, SIDX[("ka", g)])
                    d = DIL[g]
                    for c in range(4):
                        p, pn = fm_chunk(wv, wn, c, hT, "hT")
                        if g == 0:
                            T.op("dve", lambda e: e.tensor_copy(out=KT1[:, c, 128:640], in_=p[:, :]), reads=[pn], writes=["KT1"])
                        elif g == 1:
                            T.op("dve", lambda e: e.tensor_copy(out=KT2[:, c, :, par2, :], in_=p[:, :].rearrange("p (u r) -> p r u", r=4)),
                                 reads=[pn], writes=["KT2"])
                        else:
                            T.op("dve", lambda e: e.tensor_copy(out=k3n[:], in_=p[:, :].rearrange("p (u r) -> p r u", r=16)), reads=[pn], writes=["k3n"])
                            T.dma("pool", k3d[c, :, :, par3, 32 * v3:32 * v3 + 32], k3n[:], reads=["k3n"], writes=["k3d%d" % c], slow=True)
                    row0 = t0 - (NT * TT - WIN[g])
                    for s in range(4):
                        r0 = row0 + s * 128
                        if r0 >= 0:
                            p, pn = nps()
                            for kc in range(8):
                                mm(p[:, :], hT[:, kc, s * 128:(s + 1) * 128], wv[:, kc, :], kc == 0, kc == 7, [wn, "hT"], [pn])
                            T.op("act", lambda e: e.activation(out=stf[:], in_=p[:, :], func=AF.Copy), reads=[pn], writes=["rden"])
                            T.dma("pool", o_kvp[g][l, r0:r0 + 128, 0:512], stf[:], reads=["rden"], writes=["okv"])
                chk(5)
                for g in range(3):
                    wv, wn = ws.get(l, SIDX[("va", g)])
                    row0 = t0 - (NT * TT - WIN[g])
                    if g == 0:
                        for s in range(4):
                            p, pn = nps()
                            for kc in range(8):
                                mm(p[:, :], hT[:, kc, s * 128:(s + 1) * 128], wv[:, kc, :], kc == 0, kc == 7, [wn, "hT"], [pn])
                            T.op("act", lambda e: e.activation(out=V1[:, s + 1, :], in_=p[:, :], func=AF.Copy), reads=[pn], writes=["V1"])
                            r0 = row0 + s * 128
                            if r0 >= 0:
                                T.op("dve", lambda e: e.tensor_copy(out=stf[:], in_=p[:, :]), reads=[pn], writes=["rden", pn])
                                T.dma("pool", o_kvp[g][l, r0:r0 + 128, 512:1024], stf[:], reads=["rden"], writes=["okv"])
                    else:
                        d = DIL[g]
                        hTr = hT[:, :, :].rearrange("p k (u r) -> p k r u", r=d)
                        for r in range(d):
                            p, pn = nps()
                            if g == 1:
                                for kc in range(8):
                                    mm(p[:, :], hTr[:, kc, r, :], wv[:, kc, :], kc == 0, kc == 7, [wn, "hT"], [pn])
                                T.op("act", lambda e: e.activation(out=V2[:, r, par2, :], in_=p[:, :], func=AF.Copy), reads=[pn], writes=["V2"])
                            else:
                                lo = 32 * v3
                                for kc in range(8):
                                    mm(p[lo:lo + 32, :], hTr[:, kc, r, :], wv[:, kc, :], kc == 0, kc == 7, [wn, "hT"], [pn], tile_position=(0, lo))
                                vk = r % 2
                                T.op("act", lambda e: e.activation(out=v3n[lo:lo + 32, vk, :], in_=p[lo:lo + 32, :], func=AF.Copy), reads=[pn], writes=["v3n%d" % vk])
                                T.dma("pool", v3d[:, lo:lo + 32, r, par3, :].rearrange("c p f -> p c f"), v3n[lo:lo + 32, vk, :].rearrange("p (c f) -> p c f", c=4),
                                      reads=["v3n%d" % vk], writes=["v3d0", "v3d1", "v3d2", "v3d3"], slow=True)
                        for s in range(4):
                            r0 = row0 + s * 128
                            if r0 >= 0:
                                p, pn = nps()
                                for kc in range(8):
                                    mm(p[:, :], hT[:, kc, s * 128:(s + 1) * 128], wv[:, kc, :], kc == 0, kc == 7, [wn, "hT"], [pn])
                                T.op("dve", lambda e: e.tensor_copy(out=stf[:], in_=p[:, :]), reads=[pn], writes=["rden"])
                                T.dma("pool", o_kvp[g][l, r0:r0 + 128, 512:1024], stf[:], reads=["rden"], writes=["okv"])
                chk(6)
                for which, dst in (("qb", qcT), ("kb", kcT)):
                    for i2 in range(2):
                        wv, wn = ws.get(l, SIDX[(which, i2)])
                        for c in range(4):
                            ch = (0 if which == "qb" else 8) + 4 * i2 + c
                            k2 = ch % 2
                            p, pn = fm_chunk(wv, wn, c, hT, "hT")
                            pr, ac = "pre%d" % k2, "acc%d" % k2
                            T.op("dve", lambda e: e.tensor_copy(out=pre[:, k2, 0:3], in_=carry[:, ch, :]), reads=["carry"], writes=[pr])
                            T.op("act", lambda e: e.activation(out=pre[:, k2, 3:515], in_=p[:, :], func=AF.Copy), reads=[pn], writes=[pr])
                            T.op("dve", lambda e: e.tensor_copy(out=carry[:, ch, :], in_=pre[:, k2, 512:515]), reads=[pr], writes=["carry"])
                            T.op("dve", lambda e: e.tensor_scalar(out=acc[:, k2, :], in0=pre[:, k2, 0:512], scalar1=cw[:, ch, 0:1], scalar2=cb[:, ch:ch + 1],
                                 op0=ALU.mult, op1=ALU.add), reads=[pr, "cw", "cb"], writes=[ac])
                            for tp in range(1, 4):
                                T.op("dve", lambda e: e.scalar_tensor_tensor(out=acc[:, k2, :], in0=pre[:, k2, tp:tp + 512], scalar=cw[:, ch, tp:tp + 1],
                                     in1=acc[:, k2, :], op0=ALU.mult, op1=ALU.add), reads=[pr, "cw", ac], writes=[ac])
                            cc = 4 * i2 + c
                            if which == "qb":
                                T.op("act", lambda e: e.activation(out=dst[:, cc, :], in_=acc[:, k2, :], func=AF.Silu), reads=[ac], writes=["qcT"])
                            else:
                                T.op("act", lambda e: e.activation(out=acc[:, k2, :], in_=acc[:, k2, :], func=AF.Silu), reads=[ac], writes=[ac])
                                T.op("dve", lambda e: e.tensor_scalar(out=dst[:, cc, :], in0=acc[:, k2, :], scalar1=1.0 / 16.0, scalar2=None, op0=ALU.mult),
                                     reads=[ac], writes=["kcT"])
                chk(7)
                for which in ("vb", "ob"):
                    for i2 in range(2):
                        wv, wn = ws.get(l, SIDX[(which, i2)])
                        if which == "ob":
                            for c in range(4):
                                p, pn = fm_chunk(wv, wn, c, hT, "hT")
                                T.op("act", lambda e: e.activation(out=ogT[:, 4 * i2 + c, :], in_=p[:, :], func=AF.Sigmoid), reads=[pn], writes=["ogT"])
                            continue
                        for ch in range(8):
                            p, pn = nps()
                            for kc in range(8):
                                mm(p[0:64, :], hT[:, kc, ch * 64:(ch + 1) * 64], wv[:, kc, :], kc == 0, kc == 7, [wn, "hT"], [pn])
                            if which == "vb":
                                T.op("act", lambda e: e.activation(out=vaug[:, ch, 2 * i2:2 * i2 + 2, 0:256], in_=p[0:64, :].rearrange("p (h v) -> p h v", h=2),
                                     func=AF.Copy), reads=[pn], writes=["vaug"])
                            else:
                                T.op("act", lambda e: e.activation(out=og[:, ch, 512 * i2:512 * i2 + 512], in_=p[0:64, :], func=AF.Sigmoid), reads=[pn], writes=["og"])
                wv, wn = ws.get(l, SIDX[("gt", 0)])
                p, pn = nps()
                for ch in range(8):
                    for kc in range(8):
                        mm(p[0:64, ch * 8:(ch + 1) * 8], hT[:, kc, ch * 64:(ch + 1) * 64], wv[:, kc, :], (kc == 0 and ch == 0), kc == 7, [wn, "hT"], [pn])
                for ch in range(8):
                    T.op("dve", lambda e: e.tensor_tensor(out=graw[:, ch, :], in0=p[0:64, ch * 8:(ch + 1) * 8], in1=gbias[:, :], op=ALU.add),
                         reads=[pn, "gbias"], writes=["graw"])
                T.op("dve", lambda e: e.tensor_copy(out=igt[:], in_=graw[:, :, 0:4]), reads=["graw"], writes=["igt"])
                T.op("act", lambda e: e.activation(out=lft[:], in_=graw[:, :, 4:8], func=AF.Exp, scale=-1.0), reads=["graw"], writes=["lft"])
                T.op("act", lambda e: e.activation(out=lft[:], in_=lft[:], func=AF.Ln, bias=cst[0:64, C_ONE:C_ONE + 1]), reads=["lft", "cst"], writes=["lft"])
                T.op("dve", lambda e: e.tensor_scalar(out=lft[:], in0=lft[:], scalar1=-1.0, scalar2=None, op0=ALU.mult), reads=["lft"], writes=["lft"])
                chk(8)
                for c in range(4):
                    num, den = psum[4], psum[5]
                    T.dma("pool", KT3[:], k3d[c], reads=["k3d%d" % c], writes=["KT3"])
                    T.dma("pool", V3[:], v3d[c], reads=["v3d%d" % c], writes=["V3"])
                    first = {0: True, 64: True}
                    for hh in range(2):
                        h = 2 * c + hh
                        pb = 64 * hh
                        blocks = []
                        for b in range(4):
                            kb = []
                            if not (ti == 0 and b == 0):
                                kb.append((KT1[pb:pb + 64, c, b * 128:(b + 1) * 128], V1[:, b, h * 64:(h + 1) * 64], 0))
                            kb.append((KT1[pb:pb + 64, c, (b + 1) * 128:(b + 2) * 128], V1[:, b + 1, h * 64:(h + 1) * 64], 1))
                            blocks.append((0, QT[pb:pb + 64, c, b * 128:(b + 1) * 128], 128, kb, slice(b * 128, (b + 1) * 128), slice(0, 128)))
                        for r in range(4):
                            kb = []
                            if ti > 0:
                                kb.append((KT2[pb:pb + 64, c, r, 1 - par2, :], V2[:, r, 1 - par2, h * 64:(h + 1) * 64], 0))
                            kb.append((KT2[pb:pb + 64, c, r, par2, :], V2[:, r, par2, h * 64:(h + 1) * 64], 1))
                            blocks.append((1, QT[pb:pb + 64, 4 + c, r * 128:(r + 1) * 128], 128, kb, ("str", r, 4), slice(0, 128)))
                        for r in range(16):
                            kb = []
                            if ti // 4 > 0:
                                kb.append((KT3[pb:pb + 64, r, 1 - par3, :], V3[:, r, 1 - par3, hh * 64:(hh + 1) * 64], 0))
                            kb.append((KT3[pb:pb + 64, r, par3, :], V3[:, r, par3, hh * 64:(hh + 1) * 64], 1))
                            blocks.append((2, QT[pb:pb + 64, 8 + c, r * 32:(r + 1) * 32], 32, kb, ("str", r, 16), slice(32 * v3, 32 * v3 + 32)))
                        for (g, qv, nq, kb, ocol, qsl) in blocks:
                            p, pn = nps()
                            for (ktv, vv, kind) in kb:
                                mm(p[:, kind * nq:(kind + 1) * nq], ktv, qv, True, True, ["KT%d" % (g + 1), "QT"], [pn])
                            k0 = kb[0][2]
                            nk_ = len(kb)
                            pv_ = p[:, k0 * nq:(k0 + nk_) * nq].rearrange("p (k q) -> p k q", k=nk_)
                            tv = stmp[:, k0 * nq:(k0 + nk_) * nq].rearrange("p (k q) -> p k q", k=nk_)
                            sv = sm[:, k0 * nq:(k0 + nk_) * nq].rearrange("p (k q) -> p k q", k=nk_)
                            T.op("dve", lambda e: e.scalar_tensor_tensor(out=tv, in0=pv_, scalar=0.125, in1=btab[:, 8 * g + h, k0:k0 + nk_, qsl],
                                 op0=ALU.mult, op1=ALU.add), reads=[pn, "btab"], writes=["stmp"])
                            T.op("act", lambda e: e.activation(out=sv, in_=tv, func=AF.Exp), reads=["stmp"], writes=["sm"])
                            if isinstance(ocol, tuple):
                                _, r_, d_ = ocol
                                no = num[pb:pb + 64, :].rearrange("p (u r) -> p r u", r=d_)[:, r_, :]
                                do = den[pb:pb + 64, :].rearrange("p (u r) -> p r u", r=d_)[:, r_, :]
                            else:
                                no = num[pb:pb + 64, ocol]
                                do = den[pb:pb + 64, ocol]
                            for (ktv, vv, kind) in kb:
                                mm(no, vv, sm[:, kind * nq:(kind + 1) * nq], first[pb], False, ["V%d" % (g + 1), "sm"], ["psnum"], tile_position=(0, pb))
                                mm(do, onesb[:, 0:64], sm[:, kind * nq:(kind + 1) * nq], first[pb], False, ["cstb", "sm"], ["psden"], tile_position=(0, pb))
                                first[pb] = False
                    T.op("dve", lambda e: e.reciprocal(out=rden[:], in_=den[:, :]), reads=["psden"], writes=["rden"])
                    T.op("dve", lambda e: e.tensor_tensor(out=aT[:, c, :], in0=num[:, :], in1=rden[:], op=ALU.mult), reads=["psnum", "rden"], writes=["aT"])
                T.op("act", lambda e: e.activation(out=KT1[:, :, 0:128], in_=KT1[:, :, 512:640], func=AF.Copy), reads=["KT1"], writes=["KT1"])
                T.op("act", lambda e: e.activation(out=V1[:, 0, :], in_=V1[:, 4, :], func=AF.Copy), reads=["V1"], writes=["V1"])

                chk(9)
                T.alias(["QT"], ["boT"])
                tri = cst[0:64, C_TRI:C_TRI + 64]
                ones64 = cst[0:64, C_ONE:C_ONE + 64]
                sel63 = cst[0:64, C_SEL63:C_SEL63 + 128]
                cneg = cst[0:64, C_CNEG:C_CNEG + 256]
                id64 = cst[0:64, C_ID:C_ID + 64]
                for ch in range(8):
                    c0 = ch * 64
                    pb_, pbn = npb()
                    for j in range(8):
                        T.op("pe", lambda e: e.transpose(out=pb_[0:64, j * 128:(j + 1) * 128], in_=kcT[:, j, c0:c0 + 64], identity=identb),
                             reads=["kcT", "cstb"], writes=[pbn])
                    T.op("act", lambda e: e.activation(out=ktok[:], in_=pb_[0:64, :], func=AF.Copy), reads=[pbn], writes=["ktok"])
                    p, pn = nps()
                    mm(p[0:64, 0:4], tri, lft[:, ch, :], True, True, ["cst", "lft"], [pn])
                    T.op("dve", lambda e: e.tensor_copy(out=b_sb[:], in_=p[0:64, 0:4]), reads=[pn], writes=["b_sb"])
                    T.op("dve", lambda e: e.tensor_tensor(out=g_sb[:], in0=igt[:, ch, :], in1=b_sb[:], op=ALU.subtract), reads=["igt", "b_sb"], writes=["g_sb"])
                    for h in range(4):
                        T.op("dve", lambda e: e.tensor_scalar(out=dg[:, h * 64:(h + 1) * 64], in0=id64, scalar1=g_sb[:, h:h + 1], scalar2=None, op0=ALU.mult),
                             reads=["cst", "g_sb"], writes=["dg"])
                    p, pn = nps()
                    mm(p[0:64, 0:256], ones64, dg[:], True, True, ["cst", "dg"], [pn])
                    T.op("dve", lambda e: e.tensor_tensor(out=Lm[:], in0=p[0:64, 0:256], in1=cneg, op=ALU.add), reads=[pn, "cst"], writes=["Lm"])
                    mmx, iw_, mt_ = stat[:, 0:4], stat[:, 4:8], stat[:, 8:12]
                    T.op("dve", lambda e: e.tensor_reduce(out=small[:, 0:4], in_=Lm[:].rearrange("p (h s) -> p h s", h=4), axis=AX.X, op=ALU.max),
                         reads=["Lm"], writes=["small"])
                    T.op("dve", lambda e: e.tensor_tensor(out=mmx, in0=small[:, 0:4], in1=mprev[0:64, :], op=ALU.max), reads=["small", "mprev"], writes=["stat"])
                    T.op("dve", lambda e: e.tensor_scalar(out=small[:, 4:8], in0=mmx, scalar1=-1.0, scalar2=None, op0=ALU.mult), reads=["stat"], writes=["small"])
                    for h in range(4):
                        T.op("act", lambda e: e.activation(out=dw[:, h * 64:(h + 1) * 64], in_=Lm[:, h * 64:(h + 1) * 64], func=AF.Exp, bias=small[:, 4 + h:5 + h]),
                             reads=["Lm", "small"], writes=["Lm"])
                    T.op("dve", lambda e: e.tensor_tensor(out=small[:, 8:12], in0=mprev[0:64, :], in1=mmx, op=ALU.subtract), reads=["mprev", "stat"], writes=["small"])
                    T.op("act", lambda e: e.activation(out=iw_, in_=small[:, 8:12], func=AF.Exp), reads=["small"], writes=["stat"])
                    T.op("dve", lambda e: e.tensor_tensor(out=mt_, in0=b_sb[:], in1=mmx, op=ALU.add), reads=["b_sb", "stat"], writes=["stat"])
                    T.op("act", lambda e: e.activation(out=small[:, 12:16], in_=mt_, func=AF.Exp, scale=-1.0), reads=["stat"], writes=["small"])
                    p, pn = nps()
                    mm(p[:, 0:12], sel63, stat[:, 0:12], True, True, ["cst", "stat"], [pn])
                    T.op("dve", lambda e: e.tensor_copy(out=bc[:], in_=p[:, 0:12]), reads=[pn], writes=["bc"])
                    T.op("dve", lambda e: e.tensor_tensor(out=wc[:], in0=g_sb[:], in1=bc[0:64, 0:4], op=ALU.subtract), reads=["g_sb", "bc"], writes=["wc"])
                    T.op("act", lambda e: e.activation(out=wc[:], in_=wc[:], func=AF.Exp), reads=["wc"], writes=["wc"])
                    T.op("dve", lambda e: e.tensor_copy(out=mprev[:], in_=bc[:, 8:12]), reads=["bc"], writes=["mprev"])
                    p, pn = nps()
                    for h in range(4):
                        for jj in range(2):
                            j = 2 * h + jj
                            mm(p[0:64, h * 64:(h + 1) * 64], qcT[:, j, c0:c0 + 64], kcT[:, j, c0:c0 + 64], (h == 0 and jj == 0), jj == 1, ["qcT", "kcT"], [pn])
                    T.op("dve", lambda e: e.tensor_tensor(out=Pbf[:], in0=p[0:64, 0:256], in1=dw[:], op=ALU.mult), reads=[pn, "Lm"], writes=["Pbf"])
                    pb_, pbn = npb()
                    for h in range(4):
                        T.op("pe", lambda e: e.transpose(out=pb_[0:64, h * 64:(h + 1) * 64], in_=Pbf[:, h * 64:(h + 1) * 64], identity=identb[0:64, 0:64]),
                             reads=["Pbf", "cstb"], writes=[pbn])
                    T.op("act", lambda e: e.activation(out=PTs[:], in_=pb_[0:64, 0:256], func=AF.Copy), reads=[pbn], writes=["PTs"])
                    for h in range(4):
                        pI, pIn = nps()
                        for jj in range(2):
                            mm(pI[0:64, 0:257], qcT[:, 2 * h + jj, c0:c0 + 64], Cb[:, jj, h, 0:257], jj == 0, jj == 1, ["qcT", "Cb"], [pIn])
                        pA, pAn = nps()
                        mm(pA[0:64, 0:257], PTs[:, h * 64:(h + 1) * 64], vaug[:, ch, h, 0:257], True, True, ["PTs", "vaug"], [pAn])
                        T.op("act", lambda e: e.activation(out=intra[:], in_=pA[0:64, 0:257], func=AF.Copy), reads=[pAn], writes=["intra"])
                        T.op("dve", lambda e: e.scalar_tensor_tensor(out=nd[:], in0=pI[0:64, 0:257], scalar=stat[:, 4 + h:5 + h], in1=intra[:],
                             op0=ALU.mult, op1=ALU.add), reads=[pIn, "stat", "intra"], writes=["intra"])
                        T.op("act", lambda e: e.activation(out=nd[:, 256:257], in_=nd[:, 256:257], func=AF.Abs), reads=["intra"], writes=["intra"])
                        T.op("dve", lambda e: e.tensor_tensor(out=nd[:, 256:257], in0=nd[:, 256:257], in1=small[:, 12 + h:13 + h], op=ALU.max),
                             reads=["intra", "small"], writes=["intra"])
                        T.op("dve", lambda e: e.reciprocal(out=nd[:, 256:257], in_=nd[:, 256:257]), reads=["intra"], writes=["intra"])
                        T.op("dve", lambda e: e.tensor_scalar(out=bo[:, h * 256:(h + 1) * 256], in0=nd[:, 0:256], scalar1=nd[:, 256:257],
                             scalar2=None, op0=ALU.mult), reads=["intra"], writes=["bo"])
                    for h in range(4):
                        T.op("dve", lambda e: e.tensor_scalar(out=ktok[:, h * 256:(h + 1) * 256], in0=ktok[:, h * 256:(h + 1) * 256], scalar1=wc[:, h:h + 1],
                             scalar2=None, op0=ALU.mult), reads=["ktok", "wc"], writes=["ktok"])
                    for h in range(4):
                        for jj in range(2):
                            pC, pCn = nps()
                            mm(pC[:, 0:257], ktok[:, h * 256 + jj * 128:h * 256 + (jj + 1) * 128], vaug[:, ch, h, 0:257], True, True, ["ktok", "vaug"], [pCn])
                            T.op("dve", lambda e: e.scalar_tensor_tensor(out=Cf[:, jj, h, :], in0=Cf[:, jj, h, :], scalar=bc[:, 4 + h:5 + h], in1=pC[:, 0:257],
                                 op0=ALU.mult, op1=ALU.add), reads=["Cf", "bc", pCn], writes=["Cf"])
                            T.op("act", lambda e: e.activation(out=Cb[:, jj, h, 0:257], in_=Cf[:, jj, h, :], func=AF.Copy), reads=["Cf"], writes=["Cb"])
                    pb_, pbn = npb()
                    for j in range(8):
                        T.op("pe", lambda e: e.transpose(out=pb_[:, j * 64:(j + 1) * 64], in_=bo[:, j * 128:(j + 1) * 128], identity=identb[0:64, 0:64]),
                             reads=["bo", "cstb"], writes=[pbn])
                    T.op("dve", lambda e: e.tensor_tensor(out=boT[:, :, c0:c0 + 64], in0=pb_[:, 0:512].rearrange("p (j t) -> p j t", j=8),
                         in1=ogT[:, :, c0:c0 + 64], op=ALU.mult), reads=[pbn, "ogT"], writes=["boT"])
                if last:
                    for h in range(4):
                        for jj in range(2):
                            T.dma("pool", o_Cp[l, h, jj * 128:(jj + 1) * 128, :], Cf[:, jj, h, 0:256], reads=["Cf"], writes=["oC"])
                    T.dma("pool", o_np[l].rearrange("p (j h) -> p j h", j=2), Cf[:, :, :, 256], reads=["Cf"], writes=["on"], slow=True)
                    T.dma("pool", o_mp[l:l + 1, :], mprev[0:1, :], reads=["mprev"], writes=["om"])
                    T.dma("pool", o_mcp[l], carry[:].rearrange("p c r -> p (c r)"), reads=["carry"], writes=["omc"])

                chk(10)
                T.alias(["QT", "qcT", "kcT"], ["sgT", "mT"])
                for gi, which in enumerate(("ga", "gb")):
                    for i2 in range(2):
                        wv, wn = ws.get(l, SIDX[(which, i2)])
                        for c in range(4):
                            p, pn = fm_chunk(wv, wn, c, hT, "hT")
                            T.op("act", lambda e: e.activation(out=sgT[:, 8 * gi + 4 * i2 + c, :], in_=p[:, :], func=AF.Sigmoid), reads=[pn], writes=["sgT"])
                wv, wn = ws.get(l, SIDX[("pa", 0)])
                for cc in range(8):
                    p, pn = nps()
                    for kc in range(4):
                        mm(p[:, :], wv[:, kc, cc * 128:(cc + 1) * 128], aT[:, kc, :], kc == 0, kc == 3, [wn, "aT"], [pn])
                    T.op("dve", lambda e: e.tensor_tensor(out=mT[:, cc, :], in0=p[:, :], in1=sgT[:, cc, :], op=ALU.mult), reads=[pn, "sgT"], writes=["mT"])
                for i2 in range(2):
                    wv, wn = ws.get(l, SIDX[("pb", i2)])
                    for c in range(4):
                        cc = 4 * i2 + c
                        p, pn = fm_chunk(wv, wn, c, boT, "boT")
                        T.op("dve", lambda e: e.tensor_tensor(out=mgA[:], in0=p[:, :], in1=sgT[:, 8 + cc, :], op=ALU.mult), reads=[pn, "sgT"], writes=["sm"])
                        T.op("dve", lambda e: e.tensor_tensor(out=mT[:, cc, :], in0=mT[:, cc, :], in1=mgA[:], op=ALU.add), reads=["mT", "sm"], writes=["mT"])
                for i2 in range(2):
                    wv, wn = ws.get(l, SIDX[("wo", i2)])
                    for s in range(4):
                        p, pn = nps()
                        for kc in range(8):
                            mm(p[:, :], mT[:, kc, s * 128:(s + 1) * 128], wv[:, kc, :], kc == 0, kc == 7, [wn, "mT"], [pn])
                        T.op("dve", lambda e: e.tensor_tensor(out=xt[:, s, 512 * i2:512 * i2 + 512], in0=xt[:, s, 512 * i2:512 * i2 + 512], in1=p[:, :], op=ALU.add),
                             reads=[pn, "xt%d" % s], writes=["xt%d" % s])

                chk(11)
                norm_T(None)
                T.alias(["boT", "sgT", "QT", "qcT", "kcT"], ["actT"])
                for i2 in range(11):
                    wv, wn = ws.get(l, SIDX[("up", i2)])
                    for jc in range(2):
                        j = 2 * i2 + jc
                        for hf in range(2):
                            c = 2 * hf + jc
                            p, pn = fm_chunk(wv, wn, c, hT, "hT")
                            pr, fcn = "pre%d" % hf, "acc%d" % hf
                            T.op("dve", lambda e: e.tensor_copy(out=fpre[:, hf, 0:2], in_=fcar[:, hf, j, :]), reads=["fcar"], writes=[pr])
                            T.op("act", lambda e: e.activation(out=fpre[:, hf, 2:514], in_=p[:, :], func=AF.Copy), reads=[pn], writes=[pr])
                            T.op("dve", lambda e: e.tensor_copy(out=fcar[:, hf, j, :], in_=fpre[:, hf, 512:514]), reads=[pr], writes=["fcar"])
                            T.op("dve", lambda e: e.tensor_scalar(out=fc[:, hf, :], in0=fpre[:, hf, 0:512], scalar1=fw[:, hf, j, 0:1], scalar2=fb[:, hf, j:j + 1],
                                 op0=ALU.mult, op1=ALU.add), reads=[pr, "fw", "fb"], writes=[fcn])
                            for tp in range(1, 3):
                                T.op("dve", lambda e: e.scalar_tensor_tensor(out=fc[:, hf, :], in0=fpre[:, hf, tp:tp + 512], scalar=fw[:, hf, j, tp:tp + 1],
                                     in1=fc[:, hf, :], op0=ALU.mult, op1=ALU.add), reads=[pr, "fw", fcn], writes=[fcn])
                        T.op("act", lambda e: e.activation(out=ft[:], in_=fc[:, 0, :], func=AF.Square), reads=["acc0"], writes=["rden"])
                        T.op("dve", lambda e: e.tensor_scalar(out=ft[:], in0=ft[:], scalar1=0.044715, scalar2=1.0, op0=ALU.mult, op1=ALU.add), reads=["rden"], writes=["rden"])
                        T.op("dve", lambda e: e.tensor_tensor(out=ft[:], in0=ft[:], in1=fc[:, 0, :], op=ALU.mult), reads=["rden", "acc0"], writes=["rden"])
                        T.op("act", lambda e: e.activation(out=ft[:], in_=ft[:], func=AF.Sigmoid, scale=1.5957691216057308), reads=["rden"], writes=["rden"])
                        T.op("dve", lambda e: e.tensor_tensor(out=ft[:], in0=ft[:], in1=fc[:, 0, :], op=ALU.mult), reads=["rden", "acc0"], writes=["rden"])
                        T.op("dve", lambda e: e.tensor_tensor(out=actT[:, j, :], in0=ft[:], in1=fc[:, 1, :], op=ALU.mult), reads=["rden", "acc1"], writes=["actT"])
                if last:
                    T.dma("pool", o_fcp[l], fcar[:].rearrange("p h j r -> p (h j r)"), reads=["fcar"], writes=["ofc"])
                for i2 in range(8):
                    wv, wn = ws.get(l, SIDX[("dn", i2)])
                    for s in range(4):
                        p, pn = nps()
                        for kc in range(22):
                            mm(p[:, 0:128], actT[:, kc, s * 128:(s + 1) * 128], wv[:, kc, :], kc == 0, kc == 21, [wn, "actT"], [pn])
                        T.op("dve", lambda e: e.tensor_tensor(out=xt[:, s, 128 * i2:128 * i2 + 128], in0=xt[:, s, 128 * i2:128 * i2 + 128], in1=p[:, 0:128], op=ALU.add),
                             reads=[pn, "xt%d" % s], writes=["xt%d" % s])
                chk(12)
                if l == 0:
                    for s in range(4):
                        T.dma("pool", xscr[t0 + s * 128:t0 + (s + 1) * 128, :], xt[:, s, :], reads=["xt%d" % s], writes=["xscr"])
                else:
                    T.op("dve", lambda e: e.memset(ss[:], 0.0), writes=["ss"])
                    for s in range(4):
                        T.op("act", lambda e: e.activation(out=junk[:], in_=xt[:, s, :], func=AF.Square, accum_out=ss[:, s:s + 1]),
                             reads=["xt%d" % s, "ss"], writes=["hb", "ss"])
                    T.op("act", lambda e: e.activation(out=rs[:], in_=ss[:], func=AF.Ln, scale=1.0 / D, bias=cst[:, C_EPS:C_EPS + 1]), reads=["ss", "cst"], writes=["rs"])
                    T.op("act", lambda e: e.activation(out=rs[:], in_=rs[:], func=AF.Exp, scale=-0.5), reads=["rs"], writes=["rs"])
                    for s in range(4):
                        if s == 0:
                            T.dma("pool", pre[:, :, 0:512], fin_g[0:1, :].broadcast_to([128, D]).rearrange("p (a b) -> p a b", a=2), writes=["pre0", "pre1"])
                        T.op("dve", lambda e: e.scalar_tensor_tensor(out=acc[:], in0=xt[:, s, :].rearrange("p (a b) -> p a b", a=2), scalar=rs[:, s:s + 1],
                             in1=pre[:, :, 0:512], op0=ALU.mult, op1=ALU.mult), reads=["xt%d" % s, "rs", "pre0", "pre1"], writes=["acc0", "acc1"])
                        T.dma("pool", o_yp[t0 + s * 128:t0 + (s + 1) * 128, :].rearrange("p (a b) -> p a b", a=2), acc[:], reads=["acc0", "acc1"], writes=["oy"])
        T.barrier()
    except StopBuild:
        T.barrier()
        return nc


    if do_sample:
      try:
        ws.extend([(l, si) for l in range(DEPTH) for si in range(NSLAB)])
        SIDX = {}
        for si, sdef in enumerate(SLABS):
            SIDX[(sdef[0], sdef[1])] = si

        def mm(out, lhsT, rhs, start, stop, reads, writes, **kw):
            T.op("pe", lambda e: e.matmul(out, lhsT, rhs, start=start, stop=stop, skip_group_check=True, **kw), reads=reads, writes=writes)

        sctr = [0]

        def nps2():
            i = 4 + (sctr[0] % 2)
            sctr[0] += 1
            return psum[i], "ps%d" % i

        with contextlib.ExitStack() as esx:
            xs_t = esx.enter_context(nc.sbuf_tensor("xs_t", [4, D], F32))
            cst2 = esx.enter_context(nc.sbuf_tensor("cst2", [128, 528], F32))
            s_ss = esx.enter_context(nc.sbuf_tensor("s_ss", [4, 2], F32))
            s_hb = esx.enter_context(nc.sbuf_tensor("s_hb", [4, D], BF16))
            s_hT = esx.enter_context(nc.sbuf_tensor("s_hT", [128, 8, 4], BF16))
            s_mT = esx.enter_context(nc.sbuf_tensor("s_mT", [128, 22, 4], BF16))
            s_jk = esx.enter_context(nc.sbuf_tensor("s_jk", [4, D], F32))
            T.dma("pool", xs_t[:], xs[:, :], writes=["xs_t"])
            T.dma("pool", cst2[:], consts2[:, :], writes=["cst2"])
            selrow = lambda b: cst2[0:4, 128 * b:128 * (b + 1)]
            selcol = lambda b: cst2[:, 512 + 4 * b:512 + 4 * (b + 1)]
            oh4 = cst2[0:4, 512:516]
            identf4 = cst[0:4, C_ID:C_ID + 4]

            def s_norm(dstT, nm):
                T.op("dve", lambda e: e.memset(s_ss[:], 0.0), writes=["s_ss"])
                T.op("act", lambda e: e.activation(out=s_jk[:], in_=xs_t[:], func=AF.Square, accum_out=s_ss[:, 0:1]), reads=["xs_t", "s_ss"], writes=["s_jk", "s_ss"])
                T.op("act", lambda e: e.activation(out=s_ss[:, 1:2], in_=s_ss[:, 0:1], func=AF.Ln, scale=1.0 / D, bias=cst[0:4, C_EPS:C_EPS + 1]), reads=["s_ss", "cst"], writes=["s_ss"])
                T.op("act", lambda e: e.activation(out=s_ss[:, 1:2], in_=s_ss[:, 1:2], func=AF.Exp, scale=-0.5), reads=["s_ss"], writes=["s_ss"])
                T.op("dve", lambda e: e.tensor_scalar(out=s_hb[:], in0=xs_t[:], scalar1=s_ss[:, 1:2], scalar2=None, op0=ALU.mult), reads=["xs_t", "s_ss"], writes=["s_hb"])
                tr_bf(s_hb, "s_hb", 8, dstT, nm)

            def tr_bf(src, sname, nchunk, dstT, dname):
                pb_, pbn = npb()
                for kc in range(nchunk):
                    T.op("pe", lambda e: e.transpose(out=pb_[:, kc * 4:(kc + 1) * 4], in_=src[0:4, kc * 128:(kc + 1) * 128], identity=identb[0:4, 0:4]),
                         reads=[sname, "cstb"], writes=[pbn])
                T.op("act", lambda e: e.activation(out=dstT[:, 0:nchunk, :], in_=pb_[:, 0:nchunk * 4].rearrange("p (k t) -> p k t", k=nchunk), func=AF.Copy),
                     reads=[pbn], writes=[dname])

            def proj4(wv, wn, srcT, sname, nk, ncols, c0=0):
                p, pn = nps2()
                for kc in range(nk):
                    mm(p[0:4, 0:ncols], srcT[:, kc, 0:4], wv[:, kc, c0:c0 + ncols], kc == 0, kc == nk - 1, [wn, sname], [pn])
                return p, pn

            for l in range(DEPTH):
                with contextlib.ExitStack() as esm:
                    def sbm(name, shape, dt=F32):
                        return esm.enter_context(nc.sbuf_tensor(name + "_L%d" % l, list(shape), dt))
                    qa4 = sbm("qa4", [4, 1536]); ka4 = sbm("ka4", [4, 1536]); va4 = sbm("va4", [4, 1536])
                    qkpre4 = sbm("qkpre4", [4, 2048]); qk4 = sbm("qk4", [4, 2048]); tmp4 = sbm("tmp4", [4, 2048])
                    vaug4 = sbm("vaug4", [4, 4, 257]); og4 = sbm("og4", [4, D]); sg4 = sbm("sg4", [4, 2048])
                    gt4 = sbm("gt4", [4, 8]); gb4 = sbm("gb4", [4, 8])
                    bs_t = sbm("bs_t", [128, 24]); b0_t = sbm("b0_t", [4, 24])
                    p0 = sbm("p0", [4, 24]); sm4 = sbm("sm4", [4, 64])

                    T.dma("pool", gb4[:], mgate_b[l:l + 1, :].broadcast_to([4, 8]), writes=["gb4"])
                    T.dma("pool", bs_t[:], bias_s[:, :], writes=["bs_t"])
                    T.dma("pool", b0_t[:], bias_0[:, :], writes=["b0_t"])
                    T.dma("pool", sm4[:, 0:4], st_m[l], writes=["sm4"])
                    T.op("dve", lambda e: e.memset(vaug4[:], 1.0), writes=["vaug4"])

                    s_norm(s_hT, "s_hT")
                    for kind, dst, dn_ in (("qa", qa4, "qa4"), ("ka", ka4, "ka4"), ("va", va4, "va4")):
                        for i in range(3):
                            wv, wn = ws.get(l, SIDX[(kind, i)])
                            p, pn = proj4(wv, wn, s_hT, "s_hT", 8, 512)
                            T.op("act", lambda e: e.activation(out=dst[:, 512 * i:512 * (i + 1)], in_=p[0:4, 0:512], func=AF.Copy), reads=[pn], writes=[dn_])
                    for kind, off in (("qb", 0), ("kb", 1024)):
                        for i in range(2):
                            wv, wn = ws.get(l, SIDX[(kind, i)])
                            p, pn = proj4(wv, wn, s_hT, "s_hT", 8, 512)
                            T.op("act", lambda e: e.activation(out=qkpre4[:, off + 512 * i:off + 512 * (i + 1)], in_=p[0:4, 0:512], func=AF.Copy), reads=[pn], writes=["qkpre4"])
                    for i in range(2):
                        wv, wn = ws.get(l, SIDX[("vb", i)])
                        p, pn = proj4(wv, wn, s_hT, "s_hT", 8, 512)
                        T.op("act", lambda e: e.activation(out=vaug4[:, 2 * i:2 * i + 2, 0:256], in_=p[0:4, 0:512].rearrange("p (h v) -> p h v", h=2), func=AF.Copy),
                             reads=[pn], writes=["vaug4"])
                    for i in range(2):
                        wv, wn = ws.get(l, SIDX[("ob", i)])
                        p, pn = proj4(wv, wn, s_hT, "s_hT", 8, 512)
                        T.op("act", lambda e: e.activation(out=og4[:, 512 * i:512 * (i + 1)], in_=p[0:4, 0:512], func=AF.Sigmoid), reads=[pn], writes=["og4"])
                    wv, wn = ws.get(l, SIDX[("gt", 0)])
                    p, pn = proj4(wv, wn, s_hT, "s_hT", 8, 8)
                    T.op("dve", lambda e: e.tensor_tensor(out=gt4[:], in0=p[0:4, 0:8], in1=gb4[:], op=ALU.add), reads=[pn, "gb4"], writes=["gt4"])
                    for gi, kind in enumerate(("ga", "gb")):
                        for i in range(2):
                            wv, wn = ws.get(l, SIDX[(kind, i)])
                            p, pn = proj4(wv, wn, s_hT, "s_hT", 8, 512)
                            T.op("act", lambda e: e.activation(out=sg4[:, 1024 * gi + 512 * i:1024 * gi + 512 * (i + 1)], in_=p[0:4, 0:512], func=AF.Sigmoid),
                                 reads=[pn], writes=["sg4"])
                    chk(20)
                    for g in range(3):
                        T.dma("pool", o_kvs[g][l, :, 0:512], ka4[:, 512 * g:512 * (g + 1)], reads=["ka4"], writes=["okvs"])
                        T.dma("pool", o_kvs[g][l, :, 512:1024], va4[:, 512 * g:512 * (g + 1)], reads=["va4"], writes=["okvs"])

                    chk(21)
                    with contextlib.ExitStack() as esc:
                        stm = esc.enter_context(nc.sbuf_tensor("stm_L%d" % l, [4, 3, 2048], F32))
                        cw4 = esc.enter_context(nc.sbuf_tensor("cw4_L%d" % l, [4, 4, 2048], F32))
                        cb4 = esc.enter_context(nc.sbuf_tensor("cb4_L%d" % l, [4, 2048], F32))
                        T.dma("pool", stm[:], st_mconv[l], writes=["stm"])
                        T.dma("pool", cw4[:], mconv_w[l:l + 1].broadcast_to([4, 4, 2048]), writes=["cw4"])
                        T.dma("pool", cb4[:], mconv_b[l:l + 1, :].broadcast_to([4, 2048]), writes=["cb4"])
                        T.op("dve", lambda e: e.tensor_tensor(out=qk4[:], in0=qkpre4[:], in1=cw4[:, 3, :], op=ALU.mult), reads=["qkpre4", "cw4"], writes=["qk4"])
                        T.op("dve", lambda e: e.tensor_tensor(out=qk4[:], in0=qk4[:], in1=cb4[:], op=ALU.add), reads=["qk4", "cb4"], writes=["qk4"])
                        for i in range(3):
                            T.op("dve", lambda e: e.tensor_tensor(out=tmp4[:], in0=stm[:, i, :], in1=cw4[:, i, :], op=ALU.mult), reads=["stm", "cw4"], writes=["tmp4"])
                            T.op("dve", lambda e: e.tensor_tensor(out=qk4[:], in0=qk4[:], in1=tmp4[:], op=ALU.add), reads=["qk4", "tmp4"], writes=["qk4"])
                        T.op("act", lambda e: e.activation(out=qk4[:], in_=qk4[:], func=AF.Silu), reads=["qk4"], writes=["qk4"])
                        T.op("dve", lambda e: e.tensor_scalar(out=qk4[:, 1024:2048], in0=qk4[:, 1024:2048], scalar1=1.0 / 16.0, scalar2=None, op0=ALU.mult), reads=["qk4"], writes=["qk4"])

                        T.barrier()
                    ckv = [sbm("ckv%d" % i, [128, 1024]) for i in range(2)]
                    prod = sbm("prod", [128, 512]); pv = sbm("pv", [128, 512])
                    sT = sbm("sT", [128, 8]); pT = sbm("pT", [128, 8])
                    nv = sbm("nv", [4, 1536]); numt = sbm("numt", [4, 512]); dent = sbm("dent", [4, 8])
                    a4b = sbm("a4b", [4, 512], BF16); aT4 = sbm("aT4", [128, 4, 4], BF16)
                    qT4f = sbm("qT4f", [128, 8, 4]); qTm = sbm("qTm", [128, 4, 8, 4])
                    Ct = [sbm("Ct%d" % i, [128, 2, 257]) for i in range(2)]
                    Cn = [sbm("Cn%d" % i, [128, 2, 257]) for i in range(2)]
                    vm = sbm("vm", [4, 4, 4, 257]); ibc = sbm("ibc", [128, 4, 4])
                    hq = sbm("hq", [4, 4, 257]); bo4b = sbm("bo4b", [4, D], BF16); boT4 = sbm("boT4", [128, 8, 4], BF16)
                    mg4 = sbm("mg4", [4, D]); mg4b = sbm("mg4b", [4, D], BF16)
                    T.op("dve", lambda e: e.memset(qTm[:], 0.0), writes=["qTm"])
                    chk(22)
                    psO, psD = psum[0], psum[1]
                    firstO = True
                    it = 0
                    for b in range(4):
                        for g in range(3):
                            ck = ckv[it % 2]
                            ckn = "ckv%d" % (it % 2)
                            it += 1
                            T.dma("pool", ck[:], cache[g][l, b], writes=[ckn])
                            pq, pqn = nps2()
                            mm(pq[:, 0:512], selrow(b), qa4[:, 512 * g:512 * (g + 1)], True, True, ["cst2", "qa4"], [pqn])
                            T.op("dve", lambda e: e.tensor_tensor(out=prod[:], in0=ck[:, 0:512], in1=pq[:, 0:512], op=ALU.mult), reads=[ckn, pqn], writes=["prod"])
                            T.op("dve", lambda e: e.tensor_reduce(out=sT[:], in_=prod[:].rearrange("p (h c) -> p h c", h=8), axis=AX.X, op=ALU.add),
                                 reads=["prod"], writes=["sT"])
                            T.op("dve", lambda e: e.scalar_tensor_tensor(out=sT[:], in0=sT[:], scalar=0.125, in1=bs_t[:, 8 * g:8 * (g + 1)], op0=ALU.mult, op1=ALU.add),
                                 reads=["sT", "bs_t"], writes=["sT"])
                            T.op("act", lambda e: e.activation(out=pT[:], in_=sT[:], func=AF.Exp), reads=["sT"], writes=["pT"])
                            for h in range(8):
                                T.op("dve", lambda e: e.tensor_scalar(out=pv[:, 64 * h:64 * (h + 1)], in0=ck[:, 512 + 64 * h:512 + 64 * (h + 1)], scalar1=pT[:, h:h + 1],
                                     scalar2=None, op0=ALU.mult), reads=[ckn, "pT"], writes=["pv"])
                            mm(psO[0:4, 0:512], selcol(b), pv[:], firstO, False, ["cst2", "pv"], ["ps0"])
                            mm(psD[0:4, 0:8], selcol(b), pT[:], firstO, False, ["cst2", "pT"], ["ps1"])
                            firstO = False
                    T.op("dve", lambda e: e.tensor_tensor(out=nv[:], in0=qa4[:], in1=ka4[:], op=ALU.mult), reads=["qa4", "ka4"], writes=["nv"])
                    T.op("dve", lambda e: e.tensor_reduce(out=p0[:], in_=nv[:].rearrange("p (h c) -> p h c", h=24), axis=AX.X, op=ALU.add), reads=["nv"], writes=["p0"])
                    T.op("dve", lambda e: e.scalar_tensor_tensor(out=p0[:], in0=p0[:], scalar=0.125, in1=b0_t[:], op0=ALU.mult, op1=ALU.add), reads=["p0", "b0_t"], writes=["p0"])
                    T.op("act", lambda e: e.activation(out=p0[:], in_=p0[:], func=AF.Exp), reads=["p0"], writes=["p0"])
                    for hh in range(24):
                        T.op("dve", lambda e: e.tensor_scalar(out=nv[:, 64 * hh:64 * (hh + 1)], in0=va4[:, 64 * hh:64 * (hh + 1)], scalar1=p0[:, hh:hh + 1], scalar2=None,
                             op0=ALU.mult), reads=["va4", "p0", "nv"], writes=["nv"])
                    T.op("dve", lambda e: e.tensor_tensor(out=numt[:], in0=nv[:, 0:512], in1=psO[0:4, 0:512], op=ALU.add), reads=["nv", "ps0"], writes=["numt"])
                    T.op("dve", lambda e: e.tensor_tensor(out=dent[:], in0=p0[:, 0:8], in1=psD[0:4, 0:8], op=ALU.add), reads=["p0", "ps1"], writes=["dent"])
                    for g in (1, 2):
                        T.op("dve", lambda e: e.tensor_tensor(out=numt[:], in0=numt[:], in1=nv[:, 512 * g:512 * (g + 1)], op=ALU.add), reads=["numt", "nv"], writes=["numt"])
                        T.op("dve", lambda e: e.tensor_tensor(out=dent[:], in0=dent[:], in1=p0[:, 8 * g:8 * (g + 1)], op=ALU.add), reads=["dent", "p0"], writes=["dent"])
                    T.op("dve", lambda e: e.reciprocal(out=dent[:], in_=dent[:]), reads=["dent"], writes=["dent"])
                    for h in range(8):
                        T.op("dve", lambda e: e.tensor_scalar(out=a4b[:, 64 * h:64 * (h + 1)], in0=numt[:, 64 * h:64 * (h + 1)], scalar1=dent[:, h:h + 1], scalar2=None,
                             op0=ALU.mult), reads=["numt", "dent"], writes=["a4b"])
                    tr_bf(a4b, "a4b", 4, aT4, "aT4")

                    chk(23)
                    T.dma("pool", o_mcs[l, :, 0:2, :], st_mconv[l, :, 1:3, :], writes=["omcs"])
                    T.dma("pool", o_mcs[l, :, 2, :], qkpre4[:], reads=["qkpre4"], writes=["omcs"])
                    m0, ig, bm, mt, dwv, iwv, emm, qkw, denv, tq = (sm4[:, 4 * i:4 * i + 4] for i in range(10))
                    T.op("dve", lambda e: e.tensor_copy(out=ig, in_=gt4[:, 0:4]), reads=["gt4"], writes=["sm4"])
                    T.op("act", lambda e: e.activation(out=bm, in_=gt4[:, 4:8], func=AF.Exp, scale=-1.0), reads=["gt4"], writes=["sm4"])
                    T.op("act", lambda e: e.activation(out=bm, in_=bm, func=AF.Ln, bias=cst[0:4, C_ONE:C_ONE + 1]), reads=["sm4", "cst"], writes=["sm4"])
                    T.op("dve", lambda e: e.scalar_tensor_tensor(out=bm, in0=bm, scalar=-1.0, in1=m0, op0=ALU.mult, op1=ALU.add), reads=["sm4"], writes=["sm4"])
                    T.op("dve", lambda e: e.tensor_tensor(out=mt, in0=bm, in1=ig, op=ALU.max), reads=["sm4"], writes=["sm4"])
                    T.op("dve", lambda e: e.tensor_tensor(out=dwv, in0=ig, in1=mt, op=ALU.subtract), reads=["sm4"], writes=["sm4"])
                    T.op("act", lambda e: e.activation(out=dwv, in_=dwv, func=AF.Exp), reads=["sm4"], writes=["sm4"])
                    T.op("dve", lambda e: e.tensor_tensor(out=iwv, in0=bm, in1=mt, op=ALU.subtract), reads=["sm4"], writes=["sm4"])
                    T.op("act", lambda e: e.activation(out=iwv, in_=iwv, func=AF.Exp), reads=["sm4"], writes=["sm4"])
                    T.op("act", lambda e: e.activation(out=emm, in_=mt, func=AF.Exp, scale=-1.0), reads=["sm4"], writes=["sm4"])
                    T.dma("pool", o_ms[l], mt, reads=["sm4"], writes=["oms"])
                    T.op("dve", lambda e: e.tensor_tensor(out=tmp4[:, 0:1024], in0=qk4[:, 0:1024], in1=qk4[:, 1024:2048], op=ALU.mult), reads=["qk4"], writes=["tmp4"])
                    T.op("dve", lambda e: e.tensor_reduce(out=qkw, in_=tmp4[:, 0:1024].rearrange("p (h c) -> p h c", h=4), axis=AX.X, op=ALU.add), reads=["tmp4"], writes=["sm4"])
                    T.op("dve", lambda e: e.tensor_tensor(out=qkw, in0=qkw, in1=dwv, op=ALU.mult), reads=["sm4"], writes=["sm4"])
                    pq, pqn = nps2()
                    for kc in range(8):
                        T.op("pe", lambda e: e.transpose(out=pq[:, kc * 4:(kc + 1) * 4], in_=qk4[0:4, kc * 128:(kc + 1) * 128], identity=identf4), reads=["qk4", "cst"], writes=[pqn])
                    T.op("act", lambda e: e.activation(out=qT4f[:], in_=pq[:, 0:32].rearrange("p (k t) -> p k t", k=8), func=AF.Copy), reads=[pqn], writes=["qT4f"])
                    for b in range(4):
                        T.op("dve", lambda e: e.tensor_copy(out=qTm[:, b, :, b], in_=qT4f[:, :, b]), reads=["qT4f", "qTm"], writes=["qTm"])
                    for b in range(4):
                        for h in range(4):
                            T.op("dve", lambda e: e.tensor_scalar(out=vm[:, b, h, :], in0=vaug4[:, h, :], scalar1=dwv[:, h:h + 1], scalar2=cst[0:4, C_ID + b:C_ID + b + 1],
                                 op0=ALU.mult, op1=ALU.mult), reads=["vaug4", "sm4", "cst", "vm"], writes=["vm"])
                    it = 0
                    for b in range(4):
                        pi, pin = nps2()
                        mm(pi[:, 0:4], selrow(b), iwv, True, True, ["cst2", "sm4"], [pin])
                        T.op("dve", lambda e: e.tensor_copy(out=ibc[:, b, :], in_=pi[:, 0:4]), reads=[pin], writes=["ibc%d" % b])
                        for h in range(4):
                            k2 = it % 2
                            it += 1
                            ctn, cnn = "Ct%d" % k2, "Cn%d" % k2
                            T.dma("pool", Ct[k2][:, :, 0:256], st_C[l, b, h].rearrange("(j p) v -> p j v", p=128), writes=[ctn])
                            T.dma("pool", Ct[k2][:, :, 256], st_n[l, b, h].rearrange("(j p) -> p j", p=128), writes=[ctn], slow=True)
                            for jj in range(2):
                                mm(psum[h][0:4, 0:257], qTm[:, b, 2 * h + jj, :], Ct[k2][:, jj, :], (b == 0 and jj == 0), False, ["qTm", ctn], ["ps%d" % h])
                            for jj in range(2):
                                pk, pkn = nps2()
                                mm(pk[:, 0:257], qk4[0:4, 1024 + h * 256 + jj * 128:1024 + h * 256 + (jj + 1) * 128], vm[:, b, h, :], True, True, ["qk4", "vm"], [pkn])
                                T.op("dve", lambda e: e.scalar_tensor_tensor(out=Cn[k2][:, jj, :], in0=Ct[k2][:, jj, :], scalar=ibc[:, b, h:h + 1], in1=pk[:, 0:257],
                                     op0=ALU.mult, op1=ALU.add), reads=[ctn, "ibc%d" % b, pkn], writes=[cnn])
                            T.dma("pool", o_Cs[l, b, h].rearrange("(j p) v -> p j v", p=128), Cn[k2][:, :, 0:256], reads=[cnn], writes=["oCs"])
                            T.dma("pool", o_ns[l, b, h].rearrange("(j p) -> p j", p=128), Cn[k2][:, :, 256], reads=[cnn], writes=["ons"], slow=True)
                    for h in range(4):
                        T.op("act", lambda e: e.activation(out=hq[:, h, :], in_=psum[h][0:4, 0:257], func=AF.Copy), reads=["ps%d" % h], writes=["hq"])
                    for h in range(4):
                        T.op("dve", lambda e: e.tensor_scalar(out=hq[:, h, :], in0=hq[:, h, :], scalar1=iwv[:, h:h + 1], scalar2=None, op0=ALU.mult), reads=["hq", "sm4"], writes=["hq"])
                        T.op("dve", lambda e: e.scalar_tensor_tensor(out=hq[:, h, :], in0=vaug4[:, h, :], scalar=qkw[:, h:h + 1], in1=hq[:, h, :], op0=ALU.mult, op1=ALU.add),
                             reads=["vaug4", "sm4", "hq"], writes=["hq"])
                        T.op("act", lambda e: e.activation(out=denv[:, h:h + 1], in_=hq[:, h, 256:257], func=AF.Abs), reads=["hq"], writes=["sm4"])
                    T.op("dve", lambda e: e.tensor_tensor(out=denv, in0=denv, in1=emm, op=ALU.max), reads=["sm4"], writes=["sm4"])
                    T.op("dve", lambda e: e.reciprocal(out=denv, in_=denv), reads=["sm4"], writes=["sm4"])
                    for h in range(4):
                        T.op("dve", lambda e: e.scalar_tensor_tensor(out=bo4b[:, 256 * h:256 * (h + 1)], in0=hq[:, h, 0:256], scalar=denv[:, h:h + 1], in1=og4[:, 256 * h:256 * (h + 1)],
                             op0=ALU.mult, op1=ALU.mult), reads=["hq", "sm4", "og4"], writes=["bo4b"])
                    tr_bf(bo4b, "bo4b", 8, boT4, "boT4")

                    chk(24)
                    wv, wn = ws.get(l, SIDX[("pa", 0)])
                    for i in range(2):
                        p, pn = proj4(wv, wn, aT4, "aT4", 4, 512, c0=512 * i)
                        T.op("dve", lambda e: e.tensor_tensor(out=mg4[:, 512 * i:512 * (i + 1)], in0=p[0:4, 0:512], in1=sg4[:, 512 * i:512 * (i + 1)], op=ALU.mult),
                             reads=[pn, "sg4"], writes=["mg4"])
                    for i in range(2):
                        wv, wn = ws.get(l, SIDX[("pb", i)])
                        p, pn = proj4(wv, wn, boT4, "boT4", 8, 512)
                        T.op("dve", lambda e: e.tensor_tensor(out=tmp4[:, 0:512], in0=p[0:4, 0:512], in1=sg4[:, 1024 + 512 * i:1024 + 512 * (i + 1)], op=ALU.mult),
                             reads=[pn, "sg4"], writes=["tmp4"])
                        T.op("dve", lambda e: e.tensor_tensor(out=mg4[:, 512 * i:512 * (i + 1)], in0=mg4[:, 512 * i:512 * (i + 1)], in1=tmp4[:, 0:512], op=ALU.add),
                             reads=["mg4", "tmp4"], writes=["mg4"])
                    T.op("dve", lambda e: e.tensor_copy(out=mg4b[:], in_=mg4[:]), reads=["mg4"], writes=["mg4b"])
                    tr_bf(mg4b, "mg4b", 8, s_mT, "s_mT")
                    for i in range(2):
                        wv, wn = ws.get(l, SIDX[("wo", i)])
                        p, pn = proj4(wv, wn, s_mT, "s_mT", 8, 512)
                        T.op("dve", lambda e: e.tensor_tensor(out=xs_t[:, 512 * i:512 * (i + 1)], in0=xs_t[:, 512 * i:512 * (i + 1)], in1=p[0:4, 0:512], op=ALU.add),
                             reads=[pn, "xs_t"], writes=["xs_t"])
                    T.barrier()
                chk(25)
                with contextlib.ExitStack() as esf:
                    def sbf(name, shape, dt=F32):
                        return esf.enter_context(nc.sbuf_tensor(name + "_L%d" % l, list(shape), dt))
                    u4 = sbf("u4", [4, 2 * DFF]); cv4 = sbf("cv4", [4, 2 * DFF]); tf4 = sbf("tf4", [4, DFF])
                    sfc = sbf("sfc", [4, 2, DFF]); fw4 = sbf("fw4", [4, 3, DFF]); fb4 = sbf("fb4", [4, DFF])
                    ac4b = sbf("ac4b", [4, DFF], BF16)
                    s_norm(s_hT, "s_hT")
                    for i in range(11):
                        wv, wn = ws.get(l, SIDX[("up", i)])
                        p, pn = proj4(wv, wn, s_hT, "s_hT", 8, 512)
                        T.op("act", lambda e: e.activation(out=u4[:, 256 * i:256 * (i + 1)], in_=p[0:4, 0:256], func=AF.Copy), reads=[pn], writes=["u4"])
                        T.op("act", lambda e: e.activation(out=u4[:, DFF + 256 * i:DFF + 256 * (i + 1)], in_=p[0:4, 256:512], func=AF.Copy), reads=[pn], writes=["u4"])
                    T.dma("pool", o_fcs[l, :, 0, :], st_fconv[l, :, 1, :], writes=["ofcs"])
                    T.dma("pool", o_fcs[l, :, 1, :], u4[:], reads=["u4"], writes=["ofcs"])
                    for hh in range(2):
                        hs = slice(hh * DFF, (hh + 1) * DFF)
                        T.dma("pool", sfc[:], st_fconv[l, :, :, hs], writes=["sfc"])
                        T.dma("pool", fw4[:], fconv_w[l:l + 1, :, hs].broadcast_to([4, 3, DFF]), writes=["fw4"])
                        T.dma("pool", fb4[:], fconv_b[l:l + 1, hs].broadcast_to([4, DFF]), writes=["fb4"])
                        T.op("dve", lambda e: e.tensor_tensor(out=cv4[:, hs], in0=u4[:, hs], in1=fw4[:, 2, :], op=ALU.mult), reads=["u4", "fw4"], writes=["cv4"])
                        T.op("dve", lambda e: e.tensor_tensor(out=cv4[:, hs], in0=cv4[:, hs], in1=fb4[:], op=ALU.add), reads=["cv4", "fb4"], writes=["cv4"])
                        for i in range(2):
                            T.op("dve", lambda e: e.tensor_tensor(out=tf4[:], in0=sfc[:, i, :], in1=fw4[:, i, :], op=ALU.mult), reads=["sfc", "fw4"], writes=["tf4"])
                            T.op("dve", lambda e: e.tensor_tensor(out=cv4[:, hs], in0=cv4[:, hs], in1=tf4[:], op=ALU.add), reads=["cv4", "tf4"], writes=["cv4"])
                    c1, c2, tt_ = cv4[:, 0:DFF], cv4[:, DFF:2 * DFF], tf4[:, 0:DFF]
                    T.op("act", lambda e: e.activation(out=tt_, in_=c1, func=AF.Square), reads=["cv4"], writes=["tf4"])
                    T.op("dve", lambda e: e.tensor_scalar(out=tt_, in0=tt_, scalar1=0.044715, scalar2=1.0, op0=ALU.mult, op1=ALU.add), reads=["tf4"], writes=["tf4"])
                    T.op("dve", lambda e: e.tensor_tensor(out=tt_, in0=tt_, in1=c1, op=ALU.mult), reads=["tf4", "cv4"], writes=["tf4"])
                    T.op("act", lambda e: e.activation(out=tt_, in_=tt_, func=AF.Sigmoid, scale=1.5957691216057308), reads=["tf4"], writes=["tf4"])
                    T.op("dve", lambda e: e.tensor_tensor(out=tt_, in0=tt_, in1=c1, op=ALU.mult), reads=["tf4", "cv4"], writes=["tf4"])
                    T.op("dve", lambda e: e.tensor_tensor(out=ac4b[:], in0=tt_, in1=c2, op=ALU.mult), reads=["tf4", "cv4"], writes=["ac4b"])
                    tr_bf(ac4b, "ac4b", 22, s_mT, "s_mT")
                    for i in range(8):
                        wv, wn = ws.get(l, SIDX[("dn", i)])
                        p, pn = proj4(wv, wn, s_mT, "s_mT", 22, 128)
                        T.op("dve", lambda e: e.tensor_tensor(out=xs_t[:, 128 * i:128 * (i + 1)], in0=xs_t[:, 128 * i:128 * (i + 1)], in1=p[0:4, 0:128], op=ALU.add),
                             reads=[pn, "xs_t"], writes=["xs_t"])
                    T.barrier()
            with contextlib.ExitStack() as esy:
                gf4 = esy.enter_context(nc.sbuf_tensor("gf4", [4, D], F32))
                T.dma("pool", gf4[:], fin_g[0:1, :].broadcast_to([4, D]), writes=["gf4"])
                T.op("dve", lambda e: e.memset(s_ss[:], 0.0), writes=["s_ss"])
                T.op("act", lambda e: e.activation(out=s_jk[:], in_=xs_t[:], func=AF.Square, accum_out=s_ss[:, 0:1]), reads=["xs_t", "s_ss"], writes=["s_jk", "s_ss"])
                T.op("act", lambda e: e.activation(out=s_ss[:, 1:2], in_=s_ss[:, 0:1], func=AF.Ln, scale=1.0 / D, bias=cst[0:4, C_EPS:C_EPS + 1]), reads=["s_ss", "cst"], writes=["s_ss"])
                T.op("act", lambda e: e.activation(out=s_ss[:, 1:2], in_=s_ss[:, 1:2], func=AF.Exp, scale=-0.5), reads=["s_ss"], writes=["s_ss"])
                T.op("dve", lambda e: e.scalar_tensor_tensor(out=s_jk[:], in0=xs_t[:], scalar=s_ss[:, 1:2], in1=gf4[:], op0=ALU.mult, op1=ALU.mult),
                     reads=["xs_t", "s_ss", "gf4"], writes=["s_jk"])
                T.dma("pool", o_ys[:, :], s_jk[:], reads=["s_jk"], writes=["oys"])
                T.barrier()
      except StopBuild:
        T.barrier()
        return nc

    T.barrier()
    es.close()
    return nc


def _bias_tables(rel_bias):
    k = np.arange(128)[:, None]
    q = np.arange(128)[None, :]
    bg = np.zeros((128, 24, 2, 128), np.float32)
    bm = np.zeros((128, 24, 2, 128), np.float32)
    for h in range(24):
        d = DIL[h // 8]
        rel0 = q + 128 - k
        rel1 = q - k
        for kind, rel in ((0, rel0), (1, rel1)):
            valid = (rel >= 0) & (rel <= 128)
            bk = t5_bucket(np.maximum(rel, 0) * d)
            bg[:, h, kind, :] = rel_bias[bk, h]
            bm[:, h, kind, :] = np.where(valid, 0.0, NEG)
    return bg.reshape(128, -1), bm.reshape(128, -1)


def _sample_bias(rel_bias):
    jj = np.arange(128)
    bs = np.zeros((128, 24), np.float32)
    b0 = np.zeros((4, 24), np.float32)
    for h in range(24):
        d = DIL[h // 8]
        bs[:, h] = rel_bias[t5_bucket((128 - jj) * d), h]
        b0[:, h] = rel_bias[0, h]
    return bs, b0


def make_in_maps(inp, ncores=8):
    f = lambda a: np.ascontiguousarray(a, dtype=np.float32)
    rel_bias = np.asarray(inp["rel_bias"], np.float32)
    bg, bm = _bias_tables(rel_bias)
    bs, b0 = _sample_bias(rel_bias)
    consts = make_consts()
    c2 = np.zeros((128, 528), np.float32)
    for b in range(4):
        c2[b, 128 * b:128 * (b + 1)] = 1.0
        c2[:, 512 + 4 * b + b] = 1.0
    shared = {
        "consts2": c2, "bias_g": bg, "bias_m": bm, "bias_s": bs, "bias_0": b0, "consts": consts,
        "w_in": f(inp["w_in"]), "w_pa": f(inp["w_pa"]), "w_pb": f(inp["w_pb"]), "w_o": f(inp["w_o"]),
        "w_up": f(inp["w_up"]), "w_down": f(inp["w_down"]),
        "norm1_g": f(inp["norm1_g"]), "norm2_g": f(inp["norm2_g"]),
        "mconv_w": f(inp["mconv_w"]), "mconv_b": f(inp["mconv_b"]),
        "mgate_b": f(np.asarray(inp["mgate_b"]).reshape(2, 8)),
        "fconv_w": f(inp["fconv_w"]), "fconv_b": f(inp["fconv_b"]),
        "fin_g": f(np.asarray(inp["final_norm_g"]).reshape(1, D)),
        "cw_t": f(np.asarray(inp["mconv_w"]).reshape(2, 4, 16, 128).transpose(0, 3, 2, 1).reshape(2, 128, 64)),
        "cb_t": f(np.asarray(inp["mconv_b"]).reshape(2, 16, 128).transpose(0, 2, 1)),
        "fw_t": f(np.asarray(inp["fconv_w"]).reshape(2, 3, 2, 22, 128).transpose(0, 4, 2, 3, 1).reshape(2, 128, 132)),
        "fb_t": f(np.asarray(inp["fconv_b"]).reshape(2, 2, 22, 128).transpose(0, 3, 1, 2).reshape(2, 128, 44)),
        "gn_t": f(np.stack([np.asarray(inp["norm1_g"]), np.asarray(inp["norm2_g"])], 0).reshape(2, 2, 8, 128).transpose(3, 0, 1, 2).reshape(128, 32)),
    }
    caches = [np.asarray(inp["cache_kv_w128"]), np.asarray(inp["cache_kv_w512"]), np.asarray(inp["cache_kv_w2048"])]
    maps = []
    for c in range(ncores):
        sl = slice(4 * c, 4 * c + 4)
        m = dict(shared)
        m["xp"] = f(np.asarray(inp["x_prompt"])[c % 4])
        m["xs"] = f(np.asarray(inp["x_sample"])[sl, 0, :])
        for g in range(3):
            m["cache%d" % g] = f(caches[g][:, sl, ::DIL[g]].reshape(2, 4, 128, 1024))
        m["st_mconv"] = f(np.asarray(inp["state_mlstm_conv"])[:, sl])
        m["st_C"] = f(np.asarray(inp["state_mlstm_C"])[:, sl])
        m["st_n"] = f(np.asarray(inp["state_mlstm_n"])[:, sl])
        m["st_m"] = f(np.asarray(inp["state_mlstm_m"])[:, sl])
        m["st_fconv"] = f(np.asarray(inp["state_ffn_conv"])[:, sl])
        maps.append(m)
    return maps


def assemble(results):
    R = results
    cat_s = lambda name, axis=1: np.concatenate([R[c][name] for c in range(8)], axis=axis)
    stk_p = lambda name: np.stack([R[b][name] for b in range(4)], axis=1)
    y_prompt = np.stack([R[b]["o_yp"] for b in range(4)], axis=0)
    y_sample = np.concatenate([R[c]["o_ys"] for c in range(8)], axis=0).reshape(32, 1, D)
    outs = [y_prompt, y_sample]
    for g in range(3):
        outs.append(stk_p("o_kvp%d" % g).reshape(2, 4, WIN[g], 2, 8, 64))
        outs.append(cat_s("o_kvs%d" % g).reshape(2, 32, 1, 2, 8, 64))
    outs.append(stk_p("o_mcp").reshape(2, 4, 128, 16, 3).transpose(0, 1, 4, 3, 2).reshape(2, 4, 3, 2048))
    outs.append(cat_s("o_mcs"))
    outs.append(stk_p("o_Cp"))
    outs.append(cat_s("o_Cs"))
    outs.append(stk_p("o_np").reshape(2, 4, 128, 2, 4).transpose(0, 1, 4, 3, 2).reshape(2, 4, 4, 256))
    outs.append(cat_s("o_ns"))
    outs.append(stk_p("o_mp"))
    outs.append(cat_s("o_ms"))
    outs.append(stk_p("o_fcp").reshape(2, 4, 128, 2, 22, 2).transpose(0, 1, 5, 3, 4, 2).reshape(2, 4, 2, 2 * DFF))
    outs.append(cat_s("o_fcs"))
    return tuple(np.ascontiguousarray(o, dtype=np.float32) for o in outs)


def kernel(**inputs):
    nc = build()
    maps = make_in_maps(inputs)
    res = run_bass_kernel_spmd(nc, maps, core_ids=list(range(8)))
    return assemble(res.results)
```

```python
import contextlib
import math
import numpy as np
import concourse.bass as bass
import concourse.mybir as mybir
from concourse.bass_utils import run_bass_kernel_spmd

F32 = mybir.dt.float32
BF16 = mybir.dt.bfloat16
AF = mybir.ActivationFunctionType
ALU = mybir.AluOpType
AX = mybir.AxisListType

D = 1024
SEQ = 8192
DEPTH = 2
TT = 512
NT_FULL = SEQ // TT
DIL = (1, 4, 16)
WIN = (128, 512, 2048)
DFF = 2816
INC = 10760
SLOT = 4096
DMA_ALL_SP = False
NEG = -30000.0
EPS = 1e-6

O_QA, O_KA, O_VA, O_QB, O_KB, O_VB, O_OB, O_I, O_GA, O_GB = 0, 1536, 3072, 4608, 5632, 6656, 7680, 8704, 8712, 9736


def slab_list():
    L = []
    for i in range(3):
        L.append(("qa", i, "w_in", 8, [(O_QA + 512 * i, 512)], 1))
    for i in range(3):
        L.append(("ka", i, "w_in", 8, [(O_KA + 512 * i, 512)], 1))
    for i in range(3):
        L.append(("va", i, "w_in", 8, [(O_VA + 512 * i, 512)], 1))
    for i in range(2):
        L.append(("qb", i, "w_in", 8, [(O_QB + 512 * i, 512)], 1))
    for i in range(2):
        L.append(("kb", i, "w_in", 8, [(O_KB + 512 * i, 512)], 1))
    for i in range(2):
        L.append(("vb", i, "w_in", 8, [(O_VB + 512 * i, 512)], 1))
    for i in range(2):
        L.append(("ob", i, "w_in", 8, [(O_OB + 512 * i, 512)], 1))
    L.append(("gt", 0, "w_in", 8, [(O_I, 8)], 1))
    for i in range(2):
        L.append(("ga", i, "w_in", 8, [(O_GA + 512 * i, 512)], 1))
    for i in range(2):
        L.append(("gb", i, "w_in", 8, [(O_GB + 512 * i, 512)], 1))
    L.append(("pa", 0, "w_pa", 4, [(0, 1024)], 0))
    for i in range(2):
        L.append(("pb", i, "w_pb", 8, [(512 * i, 512)], 0))
    for i in range(2):
        L.append(("wo", i, "w_o", 8, [(512 * i, 512)], 0))
    for i in range(11):
        L.append(("up", i, "w_up", 8, [(256 * i, 256), (DFF + 256 * i, 256)], 2))
    for i in range(8):
        L.append(("dn", i, "w_down", 22, [(128 * i, 128)], 0))
    return L


SLABS = slab_list()
NSLAB = len(SLABS)


class Trk:
    NDS = 48

    def __init__(self, nc, es):
        self.nc = nc
        self.E = {"pe": nc.tensor, "act": nc.scalar, "dve": nc.vector, "pool": nc.gpsimd, "sp": nc.sync}
        self.sem = {e: es.enter_context(nc.semaphore("sem_" + e)) for e in self.E}
        self.cnt = {e: 0 for e in self.E}
        self.seen = {e: {} for e in self.E}
        self.res = {}
        self.dsems = [es.enter_context(nc.semaphore("dsem%d" % i)) for i in range(self.NDS)]
        self.dcnt = [0] * self.NDS
        self.dnext = 0

    def semh(self, key):
        return self.sem[key] if isinstance(key, str) else self.dsems[key[1]]

    def _wait(self, e, key, val):
        if e == "pe" and key == "pe":
            return
        if self.seen[e].get(key, 0) >= val:
            return
        self.seen[e][key] = val
        self.E[e].wait_ge(self.semh(key), val)

    def deps(self, e, reads, writes):
        toks = {}

        def add(k, v):
            if toks.get(k, 0) < v:
                toks[k] = v
        for r in reads:
            st = self.res.get(r)
            if st and st["w"]:
                add(*st["w"])
        for w in writes:
            st = self.res.get(w)
            if st:
                if st["w"]:
                    add(*st["w"])
                for k, v in st["r"].items():
                    add(k, v)
        for k, v in toks.items():
            self._wait(e, k, v)

    def commit(self, tok, reads, writes):
        for r in reads:
            st = self.res.setdefault(r, {"w": None, "r": {}})
            if st["r"].get(tok[0], 0) < tok[1]:
                st["r"][tok[0]] = tok[1]
        for w in writes:
            self.res[w] = {"w": tok, "r": {}}

    def op(self, e, fn, reads=(), writes=()):
        self.deps(e, reads, writes)
        ins = fn(self.E[e])
        self.cnt[e] += 1
        ins.then_inc(self.sem[e], 1)
        self.commit((e, self.cnt[e]), reads, writes)

    def dma(self, q, out, in_, reads=(), writes=(), slow=False):
        if DMA_ALL_SP:
            q = "sp"
        self.deps(q, reads, writes)
        i = self.dnext
        self.dnext = (i + 1) % self.NDS
        if self.dcnt[i] > 0:
            self._wait(q, ("d", i), 16 * self.dcnt[i])
        self.dcnt[i] += 1
        if slow:
            ins = self.E[q].dma_start(out=out, in_=in_, allow_slow_non_contiguous=True)
        else:
            ins = self.E[q].dma_start(out=out, in_=in_)
        ins.then_inc(self.dsems[i], 16)
        self.commit((("d", i), 16 * self.dcnt[i]), reads, writes)

    def alias(self, olds, news):
        for n in news:
            st = self.res.setdefault(n, {"w": None, "r": {}})
            for o in olds:
                so = self.res.get(o)
                if not so:
                    continue
                if so["w"] and st["r"].get(so["w"][0], 0) < so["w"][1]:
                    st["r"][so["w"][0]] = so["w"][1]
                for k, v in so["r"].items():
                    if st["r"].get(k, 0) < v:
                        st["r"][k] = v

    def barrier(self):
        for e in self.E:
            for f in self.E:
                if f != e and self.cnt[f] > 0:
                    self._wait(e, f, self.cnt[f])
            for i in range(self.NDS):
                if self.dcnt[i] > 0:
                    self._wait(e, ("d", i), 16 * self.dcnt[i])
        self.res = {}


def t5_bucket(dist):
    dist = np.asarray(dist, dtype=np.int64)
    df = np.maximum(dist, 1).astype(np.float32)
    large = 16 + (np.log(df / np.float32(16)) / np.float32(math.log(2048 / 16)) * np.float32(16)).astype(np.int32)
    large = np.minimum(large, 31)
    return np.where(dist < 16, dist, large).astype(np.int64)


C_ID, C_TRI, C_ONE, C_SEL63, C_CNEG, C_EPS, C_END = 0, 128, 192, 320, 448, 704, 708


def make_consts():
    c = np.zeros((128, C_END), np.float32)
    c[:, C_ID:C_ID + 128] = np.eye(128, dtype=np.float32)
    s = np.arange(64)
    c[:64, C_TRI:C_TRI + 64] = (s[:, None] <= s[None, :]).astype(np.float32)
    c[:, C_ONE:C_ONE + 128] = 1.0
    c[63, C_SEL63:C_SEL63 + 128] = 1.0
    cn = np.where(s[None, :] <= s[:, None], 0.0, -1e30).astype(np.float32)
    c[:64, C_CNEG:C_CNEG + 256] = np.tile(cn, (1, 4))
    c[:, C_EPS] = EPS
    return c


class StopBuild(Exception):
    pass


def build(NT=NT_FULL, do_sample=True, stage=99):
    def chk(n):
        if abs(stage) == n:
            raise StopBuild()

    nc = bass.Bass("TRN2", target_bir_lowering=False)
    es = contextlib.ExitStack()

    def din(name, shape):
        return nc.dram_tensor(name, list(shape), F32, kind="ExternalInput").ap()

    def dout(name, shape):
        return nc.dram_tensor(name, list(shape), F32, kind="ExternalOutput").ap()

    xp = din("xp", [SEQ, D])
    xs = din("xs", [4, D])
    cache = [din("cache%d" % g, [2, 4, 128, 1024]) for g in range(3)]
    st_mconv = din("st_mconv", [2, 4, 3, 2048])
    st_C = din("st_C", [2, 4, 4, 256, 256])
    st_n = din("st_n", [2, 4, 4, 256])
    st_m = din("st_m", [2, 4, 4])
    st_fconv = din("st_fconv", [2, 4, 2, 2 * DFF])
    bias_g = din("bias_g", [128, 24 * 2 * 128])
    bias_m = din("bias_m", [128, 24 * 2 * 128])
    bias_s = din("bias_s", [128, 24])
    bias_0 = din("bias_0", [4, 24])
    consts = din("consts", [128, C_END])
    consts2 = din("consts2", [128, 528])
    W = {
        "w_in": din("w_in", [2, D, INC]), "w_pa": din("w_pa", [2, 512, D]), "w_pb": din("w_pb", [2, D, D]),
        "w_o": din("w_o", [2, D, D]), "w_up": din("w_up", [2, D, 2 * DFF]), "w_down": din("w_down", [2, DFF, D]),
    }
    norm1_g = din("norm1_g", [2, D])
    norm2_g = din("norm2_g", [2, D])
    mconv_w = din("mconv_w", [2, 4, 2048])
    mconv_b = din("mconv_b", [2, 2048])
    mgate_b = din("mgate_b", [2, 8])
    fconv_w = din("fconv_w", [2, 3, 2 * DFF])
    fconv_b = din("fconv_b", [2, 2 * DFF])
    fin_g = din("fin_g", [1, D])
    cw_t = din("cw_t", [2, 128, 64])
    cb_t = din("cb_t", [2, 128, 16])
    fw_t = din("fw_t", [2, 128, 132])
    fb_t = din("fb_t", [2, 128, 44])
    gn_t = din("gn_t", [128, 32])

    o_yp = dout("o_yp", [SEQ, D])
    o_ys = dout("o_ys", [4, D])
    o_kvp = [dout("o_kvp%d" % g, [2, WIN[g], 1024]) for g in range(3)]
    o_kvs = [dout("o_kvs%d" % g, [2, 4, 1024]) for g in range(3)]
    o_mcp = dout("o_mcp", [2, 128, 48])
    o_mcs = dout("o_mcs", [2, 4, 3, 2048])
    o_Cp = dout("o_Cp", [2, 4, 256, 256])
    o_Cs = dout("o_Cs", [2, 4, 4, 256, 256])
    o_np = dout("o_np", [2, 128, 8])
    o_ns = dout("o_ns", [2, 4, 4, 256])
    o_mp = dout("o_mp", [2, 4])
    o_ms = dout("o_ms", [2, 4, 4])
    o_fcp = dout("o_fcp", [2, 128, 88])
    o_fcs = dout("o_fcs", [2, 4, 2, 2 * DFF])

    wbf = nc.dram_tensor("wbf", [2, NSLAB, 128, SLOT], BF16, kind="Internal").ap()
    xscr = nc.dram_tensor("xscr", [SEQ, D], F32, kind="Internal").ap()
    k3d = nc.dram_tensor("k3d", [4, 128, 16, 2, 128], BF16, kind="Internal").ap()
    v3d = nc.dram_tensor("v3d", [4, 128, 16, 2, 128], BF16, kind="Internal").ap()

    T = Trk(nc, es)

    def sb(name, shape, dt=F32):
        return es.enter_context(nc.sbuf_tensor(name, list(shape), dt))

    cst = sb("cst", [128, C_END])
    cstb = sb("cstb", [128, 320], BF16)
    T.dma("pool", cst[:], consts[:, :], writes=["cst"])
    T.op("dve", lambda e: e.tensor_copy(out=cstb[:, 0:320], in_=cst[:, 0:320]), reads=["cst"], writes=["cstb"])
    identb = cstb[:, 0:128]
    onesb = cstb[:, 192:320]
    identf = cst[:, C_ID:C_ID + 128]

    psum = [es.enter_context(nc.psum_tensor("ps%d" % i, [128, 512], F32)) for i in range(6)]
    psb = [es.enter_context(nc.psum_tensor("psb%d" % i, [128, 1024], BF16)) for i in range(2)]
    pctr = [0, 0]

    def nps():
        i = pctr[0] % 4
        pctr[0] += 1
        return psum[i], "ps%d" % i

    def npb():
        i = pctr[1] % 2
        pctr[1] += 1
        return psb[i], "psb%d" % i

    wsl = sb("wsl", [128, 3, SLOT], BF16)

    class WS:
        def __init__(self):
            self.seq = []
            self.i = 0
            self.issued = 0

        def extend(self, items):
            self.seq.extend(items)

        def _issue(self):
            if self.issued < len(self.seq):
                l, si = self.seq[self.issued]
                k = self.issued % 3
                kind, idx, src, nk, cols, gain = SLABS[si]
                n = nk * sum(c[1] for c in cols)
                T.dma("sp", wsl[:, k, 0:n], wbf[l, si, :, 0:n], reads=["wbf%d_%d" % (l, si)], writes=["wsl%d" % k])
                self.issued += 1

        def get(self, l, si):
            assert self.seq[self.i] == (l, si), (self.seq[self.i], l, si)
            while self.issued < min(self.i + 3, len(self.seq)):
                self._issue()
            k = self.i % 3
            self.i += 1
            kind, idx, src, nk, cols, gain = SLABS[si]
            nc_ = sum(c[1] for c in cols)
            return wsl[:, k, 0:nk * nc_].rearrange("p (k c) -> p k c", k=nk), "wsl%d" % k

    ws = WS()

    with contextlib.ExitStack() as es0:
        stg = [es0.enter_context(nc.sbuf_tensor("stg%d" % i, [128, SLOT], F32)) for i in range(2)]
        stb = [es0.enter_context(nc.sbuf_tensor("stb%d" % i, [128, SLOT], BF16)) for i in range(2)]
        gn = es0.enter_context(nc.sbuf_tensor("gn", [128, 2, 2, 8], F32))
        T.dma("pool", gn[:].rearrange("p a b c -> p (a b c)"), gn_t[:, :], writes=["gn"])
        j = 0
        for l in range(2):
            for si, (kind, idx, src, nk, cols, gain) in enumerate(SLABS):
                if stage < 0:
                    continue
                k = j % 2
                j += 1
                nc_ = sum(c[1] for c in cols)
                n = nk * nc_
                sview = stg[k][:, 0:n].rearrange("p (k c) -> p k c", k=nk)
                bview = stb[k][:, 0:n].rearrange("p (k c) -> p k c", k=nk)
                wsrc = W[src][l].rearrange("(k p) c -> p k c", p=128)
                off = 0
                for (c0, cn) in cols:
                    T.dma("sp" if (j % 2) else "pool", sview[:, :, off:off + cn], wsrc[:, :, c0:c0 + cn],
                          writes=["stg%d" % k], slow=(cn < 128))
                    off += cn
                if gain:
                    for kc in range(nk):
                        eng = "dve" if kc % 2 == 0 else "act"
                        if eng == "dve":
                            T.op("dve", lambda e, kc=kc: e.tensor_scalar(out=bview[:, kc, :], in0=sview[:, kc, :],
                                 scalar1=gn[:, gain - 1, l, kc:kc + 1], scalar2=None, op0=ALU.mult),
                                 reads=["stg%d" % k, "gn"], writes=["stb%d" % k])
                        else:
                            T.op("act", lambda e, kc=kc: e.activation(out=bview[:, kc, :], in_=sview[:, kc, :],
                                 func=AF.Copy, scale=gn[:, gain - 1, l, kc:kc + 1]),
                                 reads=["stg%d" % k, "gn"], writes=["stb%d" % k])
                else:
                    h = n // 2
                    T.op("dve", lambda e: e.tensor_copy(out=stb[k][:, 0:h], in_=stg[k][:, 0:h]),
                         reads=["stg%d" % k], writes=["stb%d" % k])
                    T.op("act", lambda e: e.activation(out=stb[k][:, h:n], in_=stg[k][:, h:n], func=AF.Copy),
                         reads=["stg%d" % k], writes=["stb%d" % k])
                T.dma("sp" if (j % 2) else "pool", wbf[l, si, :, 0:n], stb[k][:, 0:n], reads=["stb%d" % k],
                      writes=["wbf%d_%d" % (l, si)])
        T.barrier()

    try:
      with contextlib.ExitStack() as es1:
        if stage == 0:
            raise StopBuild()
        def sb1(name, shape, dt=F32):
            return es1.enter_context(nc.sbuf_tensor(name, list(shape), dt))

        xt = sb1("xt", [128, 4, D])
        hb = sb1("hb", [128, D], BF16)
        junk = hb
        hT = sb1("hT", [128, 8, TT], BF16)
        ss = sb1("ss", [128, 4])
        rs = sb1("rs", [128, 4])
        U = sb1("U", [128, 32, TT], BF16)
        QT, qcT, kcT = U[:, 0:12, :], U[:, 12:20, :], U[:, 20:28, :]
        boT, sgT, mT, actT = U[:, 0:8, :], U[:, 8:24, :], U[:, 24:32, :], U[:, 0:22, :]
        KT1 = sb1("KT1", [128, 4, 640], BF16)
        KT2 = sb1("KT2", [128, 4, 4, 2, 128], BF16)
        KT3 = sb1("KT3", [128, 16, 2, 128], BF16)
        k3n = sb1("k3n", [128, 16, 32], BF16)
        v3n = sb1("v3n", [128, 2, 512], BF16)
        V1 = sb1("V1", [128, 5, 512], BF16)
        V2 = sb1("V2", [128, 4, 2, 512], BF16)
        V3 = sb1("V3", [128, 16, 2, 128], BF16)
        btab = sb1("btab", [128, 24, 2, 128], BF16)
        pre = sb1("pre", [128, 2, 516])
        acc = sb1("acc", [128, 2, 512])
        fpre, fc = pre, acc
        cw = sb1("cw", [128, 16, 4])
        cb = sb1("cb", [128, 16])
        carry = sb1("carry", [128, 16, 3])
        vaug = sb1("vaug", [64, 8, 4, 258], BF16)
        ogT = sb1("ogT", [128, 8, TT], BF16)
        graw = sb1("graw", [64, 8, 8])
        gbias = sb1("gbias", [64, 8])
        igt = sb1("igt", [64, 8, 4])
        lft = sb1("lft", [64, 8, 4])
        aT = sb1("aT", [128, 4, TT], BF16)
        Cf = sb1("Cf", [128, 2, 4, 257])
        Cb = sb1("Cb", [128, 2, 4, 258], BF16)
        mprev = sb1("mprev", [128, 4])
        sm = sb1("sm", [128, 512], BF16)
        mgA = sm
        stmp = sb1("stmp", [128, 256])
        rden = sb1("rden", [128, 512])
        stf = rden
        ft = rden
        g_sb = sb1("g_sb", [64, 4])
        b_sb = sb1("b_sb", [64, 4])
        dg = sb1("dg", [64, 256])
        Lm = sb1("Lm", [64, 256])
        dw = Lm
        stat = sb1("stat", [64, 12])
        small = sb1("small", [64, 16])
        bc = sb1("bc", [128, 12])
        wc = sb1("wc", [64, 4])
        Pbf = sb1("Pbf", [64, 256], BF16)
        PTs = sb1("PTs", [64, 256], BF16)
        ktok = sb1("ktok", [64, D], BF16)
        intra = sb1("intra", [64, 257])
        nd = intra
        bo = sb1("bo", [64, D], BF16)
        fw = sb1("fw", [128, 2, 22, 3])
        fb = sb1("fb", [128, 2, 22])
        fcar = sb1("fcar", [128, 2, 22, 2])

        T.op("dve", lambda e: e.memset(vaug[:], 1.0), writes=["vaug"])
        for nm, t in (("KT1", KT1), ("KT2", KT2), ("KT3", KT3), ("V1", V1), ("V2", V2), ("V3", V3)):
            T.op("dve", lambda e, t=t: e.memset(t[:], 0.0), writes=[nm])
        btf = acc[:, 0, :]
        btm = acc[:, 1, :]
        for q in range(12):
            T.dma("pool", btf, bias_g[:, q * 512:(q + 1) * 512], writes=["acc0"])
            T.dma("pool", btm, bias_m[:, q * 512:(q + 1) * 512], writes=["acc1"])
            T.op("dve", lambda e, q=q: e.tensor_tensor(out=btab[:, 2 * q:2 * q + 2, :, :].rearrange("p a b c -> p (a b c)"),
                 in0=btf, in1=btm, op=ALU.add), reads=["acc0", "acc1"], writes=["btab"])

        chk(1)

        def norm_T(srcs):
            T.op("dve", lambda e: e.memset(ss[:], 0.0), writes=["ss"])
            for s in range(4):
                T.op("act", lambda e: e.activation(out=junk[:], in_=xt[:, s, :], func=AF.Square, accum_out=ss[:, s:s + 1]),
                     reads=["xt%d" % s, "ss"], writes=["hb", "ss"])
            T.op("act", lambda e: e.activation(out=rs[:], in_=ss[:], func=AF.Ln, scale=1.0 / D, bias=cst[:, C_EPS:C_EPS + 1]), reads=["ss", "cst"], writes=["rs"])
            T.op("act", lambda e: e.activation(out=rs[:], in_=rs[:], func=AF.Exp, scale=-0.5), reads=["rs"], writes=["rs"])
            for s in range(4):
                T.op("dve", lambda e: e.tensor_scalar(out=hb[:], in0=xt[:, s, :], scalar1=rs[:, s:s + 1], scalar2=None, op0=ALU.mult),
                     reads=["xt%d" % s, "rs"], writes=["hb"])
                pb_, pn = npb()
                for kc in range(8):
                    T.op("pe", lambda e: e.transpose(out=pb_[:, kc * 128:(kc + 1) * 128], in_=hb[:, kc * 128:(kc + 1) * 128], identity=identb),
                         reads=["hb", "cstb"], writes=[pn])
                T.op("act", lambda e: e.activation(out=hT[:, :, s * 128:(s + 1) * 128], in_=pb_[:, :].rearrange("p (k t) -> p k t", k=8), func=AF.Copy),
                     reads=[pn], writes=["hT"])

        def mm(out, lhsT, rhs, start, stop, reads, writes, **kw):
            T.op("pe", lambda e: e.matmul(out, lhsT, rhs, start=start, stop=stop, skip_group_check=True, **kw), reads=reads, writes=writes)

        def fm_chunk(wv, wn, c, rhsT, rname, nk=8):
            p, pn = nps()
            for kc in range(nk):
                mm(p[:, :], wv[:, kc, c * 128:(c + 1) * 128], rhsT[:, kc, :], kc == 0, kc == nk - 1, [wn, rname], [pn])
            return p, pn

        for l in range(DEPTH):
            ws.extend([(l, si) for _ in range(NT) for si in range(NSLAB)])
        SIDX = {}
        for si, sdef in enumerate(SLABS):
            SIDX[(sdef[0], sdef[1])] = si

        for l in range(DEPTH):
            xsrc = xp if l == 0 else xscr
            T.dma("pool", cw[:].rearrange("p c k -> p (c k)"), cw_t[l], writes=["cw"])
            T.dma("pool", cb[:], cb_t[l], writes=["cb"])
            T.dma("pool", fw[:].rearrange("p h j k -> p (h j k)"), fw_t[l], writes=["fw"])
            T.dma("pool", fb[:].rearrange("p h j -> p (h j)"), fb_t[l], writes=["fb"])
            T.dma("pool", gbias[:], mgate_b[l:l + 1, :].broadcast_to([64, 8]), writes=["gbias"])
            T.op("dve", lambda e: e.memset(carry[:], 0.0), writes=["carry"])
            T.op("dve", lambda e: e.memset(fcar[:], 0.0), writes=["fcar"])
            T.op("dve", lambda e: e.memset(Cf[:], 0.0), writes=["Cf"])
            T.op("dve", lambda e: e.memset(Cb[:], 0.0), writes=["Cb"])
            T.op("dve", lambda e: e.memset(mprev[:], 0.0), writes=["mprev"])
            T.op("dve", lambda e: e.memset(KT3[:], 0.0), writes=["KT3"])
            for c in range(4):
                T.dma("pool", k3d[c], KT3[:], reads=["KT3"], writes=["k3d%d" % c])
                T.dma("pool", v3d[c], KT3[:], reads=["KT3"], writes=["v3d%d" % c])

            for ti in range(NT):
                t0 = ti * TT
                last = (ti == NT - 1)
                for s in range(4):
                    T.dma("pool", xt[:, s, :], xsrc[t0 + s * 128:t0 + (s + 1) * 128, :], reads=["xscr"] if l else [], writes=["xt%d" % s])
                chk(2)
                norm_T(None)
                chk(3)
                T.alias(["actT", "mT", "sgT", "boT"], ["QT", "qcT", "kcT"])
                par2 = ti % 2
                v3 = ti % 4
                par3 = (ti // 4) % 2

                for g in range(3):
                    wv, wn = ws.get(l, SIDX[("qa", g)])
                    d = DIL[g]
                    for c in range(4):
                        p, pn = fm_chunk(wv, wn, c, hT, "hT")
                        if d == 1:
                            T.op("act", lambda e: e.activation(out=QT[:, c, :], in_=p[:, :], func=AF.Copy), reads=[pn], writes=["QT"])
                        else:
                            T.op("act", lambda e: e.activation(out=QT[:, 4 * g + c, :].rearrange("p (r u) -> p r u", r=d),
                                 in_=p[:, :].rearrange("p (u r) -> p r u", r=d), func=AF.Copy), reads=[pn], writes=["QT"])
                chk(4)
                for g in range(3):
                    wv, wn = ws.get(l, SIDX[("ka", g)])
                    d = DIL[g]
                    for c in range(4):
                        p, pn = fm_chunk(wv, wn, c, hT, "hT")
                        if g == 0:
                            T.op("dve", lambda e: e.tensor_copy(out=KT1[:, c, 128:640], in_=p[:, :]), reads=[pn], writes=["KT1"])
                        elif g == 1:
                            T.op("dve", lambda e: e.tensor_copy(out=KT2[:, c, :, par2, :], in_=p[:, :].rearrange("p (u r) -> p r u", r=4)),
                                 reads=[pn], writes=["KT2"])
                        else:
                            T.op("dve", lambda e: e.tensor_copy(out=k3n[:], in_=p[:, :].rearrange("p (u r) -> p r u", r=16)), reads=[pn], writes=["k3n"])
                            T.dma("pool", k3d[c, :, :, par3, 32 * v3:32 * v3 + 32], k3n[:], reads=["k3n"], writes=["k3d%d" % c], slow=True)
                    row0 = t0 - (NT * TT - WIN[g])
                    for s in range(4):
                        r0 = row0 + s * 128
                        if r0 >= 0:
                            p, pn = nps()
                            for kc in range(8):
                                mm(p[:, :], hT[:, kc, s * 128:(s + 1) * 128], wv[:, kc, :], kc == 0, kc == 7, [wn, "hT"], [pn])
                            T.op("act", lambda e: e.activation(out=stf[:], in_=p[:, :], func=AF.Copy), reads=[pn], writes=["rden"])
                            T.dma("pool", o_kvp[g][l, r0:r0 + 128, 0:512], stf[:], reads=["rden"], writes=["okv"])
                chk(5)
                for g in range(3):
                    wv, wn = ws.get(l, SIDX[("va", g)])
                    row0 = t0 - (NT * TT - WIN[g])
                    if g == 0:
                        for s in range(4):
                            p, pn = nps()
                            for kc in range(8):
                                mm(p[:, :], hT[:, kc, s * 128:(s + 1) * 128], wv[:, kc, :], kc == 0, kc == 7, [wn, "hT"], [pn])
                            T.op("act", lambda e: e.activation(out=V1[:, s + 1, :], in_=p[:, :], func=AF.Copy), reads=[pn], writes=["V1"])
                            r0 = row0 + s * 128
                            if r0 >= 0:
                                T.op("dve", lambda e: e.tensor_copy(out=stf[:], in_=p[:, :]), reads=[pn], writes=["rden", pn])
                                T.dma("pool", o_kvp[g][l, r0:r0 + 128, 512:1024], stf[:], reads=["rden"], writes=["okv"])
                    else:
                        d = DIL[g]
                        hTr = hT[:, :, :].rearrange("p k (u r) -> p k r u", r=d)
                        for r in range(d):
                            p, pn = nps()
                            if g == 1:
                                for kc in range(8):
                                    mm(p[:, :], hTr[:, kc, r, :], wv[:, kc, :], kc == 0, kc == 7, [wn, "hT"], [pn])
                                T.op("act", lambda e: e.activation(out=V2[:, r, par2, :], in_=p[:, :], func=AF.Copy), reads=[pn], writes=["V2"])
                            else:
                                lo = 32 * v3
                                for kc in range(8):
                                    mm(p[lo:lo + 32, :], hTr[:, kc, r, :], wv[:, kc, :], kc == 0, kc == 7, [wn, "hT"], [pn], tile_position=(0, lo))
                                vk = r % 2
                                T.op("act", lambda e: e.activation(out=v3n[lo:lo + 32, vk, :], in_=p[lo:lo + 32, :], func=AF.Copy), reads=[pn], writes=["v3n%d" % vk])
                                T.dma("pool", v3d[:, lo:lo + 32, r, par3, :].rearrange("c p f -> p c f"), v3n[lo:lo + 32, vk, :].rearrange("p (c f) -> p c f", c=4),
                                      reads=["v3n%d" % vk], writes=["v3d0", "v3d1", "v3d2", "v3d3"], slow=True)
                        for s in range(4):
                            r0 = row0 + s * 128
                            if r0 >= 0:
                                p, pn = nps()
                                for kc in range(8):
                                    mm(p[:, :], hT[:, kc, s * 128:(s + 1) * 128], wv[:, kc, :], kc == 0, kc == 7, [wn, "hT"], [pn])
                                T.op("dve", lambda e: e.tensor_copy(out=stf[:], in_=p[:, :]), reads=[pn], writes=["rden"])
                                T.dma("pool", o_kvp[g][l, r0:r0 + 128, 512:1024], stf[:], reads=["rden"], writes=["okv"])
                chk(6)
                for which, dst in (("qb", qcT), ("kb", kcT)):
                    for i2 in range(2):
                        wv, wn = ws.get(l, SIDX[(which, i2)])
                        for c in range(4):
                            ch = (0 if which == "qb" else 8) + 4 * i2 + c
                            k2 = ch % 2
                            p, pn = fm_chunk(wv, wn, c, hT, "hT")
                            pr, ac = "pre%d" % k2, "acc%d" % k2
                            T.op("dve", lambda e: e.tensor_copy(out=pre[:, k2, 0:3], in_=carry[:, ch, :]), reads=["carry"], writes=[pr])
                            T.op("act", lambda e: e.activation(out=pre[:, k2, 3:515], in_=p[:, :], func=AF.Copy), reads=[pn], writes=[pr])
                            T.op("dve", lambda e: e.tensor_copy(out=carry[:, ch, :], in_=pre[:, k2, 512:515]), reads=[pr], writes=["carry"])
                            T.op("dve", lambda e: e.tensor_scalar(out=acc[:, k2, :], in0=pre[:, k2, 0:512], scalar1=cw[:, ch, 0:1], scalar2=cb[:, ch:ch + 1],
                                 op0=ALU.mult, op1=ALU.add), reads=[pr, "cw", "cb"], writes=[ac])
                            for tp in range(1, 4):
                                T.op("dve", lambda e: e.scalar_tensor_tensor(out=acc[:, k2, :], in0=pre[:, k2, tp:tp + 512], scalar=cw[:, ch, tp:tp + 1],
                                     in1=acc[:, k2, :], op0=ALU.mult, op1=ALU.add), reads=[pr, "cw", ac], writes=[ac])
                            cc = 4 * i2 + c
                            if which == "qb":
                                T.op("act", lambda e: e.activation(out=dst[:, cc, :], in_=acc[:, k2, :], func=AF.Silu), reads=[ac], writes=["qcT"])
                            else:
                                T.op("act", lambda e: e.activation(out=acc[:, k2, :], in_=acc[:, k2, :], func=AF.Silu), reads=[ac], writes=[ac])
                                T.op("dve", lambda e: e.tensor_scalar(out=dst[:, cc, :], in0=acc[:, k2, :], scalar1=1.0 / 16.0, scalar2=None, op0=ALU.mult),
                                     reads=[ac], writes=["kcT"])
                chk(7)
                for which in ("vb", "ob"):
                    for i2 in range(2):
                        wv, wn = ws.get(l, SIDX[(which, i2)])
                        if which == "ob":
                            for c in range(4):
                                p, pn = fm_chunk(wv, wn, c, hT, "hT")
                                T.op("act", lambda e: e.activation(out=ogT[:, 4 * i2 + c, :], in_=p[:, :], func=AF.Sigmoid), reads=[pn], writes=["ogT"])
                            continue
                        for ch in range(8):
                            p, pn = nps()
                            for kc in range(8):
                                mm(p[0:64, :], hT[:, kc, ch * 64:(ch + 1) * 64], wv[:, kc, :], kc == 0, kc == 7, [wn, "hT"], [pn])
                            if which == "vb":
                                T.op("act", lambda e: e.activation(out=vaug[:, ch, 2 * i2:2 * i2 + 2, 0:256], in_=p[0:64, :].rearrange("p (h v) -> p h v", h=2),
                                     func=AF.Copy), reads=[pn], writes=["vaug"])
                            else:
                                T.op("act", lambda e: e.activation(out=og[:, ch, 512 * i2:512 * i2 + 512], in_=p[0:64, :], func=AF.Sigmoid), reads=[pn], writes=["og"])
                wv, wn = ws.get(l, SIDX[("gt", 0)])
                p, pn = nps()
                for ch in range(8):
                    for kc in range(8):
                        mm(p[0:64, ch * 8:(ch + 1) * 8], hT[:, kc, ch * 64:(ch + 1) * 64], wv[:, kc, :], (kc == 0 and ch == 0), kc == 7, [wn, "hT"], [pn])
                for ch in range(8):
                    T.op("dve", lambda e: e.tensor_tensor(out=graw[:, ch, :], in0=p[0:64, ch * 8:(ch + 1) * 8], in1=gbias[:, :], op=ALU.add),
                         reads=[pn, "gbias"], writes=["graw"])
                T.op("dve", lambda e: e.tensor_copy(out=igt[:], in_=graw[:, :, 0:4]), reads=["graw"], writes=["igt"])
                T.op("act", lambda e: e.activation(out=lft[:], in_=graw[:, :, 4:8], func=AF.Exp, scale=-1.0), reads=["graw"], writes=["lft"])
                T.op("act", lambda e: e.activation(out=lft[:], in_=lft[:], func=AF.Ln, bias=cst[0:64, C_ONE:C_ONE + 1]), reads=["lft", "cst"], writes=["lft"])
                T.op("dve", lambda e: e.tensor_scalar(out=lft[:], in0=lft[:], scalar1=-1.0, scalar2=None, op0=ALU.mult), reads=["lft"], writes=["lft"])
                chk(8)
                for c in range(4):
                    num, den = psum[4], psum[5]
                    T.dma("pool", KT3[:], k3d[c], reads=["k3d%d" % c], writes=["KT3"])
                    T.dma("pool", V3[:], v3d[c], reads=["v3d%d" % c], writes=["V3"])
                    first = {0: True, 64: True}
                    for hh in range(2):
                        h = 2 * c + hh
                        pb = 64 * hh
                        blocks = []
                        for b in range(4):
                            kb = []
                            if not (ti == 0 and b == 0):
                                kb.append((KT1[pb:pb + 64, c, b * 128:(b + 1) * 128], V1[:, b, h * 64:(h + 1) * 64], 0))
                            kb.append((KT1[pb:pb + 64, c, (b + 1) * 128:(b + 2) * 128], V1[:, b + 1, h * 64:(h + 1) * 64], 1))
                            blocks.append((0, QT[pb:pb + 64, c, b * 128:(b + 1) * 128], 128, kb, slice(b * 128, (b + 1) * 128), slice(0, 128)))
                        for r in range(4):
                            kb = []
                            if ti > 0:
                                kb.append((KT2[pb:pb + 64, c, r, 1 - par2, :], V2[:, r, 1 - par2, h * 64:(h + 1) * 64], 0))
                            kb.append((KT2[pb:pb + 64, c, r, par2, :], V2[:, r, par2, h * 64:(h + 1) * 64], 1))
                            blocks.append((1, QT[pb:pb + 64, 4 + c, r * 128:(r + 1) * 128], 128, kb, ("str", r, 4), slice(0, 128)))
                        for r in range(16):
                            kb = []
                            if ti // 4 > 0:
                                kb.append((KT3[pb:pb + 64, r, 1 - par3, :], V3[:, r, 1 - par3, hh * 64:(hh + 1) * 64], 0))
                            kb.append((KT3[pb:pb + 64, r, par3, :], V3[:, r, par3, hh * 64:(hh + 1) * 64], 1))
                            blocks.append((2, QT[pb:pb + 64, 8 + c, r * 32:(r + 1) * 32], 32, kb, ("str", r, 16), slice(32 * v3, 32 * v3 + 32)))
                        def emit_st(blk):
                            g_, qv_, nq_, kb_, _oc, _qs = blk
                            p_, pn_ = nps()
                            for (ktv_, vv_, kind_) in kb_:
                                mm(p_[:, kind_ * nq_:(kind_ + 1) * nq_], ktv_, qv_, True, True, ["KT%d" % (g_ + 1), "QT"], [pn_])
                            return p_, pn_

                        pend = emit_st(blocks[0])
                        for bi, (g, qv, nq, kb, ocol, qsl) in enumerate(blocks):
                            p, pn = pend
                            if bi + 1 < len(blocks):
                                pend = emit_st(blocks[bi + 1])
                            k0 = kb[0][2]
                            nk_ = len(kb)
                            pv_ = p[:, k0 * nq:(k0 + nk_) * nq].rearrange("p (k q) -> p k q", k=nk_)
                            tv = stmp[:, k0 * nq:(k0 + nk_) * nq].rearrange("p (k q) -> p k q", k=nk_)
                            sv = sm[:, k0 * nq:(k0 + nk_) * nq].rearrange("p (k q) -> p k q", k=nk_)
                            T.op("dve", lambda e: e.scalar_tensor_tensor(out=tv, in0=pv_, scalar=0.125, in1=btab[:, 8 * g + h, k0:k0 + nk_, qsl],
                                 op0=ALU.mult, op1=ALU.add), reads=[pn, "btab"], writes=["stmp"])
                            T.op("act", lambda e: e.activation(out=sv, in_=tv, func=AF.Exp), reads=["stmp"], writes=["sm"])
                            if isinstance(ocol, tuple):
                                _, r_, d_ = ocol
                                no = num[pb:pb + 64, :].rearrange("p (u r) -> p r u", r=d_)[:, r_, :]
                                do = den[pb:pb + 64, :].rearrange("p (u r) -> p r u", r=d_)[:, r_, :]
                            else:
                                no = num[pb:pb + 64, ocol]
                                do = den[pb:pb + 64, ocol]
                            for (ktv, vv, kind) in kb:
                                mm(no, vv, sm[:, kind * nq:(kind + 1) * nq], first[pb], False, ["V%d" % (g + 1), "sm"], ["psnum"], tile_position=(0, pb))
                                mm(do, onesb[:, 0:64], sm[:, kind * nq:(kind + 1) * nq], first[pb], False, ["cstb", "sm"], ["psden"], tile_position=(0, pb))
                                first[pb] = False
                    T.op("dve", lambda e: e.reciprocal(out=rden[:], in_=den[:, :]), reads=["psden"], writes=["rden"])
                    T.op("dve", lambda e: e.tensor_tensor(out=aT[:, c, :], in0=num[:, :], in1=rden[:], op=ALU.mult), reads=["psnum", "rden"], writes=["aT"])
                T.op("act", lambda e: e.activation(out=KT1[:, :, 0:128], in_=KT1[:, :, 512:640], func=AF.Copy), reads=["KT1"], writes=["KT1"])
                T.op("act", lambda e: e.activation(out=V1[:, 0, :], in_=V1[:, 4, :], func=AF.Copy), reads=["V1"], writes=["V1"])

                chk(9)
                T.alias(["QT"], ["boT"])
                tri = cst[0:64, C_TRI:C_TRI + 64]
                ones64 = cst[0:64, C_ONE:C_ONE + 64]
                sel63 = cst[0:64, C_SEL63:C_SEL63 + 128]
                cneg = cst[0:64, C_CNEG:C_CNEG + 256]
                id64 = cst[0:64, C_ID:C_ID + 64]
                for ch in range(8):
                    c0 = ch * 64
                    pb_, pbn = npb()
                    for j in range(8):
                        T.op("pe", lambda e: e.transpose(out=pb_[0:64, j * 128:(j + 1) * 128], in_=kcT[:, j, c0:c0 + 64], identity=identb),
                             reads=["kcT", "cstb"], writes=[pbn])
                    T.op("act", lambda e: e.activation(out=ktok[:], in_=pb_[0:64, :], func=AF.Copy), reads=[pbn], writes=["ktok"])
                    p, pn = nps()
                    mm(p[0:64, 0:4], tri, lft[:, ch, :], True, True, ["cst", "lft"], [pn])
                    T.op("dve", lambda e: e.tensor_copy(out=b_sb[:], in_=p[0:64, 0:4]), reads=[pn], writes=["b_sb"])
                    T.op("dve", lambda e: e.tensor_tensor(out=g_sb[:], in0=igt[:, ch, :], in1=b_sb[:], op=ALU.subtract), reads=["igt", "b_sb"], writes=["g_sb"])
                    for h in range(4):
                        T.op("dve", lambda e: e.tensor_scalar(out=dg[:, h * 64:(h + 1) * 64], in0=id64, scalar1=g_sb[:, h:h + 1], scalar2=None, op0=ALU.mult),
                             reads=["cst", "g_sb"], writes=["dg"])
                    p, pn = nps()
                    mm(p[0:64, 0:256], ones64, dg[:], True, True, ["cst", "dg"], [pn])
                    T.op("dve", lambda e: e.tensor_tensor(out=Lm[:], in0=p[0:64, 0:256], in1=cneg, op=ALU.add), reads=[pn, "cst"], writes=["Lm"])
                    mmx, iw_, mt_ = stat[:, 0:4], stat[:, 4:8], stat[:, 8:12]
                    T.op("dve", lambda e: e.tensor_reduce(out=small[:, 0:4], in_=Lm[:].rearrange("p (h s) -> p h s", h=4), axis=AX.X, op=ALU.max),
                         reads=["Lm"], writes=["small"])
                    T.op("dve", lambda e: e.tensor_tensor(out=mmx, in0=small[:, 0:4], in1=mprev[0:64, :], op=ALU.max), reads=["small", "mprev"], writes=["stat"])
                    T.op("dve", lambda e: e.tensor_scalar(out=small[:, 4:8], in0=mmx, scalar1=-1.0, scalar2=None, op0=ALU.mult), reads=["stat"], writes=["small"])
                    for h in range(4):
                        T.op("act", lambda e: e.activation(out=dw[:, h * 64:(h + 1) * 64], in_=Lm[:, h * 64:(h + 1) * 64], func=AF.Exp, bias=small[:, 4 + h:5 + h]),
                             reads=["Lm", "small"], writes=["Lm"])
                    T.op("dve", lambda e: e.tensor_tensor(out=small[:, 8:12], in0=mprev[0:64, :], in1=mmx, op=ALU.subtract), reads=["mprev", "stat"], writes=["small"])
                    T.op("act", lambda e: e.activation(out=iw_, in_=small[:, 8:12], func=AF.Exp), reads=["small"], writes=["stat"])
                    T.op("dve", lambda e: e.tensor_tensor(out=mt_, in0=b_sb[:], in1=mmx, op=ALU.add), reads=["b_sb", "stat"], writes=["stat"])
                    T.op("act", lambda e: e.activation(out=small[:, 12:16], in_=mt_, func=AF.Exp, scale=-1.0), reads=["stat"], writes=["small"])
                    p, pn = nps()
                    mm(p[:, 0:12], sel63, stat[:, 0:12], True, True, ["cst", "stat"], [pn])
                    T.op("dve", lambda e: e.tensor_copy(out=bc[:], in_=p[:, 0:12]), reads=[pn], writes=["bc"])
                    T.op("dve", lambda e: e.tensor_tensor(out=wc[:], in0=g_sb[:], in1=bc[0:64, 0:4], op=ALU.subtract), reads=["g_sb", "bc"], writes=["wc"])
                    T.op("act", lambda e: e.activation(out=wc[:], in_=wc[:], func=AF.Exp), reads=["wc"], writes=["wc"])
                    T.op("dve", lambda e: e.tensor_copy(out=mprev[:], in_=bc[:, 8:12]), reads=["bc"], writes=["mprev"])
                    p, pn = nps()
                    for h in range(4):
                        for jj in range(2):
                            j = 2 * h + jj
                            mm(p[0:64, h * 64:(h + 1) * 64], qcT[:, j, c0:c0 + 64], kcT[:, j, c0:c0 + 64], (h == 0 and jj == 0), jj == 1, ["qcT", "kcT"], [pn])
                    T.op("dve", lambda e: e.tensor_tensor(out=Pbf[:], in0=p[0:64, 0:256], in1=dw[:], op=ALU.mult), reads=[pn, "Lm"], writes=["Pbf"])
                    pb_, pbn = npb()
                    for h in range(4):
                        T.op("pe", lambda e: e.transpose(out=pb_[0:64, h * 64:(h + 1) * 64], in_=Pbf[:, h * 64:(h + 1) * 64], identity=identb[0:64, 0:64]),
                             reads=["Pbf", "cstb"], writes=[pbn])
                    T.op("act", lambda e: e.activation(out=PTs[:], in_=pb_[0:64, 0:256], func=AF.Copy), reads=[pbn], writes=["PTs"])
                    for h in range(4):
                        pI, pIn = nps()
                        for jj in range(2):
                            mm(pI[0:64, 0:257], qcT[:, 2 * h + jj, c0:c0 + 64], Cb[:, jj, h, 0:257], jj == 0, jj == 1, ["qcT", "Cb"], [pIn])
                        pA, pAn = nps()
                        mm(pA[0:64, 0:257], PTs[:, h * 64:(h + 1) * 64], vaug[:, ch, h, 0:257], True, True, ["PTs", "vaug"], [pAn])
                        T.op("act", lambda e: e.activation(out=intra[:], in_=pA[0:64, 0:257], func=AF.Copy), reads=[pAn], writes=["intra"])
                        T.op("dve", lambda e: e.scalar_tensor_tensor(out=nd[:], in0=pI[0:64, 0:257], scalar=stat[:, 4 + h:5 + h], in1=intra[:],
                             op0=ALU.mult, op1=ALU.add), reads=[pIn, "stat", "intra"], writes=["intra"])
                        T.op("act", lambda e: e.activation(out=nd[:, 256:257], in_=nd[:, 256:257], func=AF.Abs), reads=["intra"], writes=["intra"])
                        T.op("dve", lambda e: e.tensor_tensor(out=nd[:, 256:257], in0=nd[:, 256:257], in1=small[:, 12 + h:13 + h], op=ALU.max),
                             reads=["intra", "small"], writes=["intra"])
                        T.op("dve", lambda e: e.reciprocal(out=nd[:, 256:257], in_=nd[:, 256:257]), reads=["intra"], writes=["intra"])
                        T.op("dve", lambda e: e.tensor_scalar(out=bo[:, h * 256:(h + 1) * 256], in0=nd[:, 0:256], scalar1=nd[:, 256:257],
                             scalar2=None, op0=ALU.mult), reads=["intra"], writes=["bo"])
                    for h in range(4):
                        T.op("dve", lambda e: e.tensor_scalar(out=ktok[:, h * 256:(h + 1) * 256], in0=ktok[:, h * 256:(h + 1) * 256], scalar1=wc[:, h:h + 1],
                             scalar2=None, op0=ALU.mult), reads=["ktok", "wc"], writes=["ktok"])
                    for h in range(4):
                        for jj in range(2):
                            pC, pCn = nps()
                            mm(pC[:, 0:257], ktok[:, h * 256 + jj * 128:h * 256 + (jj + 1) * 128], vaug[:, ch, h, 0:257], True, True, ["ktok", "vaug"], [pCn])
                            T.op("dve", lambda e: e.scalar_tensor_tensor(out=Cf[:, jj, h, :], in0=Cf[:, jj, h, :], scalar=bc[:, 4 + h:5 + h], in1=pC[:, 0:257],
                                 op0=ALU.mult, op1=ALU.add), reads=["Cf", "bc", pCn], writes=["Cf"])
                            T.op("act", lambda e: e.activation(out=Cb[:, jj, h, 0:257], in_=Cf[:, jj, h, :], func=AF.Copy), reads=["Cf"], writes=["Cb"])
                    pb_, pbn = npb()
                    for j in range(8):
                        T.op("pe", lambda e: e.transpose(out=pb_[:, j * 64:(j + 1) * 64], in_=bo[:, j * 128:(j + 1) * 128], identity=identb[0:64, 0:64]),
                             reads=["bo", "cstb"], writes=[pbn])
                    T.op("dve", lambda e: e.tensor_tensor(out=boT[:, :, c0:c0 + 64], in0=pb_[:, 0:512].rearrange("p (j t) -> p j t", j=8),
                         in1=ogT[:, :, c0:c0 + 64], op=ALU.mult), reads=[pbn, "ogT"], writes=["boT"])
                if last:
                    for h in range(4):
                        for jj in range(2):
                            T.dma("pool", o_Cp[l, h, jj * 128:(jj + 1) * 128, :], Cf[:, jj, h, 0:256], reads=["Cf"], writes=["oC"])
                    T.dma("pool", o_np[l].rearrange("p (j h) -> p j h", j=2), Cf[:, :, :, 256], reads=["Cf"], writes=["on"], slow=True)
                    T.dma("pool", o_mp[l:l + 1, :], mprev[0:1, :], reads=["mprev"], writes=["om"])
                    T.dma("pool", o_mcp[l], carry[:].rearrange("p c r -> p (c r)"), reads=["carry"], writes=["omc"])

                chk(10)
                T.alias(["QT", "qcT", "kcT"], ["sgT", "mT"])
                for gi, which in enumerate(("ga", "gb")):
                    for i2 in range(2):
                        wv, wn = ws.get(l, SIDX[(which, i2)])
                        for c in range(4):
                            p, pn = fm_chunk(wv, wn, c, hT, "hT")
                            T.op("act", lambda e: e.activation(out=sgT[:, 8 * gi + 4 * i2 + c, :], in_=p[:, :], func=AF.Sigmoid), reads=[pn], writes=["sgT"])
                wv, wn = ws.get(l, SIDX[("pa", 0)])
                for cc in range(8):
                    p, pn = nps()
                    for kc in range(4):
                        mm(p[:, :], wv[:, kc, cc * 128:(cc + 1) * 128], aT[:, kc, :], kc == 0, kc == 3, [wn, "aT"], [pn])
                    T.op("dve", lambda e: e.tensor_tensor(out=mT[:, cc, :], in0=p[:, :], in1=sgT[:, cc, :], op=ALU.mult), reads=[pn, "sgT"], writes=["mT"])
                for i2 in range(2):
                    wv, wn = ws.get(l, SIDX[("pb", i2)])
                    for c in range(4):
                        cc = 4 * i2 + c
                        p, pn = fm_chunk(wv, wn, c, boT, "boT")
                        T.op("dve", lambda e: e.tensor_tensor(out=mgA[:], in0=p[:, :], in1=sgT[:, 8 + cc, :], op=ALU.mult), reads=[pn, "sgT"], writes=["sm"])
                        T.op("dve", lambda e: e.tensor_tensor(out=mT[:, cc, :], in0=mT[:, cc, :], in1=mgA[:], op=ALU.add), reads=["mT", "sm"], writes=["mT"])
                for i2 in range(2):
                    wv, wn = ws.get(l, SIDX[("wo", i2)])
                    for s in range(4):
                        p, pn = nps()
                        for kc in range(8):
                            mm(p[:, :], mT[:, kc, s * 128:(s + 1) * 128], wv[:, kc, :], kc == 0, kc == 7, [wn, "mT"], [pn])
                        T.op("dve", lambda e: e.tensor_tensor(out=xt[:, s, 512 * i2:512 * i2 + 512], in0=xt[:, s, 512 * i2:512 * i2 + 512], in1=p[:, :], op=ALU.add),
                             reads=[pn, "xt%d" % s], writes=["xt%d" % s])

                chk(11)
                norm_T(None)
                T.alias(["boT", "sgT", "QT", "qcT", "kcT"], ["actT"])
                for i2 in range(11):
                    wv, wn = ws.get(l, SIDX[("up", i2)])
                    for jc in range(2):
                        j = 2 * i2 + jc
                        for hf in range(2):
                            c = 2 * hf + jc
                            p, pn = fm_chunk(wv, wn, c, hT, "hT")
                            pr, fcn = "pre%d" % hf, "acc%d" % hf
                            T.op("dve", lambda e: e.tensor_copy(out=fpre[:, hf, 0:2], in_=fcar[:, hf, j, :]), reads=["fcar"], writes=[pr])
                            T.op("act", lambda e: e.activation(out=fpre[:, hf, 2:514], in_=p[:, :], func=AF.Copy), reads=[pn], writes=[pr])
                            T.op("dve", lambda e: e.tensor_copy(out=fcar[:, hf, j, :], in_=fpre[:, hf, 512:514]), reads=[pr], writes=["fcar"])
                            T.op("dve", lambda e: e.tensor_scalar(out=fc[:, hf, :], in0=fpre[:, hf, 0:512], scalar1=fw[:, hf, j, 0:1], scalar2=fb[:, hf, j:j + 1],
                                 op0=ALU.mult, op1=ALU.add), reads=[pr, "fw", "fb"], writes=[fcn])
                            for tp in range(1, 3):
                                T.op("dve", lambda e: e.scalar_tensor_tensor(out=fc[:, hf, :], in0=fpre[:, hf, tp:tp + 512], scalar=fw[:, hf, j, tp:tp + 1],
                                     in1=fc[:, hf, :], op0=ALU.mult, op1=ALU.add), reads=[pr, "fw", fcn], writes=[fcn])
                        T.op("act", lambda e: e.activation(out=ft[:], in_=fc[:, 0, :], func=AF.Square), reads=["acc0"], writes=["rden"])
                        T.op("dve", lambda e: e.tensor_scalar(out=ft[:], in0=ft[:], scalar1=0.044715, scalar2=1.0, op0=ALU.mult, op1=ALU.add), reads=["rden"], writes=["rden"])
                        T.op("dve", lambda e: e.tensor_tensor(out=ft[:], in0=ft[:], in1=fc[:, 0, :], op=ALU.mult), reads=["rden", "acc0"], writes=["rden"])
                        T.op("act", lambda e: e.activation(out=ft[:], in_=ft[:], func=AF.Sigmoid, scale=1.5957691216057308), reads=["rden"], writes=["rden"])
                        T.op("dve", lambda e: e.tensor_tensor(out=ft[:], in0=ft[:], in1=fc[:, 0, :], op=ALU.mult), reads=["rden", "acc0"], writes=["rden"])
                        T.op("dve", lambda e: e.tensor_tensor(out=actT[:, j, :], in0=ft[:], in1=fc[:, 1, :], op=ALU.mult), reads=["rden", "acc1"], writes=["actT"])
                if last:
                    T.dma("pool", o_fcp[l], fcar[:].rearrange("p h j r -> p (h j r)"), reads=["fcar"], writes=["ofc"])
                for i2 in range(8):
                    wv, wn = ws.get(l, SIDX[("dn", i2)])
                    for s in range(4):
                        p, pn = nps()
                        for kc in range(22):
                            mm(p[:, 0:128], actT[:, kc, s * 128:(s + 1) * 128], wv[:, kc, :], kc == 0, kc == 21, [wn, "actT"], [pn])
                        T.op("dve", lambda e: e.tensor_tensor(out=xt[:, s, 128 * i2:128 * i2 + 128], in0=xt[:, s, 128 * i2:128 * i2 + 128], in1=p[:, 0:128], op=ALU.add),
                             reads=[pn, "xt%d" % s], writes=["xt%d" % s])
                chk(12)
                if l == 0:
                    for s in range(4):
                        T.dma("pool", xscr[t0 + s * 128:t0 + (s + 1) * 128, :], xt[:, s, :], reads=["xt%d" % s], writes=["xscr"])
                else:
                    T.op("dve", lambda e: e.memset(ss[:], 0.0), writes=["ss"])
                    for s in range(4):
                        T.op("act", lambda e: e.activation(out=junk[:], in_=xt[:, s, :], func=AF.Square, accum_out=ss[:, s:s + 1]),
                             reads=["xt%d" % s, "ss"], writes=["hb", "ss"])
                    T.op("act", lambda e: e.activation(out=rs[:], in_=ss[:], func=AF.Ln, scale=1.0 / D, bias=cst[:, C_EPS:C_EPS + 1]), reads=["ss", "cst"], writes=["rs"])
                    T.op("act", lambda e: e.activation(out=rs[:], in_=rs[:], func=AF.Exp, scale=-0.5), reads=["rs"], writes=["rs"])
                    for s in range(4):
                        if s == 0:
                            T.dma("pool", pre[:, :, 0:512], fin_g[0:1, :].broadcast_to([128, D]).rearrange("p (a b) -> p a b", a=2), writes=["pre0", "pre1"])
                        T.op("dve", lambda e: e.scalar_tensor_tensor(out=acc[:], in0=xt[:, s, :].rearrange("p (a b) -> p a b", a=2), scalar=rs[:, s:s + 1],
                             in1=pre[:, :, 0:512], op0=ALU.mult, op1=ALU.mult), reads=["xt%d" % s, "rs", "pre0", "pre1"], writes=["acc0", "acc1"])
                        T.dma("pool", o_yp[t0 + s * 128:t0 + (s + 1) * 128, :].rearrange("p (a b) -> p a b", a=2), acc[:], reads=["acc0", "acc1"], writes=["oy"])
        T.barrier()
    except StopBuild:
        T.barrier()
        return nc


    if do_sample:
      try:
        ws.extend([(l, si) for l in range(DEPTH) for si in range(NSLAB)])
        SIDX = {}
        for si, sdef in enumerate(SLABS):
            SIDX[(sdef[0], sdef[1])] = si

        def mm(out, lhsT, rhs, start, stop, reads, writes, **kw):
            T.op("pe", lambda e: e.matmul(out, lhsT, rhs, start=start, stop=stop, skip_group_check=True, **kw), reads=reads, writes=writes)

        sctr = [0]

        def nps2():
            i = 4 + (sctr[0] % 2)
            sctr[0] += 1
            return psum[i], "ps%d" % i

        with contextlib.ExitStack() as esx:
            xs_t = esx.enter_context(nc.sbuf_tensor("xs_t", [4, D], F32))
            cst2 = esx.enter_context(nc.sbuf_tensor("cst2", [128, 528], F32))
            s_ss = esx.enter_context(nc.sbuf_tensor("s_ss", [4, 2], F32))
            s_hb = esx.enter_context(nc.sbuf_tensor("s_hb", [4, D], BF16))
            s_hT = esx.enter_context(nc.sbuf_tensor("s_hT", [128, 8, 4], BF16))
            s_mT = esx.enter_context(nc.sbuf_tensor("s_mT", [128, 22, 4], BF16))
            s_jk = esx.enter_context(nc.sbuf_tensor("s_jk", [4, D], F32))
            T.dma("pool", xs_t[:], xs[:, :], writes=["xs_t"])
            T.dma("pool", cst2[:], consts2[:, :], writes=["cst2"])
            selrow = lambda b: cst2[0:4, 128 * b:128 * (b + 1)]
            selcol = lambda b: cst2[:, 512 + 4 * b:512 + 4 * (b + 1)]
            oh4 = cst2[0:4, 512:516]
            identf4 = cst[0:4, C_ID:C_ID + 4]

            def s_norm(dstT, nm):
                T.op("dve", lambda e: e.memset(s_ss[:], 0.0), writes=["s_ss"])
                T.op("act", lambda e: e.activation(out=s_jk[:], in_=xs_t[:], func=AF.Square, accum_out=s_ss[:, 0:1]), reads=["xs_t", "s_ss"], writes=["s_jk", "s_ss"])
                T.op("act", lambda e: e.activation(out=s_ss[:, 1:2], in_=s_ss[:, 0:1], func=AF.Ln, scale=1.0 / D, bias=cst[0:4, C_EPS:C_EPS + 1]), reads=["s_ss", "cst"], writes=["s_ss"])
                T.op("act", lambda e: e.activation(out=s_ss[:, 1:2], in_=s_ss[:, 1:2], func=AF.Exp, scale=-0.5), reads=["s_ss"], writes=["s_ss"])
                T.op("dve", lambda e: e.tensor_scalar(out=s_hb[:], in0=xs_t[:], scalar1=s_ss[:, 1:2], scalar2=None, op0=ALU.mult), reads=["xs_t", "s_ss"], writes=["s_hb"])
                tr_bf(s_hb, "s_hb", 8, dstT, nm)

            def tr_bf(src, sname, nchunk, dstT, dname):
                pb_, pbn = npb()
                for kc in range(nchunk):
                    T.op("pe", lambda e: e.transpose(out=pb_[:, kc * 4:(kc + 1) * 4], in_=src[0:4, kc * 128:(kc + 1) * 128], identity=identb[0:4, 0:4]),
                         reads=[sname, "cstb"], writes=[pbn])
                T.op("act", lambda e: e.activation(out=dstT[:, 0:nchunk, :], in_=pb_[:, 0:nchunk * 4].rearrange("p (k t) -> p k t", k=nchunk), func=AF.Copy),
                     reads=[pbn], writes=[dname])

            def proj4(wv, wn, srcT, sname, nk, ncols, c0=0):
                p, pn = nps2()
                for kc in range(nk):
                    mm(p[0:4, 0:ncols], srcT[:, kc, 0:4], wv[:, kc, c0:c0 + ncols], kc == 0, kc == nk - 1, [wn, sname], [pn])
                return p, pn

            for l in range(DEPTH):
                with contextlib.ExitStack() as esm:
                    def sbm(name, shape, dt=F32):
                        return esm.enter_context(nc.sbuf_tensor(name + "_L%d" % l, list(shape), dt))
                    qa4 = sbm("qa4", [4, 1536]); ka4 = sbm("ka4", [4, 1536]); va4 = sbm("va4", [4, 1536])
                    qkpre4 = sbm("qkpre4", [4, 2048]); qk4 = sbm("qk4", [4, 2048]); tmp4 = sbm("tmp4", [4, 2048])
                    vaug4 = sbm("vaug4", [4, 4, 257]); og4 = sbm("og4", [4, D]); sg4 = sbm("sg4", [4, 2048])
                    gt4 = sbm("gt4", [4, 8]); gb4 = sbm("gb4", [4, 8])
                    bs_t = sbm("bs_t", [128, 24]); b0_t = sbm("b0_t", [4, 24])
                    p0 = sbm("p0", [4, 24]); sm4 = sbm("sm4", [4, 64])

                    T.dma("pool", gb4[:], mgate_b[l:l + 1, :].broadcast_to([4, 8]), writes=["gb4"])
                    T.dma("pool", bs_t[:], bias_s[:, :], writes=["bs_t"])
                    T.dma("pool", b0_t[:], bias_0[:, :], writes=["b0_t"])
                    T.dma("pool", sm4[:, 0:4], st_m[l], writes=["sm4"])
                    T.op("dve", lambda e: e.memset(vaug4[:], 1.0), writes=["vaug4"])

                    s_norm(s_hT, "s_hT")
                    for kind, dst, dn_ in (("qa", qa4, "qa4"), ("ka", ka4, "ka4"), ("va", va4, "va4")):
                        for i in range(3):
                            wv, wn = ws.get(l, SIDX[(kind, i)])
                            p, pn = proj4(wv, wn, s_hT, "s_hT", 8, 512)
                            T.op("act", lambda e: e.activation(out=dst[:, 512 * i:512 * (i + 1)], in_=p[0:4, 0:512], func=AF.Copy), reads=[pn], writes=[dn_])
                    for kind, off in (("qb", 0), ("kb", 1024)):
                        for i in range(2):
                            wv, wn = ws.get(l, SIDX[(kind, i)])
                            p, pn = proj4(wv, wn, s_hT, "s_hT", 8, 512)
                            T.op("act", lambda e: e.activation(out=qkpre4[:, off + 512 * i:off + 512 * (i + 1)], in_=p[0:4, 0:512], func=AF.Copy), reads=[pn], writes=["qkpre4"])
                    for i in range(2):
                        wv, wn = ws.get(l, SIDX[("vb", i)])
                        p, pn = proj4(wv, wn, s_hT, "s_hT", 8, 512)
                        T.op("act", lambda e: e.activation(out=vaug4[:, 2 * i:2 * i + 2, 0:256], in_=p[0:4, 0:512].rearrange("p (h v) -> p h v", h=2), func=AF.Copy),
                             reads=[pn], writes=["vaug4"])
                    for i in range(2):
                        wv, wn = ws.get(l, SIDX[("ob", i)])
                        p, pn = proj4(wv, wn, s_hT, "s_hT", 8, 512)
                        T.op("act", lambda e: e.activation(out=og4[:, 512 * i:512 * (i + 1)], in_=p[0:4, 0:512], func=AF.Sigmoid), reads=[pn], writes=["og4"])
                    wv, wn = ws.get(l, SIDX[("gt", 0)])
                    p, pn = proj4(wv, wn, s_hT, "s_hT", 8, 8)
                    T.op("dve", lambda e: e.tensor_tensor(out=gt4[:], in0=p[0:4, 0:8], in1=gb4[:], op=ALU.add), reads=[pn, "gb4"], writes=["gt4"])
                    for gi, kind in enumerate(("ga", "gb")):
                        for i in range(2):
                            wv, wn = ws.get(l, SIDX[(kind, i)])
                            p, pn = proj4(wv, wn, s_hT, "s_hT", 8, 512)
                            T.op("act", lambda e: e.activation(out=sg4[:, 1024 * gi + 512 * i:1024 * gi + 512 * (i + 1)], in_=p[0:4, 0:512], func=AF.Sigmoid),
                                 reads=[pn], writes=["sg4"])
                    chk(20)
                    for g in range(3):
                        T.dma("pool", o_kvs[g][l, :, 0:512], ka4[:, 512 * g:512 * (g + 1)], reads=["ka4"], writes=["okvs"])
                        T.dma("pool", o_kvs[g][l, :, 512:1024], va4[:, 512 * g:512 * (g + 1)], reads=["va4"], writes=["okvs"])

                    chk(21)
                    with contextlib.ExitStack() as esc:
                        stm = esc.enter_context(nc.sbuf_tensor("stm_L%d" % l, [4, 3, 2048], F32))
                        cw4 = esc.enter_context(nc.sbuf_tensor("cw4_L%d" % l, [4, 4, 2048], F32))
                        cb4 = esc.enter_context(nc.sbuf_tensor("cb4_L%d" % l, [4, 2048], F32))
                        T.dma("pool", stm[:], st_mconv[l], writes=["stm"])
                        T.dma("pool", cw4[:], mconv_w[l:l + 1].broadcast_to([4, 4, 2048]), writes=["cw4"])
                        T.dma("pool", cb4[:], mconv_b[l:l + 1, :].broadcast_to([4, 2048]), writes=["cb4"])
                        T.op("dve", lambda e: e.tensor_tensor(out=qk4[:], in0=qkpre4[:], in1=cw4[:, 3, :], op=ALU.mult), reads=["qkpre4", "cw4"], writes=["qk4"])
                        T.op("dve", lambda e: e.tensor_tensor(out=qk4[:], in0=qk4[:], in1=cb4[:], op=ALU.add), reads=["qk4", "cb4"], writes=["qk4"])
                        for i in range(3):
                            T.op("dve", lambda e: e.tensor_tensor(out=tmp4[:], in0=stm[:, i, :], in1=cw4[:, i, :], op=ALU.mult), reads=["stm", "cw4"], writes=["tmp4"])
                            T.op("dve", lambda e: e.tensor_tensor(out=qk4[:], in0=qk4[:], in1=tmp4[:], op=ALU.add), reads=["qk4", "tmp4"], writes=["qk4"])
                        T.op("act", lambda e: e.activation(out=qk4[:], in_=qk4[:], func=AF.Silu), reads=["qk4"], writes=["qk4"])
                        T.op("dve", lambda e: e.tensor_scalar(out=qk4[:, 1024:2048], in0=qk4[:, 1024:2048], scalar1=1.0 / 16.0, scalar2=None, op0=ALU.mult), reads=["qk4"], writes=["qk4"])

                        T.barrier()
                    ckv = [sbm("ckv%d" % i, [128, 1024]) for i in range(2)]
                    prod = sbm("prod", [128, 512]); pv = sbm("pv", [128, 512])
                    sT = sbm("sT", [128, 8]); pT = sbm("pT", [128, 8])
                    nv = sbm("nv", [4, 1536]); numt = sbm("numt", [4, 512]); dent = sbm("dent", [4, 8])
                    a4b = sbm("a4b", [4, 512], BF16); aT4 = sbm("aT4", [128, 4, 4], BF16)
                    qT4f = sbm("qT4f", [128, 8, 4]); qTm = sbm("qTm", [128, 4, 8, 4])
                    Ct = [sbm("Ct%d" % i, [128, 2, 257]) for i in range(2)]
                    Cn = [sbm("Cn%d" % i, [128, 2, 257]) for i in range(2)]
                    vm = sbm("vm", [4, 4, 4, 257]); ibc = sbm("ibc", [128, 4, 4])
                    hq = sbm("hq", [4, 4, 257]); bo4b = sbm("bo4b", [4, D], BF16); boT4 = sbm("boT4", [128, 8, 4], BF16)
                    mg4 = sbm("mg4", [4, D]); mg4b = sbm("mg4b", [4, D], BF16)
                    T.op("dve", lambda e: e.memset(qTm[:], 0.0), writes=["qTm"])
                    chk(22)
                    psO, psD = psum[0], psum[1]
                    firstO = True
                    it = 0
                    for b in range(4):
                        for g in range(3):
                            ck = ckv[it % 2]
                            ckn = "ckv%d" % (it % 2)
                            it += 1
                            T.dma("pool", ck[:], cache[g][l, b], writes=[ckn])
                            pq, pqn = nps2()
                            mm(pq[:, 0:512], selrow(b), qa4[:, 512 * g:512 * (g + 1)], True, True, ["cst2", "qa4"], [pqn])
                            T.op("dve", lambda e: e.tensor_tensor(out=prod[:], in0=ck[:, 0:512], in1=pq[:, 0:512], op=ALU.mult), reads=[ckn, pqn], writes=["prod"])
                            T.op("dve", lambda e: e.tensor_reduce(out=sT[:], in_=prod[:].rearrange("p (h c) -> p h c", h=8), axis=AX.X, op=ALU.add),
                                 reads=["prod"], writes=["sT"])
                            T.op("dve", lambda e: e.scalar_tensor_tensor(out=sT[:], in0=sT[:], scalar=0.125, in1=bs_t[:, 8 * g:8 * (g + 1)], op0=ALU.mult, op1=ALU.add),
                                 reads=["sT", "bs_t"], writes=["sT"])
                            T.op("act", lambda e: e.activation(out=pT[:], in_=sT[:], func=AF.Exp), reads=["sT"], writes=["pT"])
                            for h in range(8):
                                T.op("dve", lambda e: e.tensor_scalar(out=pv[:, 64 * h:64 * (h + 1)], in0=ck[:, 512 + 64 * h:512 + 64 * (h + 1)], scalar1=pT[:, h:h + 1],
                                     scalar2=None, op0=ALU.mult), reads=[ckn, "pT"], writes=["pv"])
                            mm(psO[0:4, 0:512], selcol(b), pv[:], firstO, False, ["cst2", "pv"], ["ps0"])
                            mm(psD[0:4, 0:8], selcol(b), pT[:], firstO, False, ["cst2", "pT"], ["ps1"])
                            firstO = False
                    T.op("dve", lambda e: e.tensor_tensor(out=nv[:], in0=qa4[:], in1=ka4[:], op=ALU.mult), reads=["qa4", "ka4"], writes=["nv"])
                    T.op("dve", lambda e: e.tensor_reduce(out=p0[:], in_=nv[:].rearrange("p (h c) -> p h c", h=24), axis=AX.X, op=ALU.add), reads=["nv"], writes=["p0"])
                    T.op("dve", lambda e: e.scalar_tensor_tensor(out=p0[:], in0=p0[:], scalar=0.125, in1=b0_t[:], op0=ALU.mult, op1=ALU.add), reads=["p0", "b0_t"], writes=["p0"])
                    T.op("act", lambda e: e.activation(out=p0[:], in_=p0[:], func=AF.Exp), reads=["p0"], writes=["p0"])
                    for hh in range(24):
                        T.op("dve", lambda e: e.tensor_scalar(out=nv[:, 64 * hh:64 * (hh + 1)], in0=va4[:, 64 * hh:64 * (hh + 1)], scalar1=p0[:, hh:hh + 1], scalar2=None,
                             op0=ALU.mult), reads=["va4", "p0", "nv"], writes=["nv"])
                    T.op("dve", lambda e: e.tensor_tensor(out=numt[:], in0=nv[:, 0:512], in1=psO[0:4, 0:512], op=ALU.add), reads=["nv", "ps0"], writes=["numt"])
                    T.op("dve", lambda e: e.tensor_tensor(out=dent[:], in0=p0[:, 0:8], in1=psD[0:4, 0:8], op=ALU.add), reads=["p0", "ps1"], writes=["dent"])
                    for g in (1, 2):
                        T.op("dve", lambda e: e.tensor_tensor(out=numt[:], in0=numt[:], in1=nv[:, 512 * g:512 * (g + 1)], op=ALU.add), reads=["numt", "nv"], writes=["numt"])
                        T.op("dve", lambda e: e.tensor_tensor(out=dent[:], in0=dent[:], in1=p0[:, 8 * g:8 * (g + 1)], op=ALU.add), reads=["dent", "p0"], writes=["dent"])
                    T.op("dve", lambda e: e.reciprocal(out=dent[:], in_=dent[:]), reads=["dent"], writes=["dent"])
                    for h in range(8):
                        T.op("dve", lambda e: e.tensor_scalar(out=a4b[:, 64 * h:64 * (h + 1)], in0=numt[:, 64 * h:64 * (h + 1)], scalar1=dent[:, h:h + 1], scalar2=None,
                             op0=ALU.mult), reads=["numt", "dent"], writes=["a4b"])
                    tr_bf(a4b, "a4b", 4, aT4, "aT4")

                    chk(23)
                    T.dma("pool", o_mcs[l, :, 0:2, :], st_mconv[l, :, 1:3, :], writes=["omcs"])
                    T.dma("pool", o_mcs[l, :, 2, :], qkpre4[:], reads=["qkpre4"], writes=["omcs"])
                    m0, ig, bm, mt, dwv, iwv, emm, qkw, denv, tq = (sm4[:, 4 * i:4 * i + 4] for i in range(10))
                    T.op("dve", lambda e: e.tensor_copy(out=ig, in_=gt4[:, 0:4]), reads=["gt4"], writes=["sm4"])
                    T.op("act", lambda e: e.activation(out=bm, in_=gt4[:, 4:8], func=AF.Exp, scale=-1.0), reads=["gt4"], writes=["sm4"])
                    T.op("act", lambda e: e.activation(out=bm, in_=bm, func=AF.Ln, bias=cst[0:4, C_ONE:C_ONE + 1]), reads=["sm4", "cst"], writes=["sm4"])
                    T.op("dve", lambda e: e.scalar_tensor_tensor(out=bm, in0=bm, scalar=-1.0, in1=m0, op0=ALU.mult, op1=ALU.add), reads=["sm4"], writes=["sm4"])
                    T.op("dve", lambda e: e.tensor_tensor(out=mt, in0=bm, in1=ig, op=ALU.max), reads=["sm4"], writes=["sm4"])
                    T.op("dve", lambda e: e.tensor_tensor(out=dwv, in0=ig, in1=mt, op=ALU.subtract), reads=["sm4"], writes=["sm4"])
                    T.op("act", lambda e: e.activation(out=dwv, in_=dwv, func=AF.Exp), reads=["sm4"], writes=["sm4"])
                    T.op("dve", lambda e: e.tensor_tensor(out=iwv, in0=bm, in1=mt, op=ALU.subtract), reads=["sm4"], writes=["sm4"])
                    T.op("act", lambda e: e.activation(out=iwv, in_=iwv, func=AF.Exp), reads=["sm4"], writes=["sm4"])
                    T.op("act", lambda e: e.activation(out=emm, in_=mt, func=AF.Exp, scale=-1.0), reads=["sm4"], writes=["sm4"])
                    T.dma("pool", o_ms[l], mt, reads=["sm4"], writes=["oms"])
                    T.op("dve", lambda e: e.tensor_tensor(out=tmp4[:, 0:1024], in0=qk4[:, 0:1024], in1=qk4[:, 1024:2048], op=ALU.mult), reads=["qk4"], writes=["tmp4"])
                    T.op("dve", lambda e: e.tensor_reduce(out=qkw, in_=tmp4[:, 0:1024].rearrange("p (h c) -> p h c", h=4), axis=AX.X, op=ALU.add), reads=["tmp4"], writes=["sm4"])
                    T.op("dve", lambda e: e.tensor_tensor(out=qkw, in0=qkw, in1=dwv, op=ALU.mult), reads=["sm4"], writes=["sm4"])
                    pq, pqn = nps2()
                    for kc in range(8):
                        T.op("pe", lambda e: e.transpose(out=pq[:, kc * 4:(kc + 1) * 4], in_=qk4[0:4, kc * 128:(kc + 1) * 128], identity=identf4), reads=["qk4", "cst"], writes=[pqn])
                    T.op("act", lambda e: e.activation(out=qT4f[:], in_=pq[:, 0:32].rearrange("p (k t) -> p k t", k=8), func=AF.Copy), reads=[pqn], writes=["qT4f"])
                    for b in range(4):
                        T.op("dve", lambda e: e.tensor_copy(out=qTm[:, b, :, b], in_=qT4f[:, :, b]), reads=["qT4f", "qTm"], writes=["qTm"])
                    for b in range(4):
                        for h in range(4):
                            T.op("dve", lambda e: e.tensor_scalar(out=vm[:, b, h, :], in0=vaug4[:, h, :], scalar1=dwv[:, h:h + 1], scalar2=cst[0:4, C_ID + b:C_ID + b + 1],
                                 op0=ALU.mult, op1=ALU.mult), reads=["vaug4", "sm4", "cst", "vm"], writes=["vm"])
                    it = 0
                    for b in range(4):
                        pi, pin = nps2()
                        mm(pi[:, 0:4], selrow(b), iwv, True, True, ["cst2", "sm4"], [pin])
                        T.op("dve", lambda e: e.tensor_copy(out=ibc[:, b, :], in_=pi[:, 0:4]), reads=[pin], writes=["ibc%d" % b])
                        for h in range(4):
                            k2 = it % 2
                            it += 1
                            ctn, cnn = "Ct%d" % k2, "Cn%d" % k2
                            T.dma("pool", Ct[k2][:, :, 0:256], st_C[l, b, h].rearrange("(j p) v -> p j v", p=128), writes=[ctn])
                            T.dma("pool", Ct[k2][:, :, 256], st_n[l, b, h].rearrange("(j p) -> p j", p=128), writes=[ctn], slow=True)
                            for jj in range(2):
                                mm(psum[h][0:4, 0:257], qTm[:, b, 2 * h + jj, :], Ct[k2][:, jj, :], (b == 0 and jj == 0), False, ["qTm", ctn], ["ps%d" % h])
                            for jj in range(2):
                                pk, pkn = nps2()
                                mm(pk[:, 0:257], qk4[0:4, 1024 + h * 256 + jj * 128:1024 + h * 256 + (jj + 1) * 128], vm[:, b, h, :], True, True, ["qk4", "vm"], [pkn])
                                T.op("dve", lambda e: e.scalar_tensor_tensor(out=Cn[k2][:, jj, :], in0=Ct[k2][:, jj, :], scalar=ibc[:, b, h:h + 1], in1=pk[:, 0:257],
                                     op0=ALU.mult, op1=ALU.add), reads=[ctn, "ibc%d" % b, pkn], writes=[cnn])
                            T.dma("pool", o_Cs[l, b, h].rearrange("(j p) v -> p j v", p=128), Cn[k2][:, :, 0:256], reads=[cnn], writes=["oCs"])
                            T.dma("pool", o_ns[l, b, h].rearrange("(j p) -> p j", p=128), Cn[k2][:, :, 256], reads=[cnn], writes=["ons"], slow=True)
                    for h in range(4):
                        T.op("act", lambda e: e.activation(out=hq[:, h, :], in_=psum[h][0:4, 0:257], func=AF.Copy), reads=["ps%d" % h], writes=["hq"])
                    for h in range(4):
                        T.op("dve", lambda e: e.tensor_scalar(out=hq[:, h, :], in0=hq[:, h, :], scalar1=iwv[:, h:h + 1], scalar2=None, op0=ALU.mult), reads=["hq", "sm4"], writes=["hq"])
                        T.op("dve", lambda e: e.scalar_tensor_tensor(out=hq[:, h, :], in0=vaug4[:, h, :], scalar=qkw[:, h:h + 1], in1=hq[:, h, :], op0=ALU.mult, op1=ALU.add),
                             reads=["vaug4", "sm4", "hq"], writes=["hq"])
                        T.op("act", lambda e: e.activation(out=denv[:, h:h + 1], in_=hq[:, h, 256:257], func=AF.Abs), reads=["hq"], writes=["sm4"])
                    T.op("dve", lambda e: e.tensor_tensor(out=denv, in0=denv, in1=emm, op=ALU.max), reads=["sm4"], writes=["sm4"])
                    T.op("dve", lambda e: e.reciprocal(out=denv, in_=denv), reads=["sm4"], writes=["sm4"])
                    for h in range(4):
                        T.op("dve", lambda e: e.scalar_tensor_tensor(out=bo4b[:, 256 * h:256 * (h + 1)], in0=hq[:, h, 0:256], scalar=denv[:, h:h + 1], in1=og4[:, 256 * h:256 * (h + 1)],
                             op0=ALU.mult, op1=ALU.mult), reads=["hq", "sm4", "og4"], writes=["bo4b"])
                    tr_bf(bo4b, "bo4b", 8, boT4, "boT4")

                    chk(24)
                    wv, wn = ws.get(l, SIDX[("pa", 0)])
                    for i in range(2):
                        p, pn = proj4(wv, wn, aT4, "aT4", 4, 512, c0=512 * i)
                        T.op("dve", lambda e: e.tensor_tensor(out=mg4[:, 512 * i:512 * (i + 1)], in0=p[0:4, 0:512], in1=sg4[:, 512 * i:512 * (i + 1)], op=ALU.mult),
                             reads=[pn, "sg4"], writes=["mg4"])
                    for i in range(2):
                        wv, wn = ws.get(l, SIDX[("pb", i)])
                        p, pn = proj4(wv, wn, boT4, "boT4", 8, 512)
                        T.op("dve", lambda e: e.tensor_tensor(out=tmp4[:, 0:512], in0=p[0:4, 0:512], in1=sg4[:, 1024 + 512 * i:1024 + 512 * (i + 1)], op=ALU.mult),
                             reads=[pn, "sg4"], writes=["tmp4"])
                        T.op("dve", lambda e: e.tensor_tensor(out=mg4[:, 512 * i:512 * (i + 1)], in0=mg4[:, 512 * i:512 * (i + 1)], in1=tmp4[:, 0:512], op=ALU.add),
                             reads=["mg4", "tmp4"], writes=["mg4"])
                    T.op("dve", lambda e: e.tensor_copy(out=mg4b[:], in_=mg4[:]), reads=["mg4"], writes=["mg4b"])
                    tr_bf(mg4b, "mg4b", 8, s_mT, "s_mT")
                    for i in range(2):
                        wv, wn = ws.get(l, SIDX[("wo", i)])
                        p, pn = proj4(wv, wn, s_mT, "s_mT", 8, 512)
                        T.op("dve", lambda e: e.tensor_tensor(out=xs_t[:, 512 * i:512 * (i + 1)], in0=xs_t[:, 512 * i:512 * (i + 1)], in1=p[0:4, 0:512], op=ALU.add),
                             reads=[pn, "xs_t"], writes=["xs_t"])
                    T.barrier()
                chk(25)
                with contextlib.ExitStack() as esf:
                    def sbf(name, shape, dt=F32):
                        return esf.enter_context(nc.sbuf_tensor(name + "_L%d" % l, list(shape), dt))
                    u4 = sbf("u4", [4, 2 * DFF]); cv4 = sbf("cv4", [4, 2 * DFF]); tf4 = sbf("tf4", [4, DFF])
                    sfc = sbf("sfc", [4, 2, DFF]); fw4 = sbf("fw4", [4, 3, DFF]); fb4 = sbf("fb4", [4, DFF])
                    ac4b = sbf("ac4b", [4, DFF], BF16)
                    s_norm(s_hT, "s_hT")
                    for i in range(11):
                        wv, wn = ws.get(l, SIDX[("up", i)])
                        p, pn = proj4(wv, wn, s_hT, "s_hT", 8, 512)
                        T.op("act", lambda e: e.activation(out=u4[:, 256 * i:256 * (i + 1)], in_=p[0:4, 0:256], func=AF.Copy), reads=[pn], writes=["u4"])
                        T.op("act", lambda e: e.activation(out=u4[:, DFF + 256 * i:DFF + 256 * (i + 1)], in_=p[0:4, 256:512], func=AF.Copy), reads=[pn], writes=["u4"])
                    T.dma("pool", o_fcs[l, :, 0, :], st_fconv[l, :, 1, :], writes=["ofcs"])
                    T.dma("pool", o_fcs[l, :, 1, :], u4[:], reads=["u4"], writes=["ofcs"])
                    for hh in range(2):
                        hs = slice(hh * DFF, (hh + 1) * DFF)
                        T.dma("pool", sfc[:], st_fconv[l, :, :, hs], writes=["sfc"])
                        T.dma("pool", fw4[:], fconv_w[l:l + 1, :, hs].broadcast_to([4, 3, DFF]), writes=["fw4"])
                        T.dma("pool", fb4[:], fconv_b[l:l + 1, hs].broadcast_to([4, DFF]), writes=["fb4"])
                        T.op("dve", lambda e: e.tensor_tensor(out=cv4[:, hs], in0=u4[:, hs], in1=fw4[:, 2, :], op=ALU.mult), reads=["u4", "fw4"], writes=["cv4"])
                        T.op("dve", lambda e: e.tensor_tensor(out=cv4[:, hs], in0=cv4[:, hs], in1=fb4[:], op=ALU.add), reads=["cv4", "fb4"], writes=["cv4"])
                        for i in range(2):
                            T.op("dve", lambda e: e.tensor_tensor(out=tf4[:], in0=sfc[:, i, :], in1=fw4[:, i, :], op=ALU.mult), reads=["sfc", "fw4"], writes=["tf4"])
                            T.op("dve", lambda e: e.tensor_tensor(out=cv4[:, hs], in0=cv4[:, hs], in1=tf4[:], op=ALU.add), reads=["cv4", "tf4"], writes=["cv4"])
                    c1, c2, tt_ = cv4[:, 0:DFF], cv4[:, DFF:2 * DFF], tf4[:, 0:DFF]
                    T.op("act", lambda e: e.activation(out=tt_, in_=c1, func=AF.Square), reads=["cv4"], writes=["tf4"])
                    T.op("dve", lambda e: e.tensor_scalar(out=tt_, in0=tt_, scalar1=0.044715, scalar2=1.0, op0=ALU.mult, op1=ALU.add), reads=["tf4"], writes=["tf4"])
                    T.op("dve", lambda e: e.tensor_tensor(out=tt_, in0=tt_, in1=c1, op=ALU.mult), reads=["tf4", "cv4"], writes=["tf4"])
                    T.op("act", lambda e: e.activation(out=tt_, in_=tt_, func=AF.Sigmoid, scale=1.5957691216057308), reads=["tf4"], writes=["tf4"])
                    T.op("dve", lambda e: e.tensor_tensor(out=tt_, in0=tt_, in1=c1, op=ALU.mult), reads=["tf4", "cv4"], writes=["tf4"])
                    T.op("dve", lambda e: e.tensor_tensor(out=ac4b[:], in0=tt_, in1=c2, op=ALU.mult), reads=["tf4", "cv4"], writes=["ac4b"])
                    tr_bf(ac4b, "ac4b", 22, s_mT, "s_mT")
                    for i in range(8):
                        wv, wn = ws.get(l, SIDX[("dn", i)])
                        p, pn = proj4(wv, wn, s_mT, "s_mT", 22, 128)
                        T.op("dve", lambda e: e.tensor_tensor(out=xs_t[:, 128 * i:128 * (i + 1)], in0=xs_t[:, 128 * i:128 * (i + 1)], in1=p[0:4, 0:128], op=ALU.add),
                             reads=[pn, "xs_t"], writes=["xs_t"])
                    T.barrier()
            with contextlib.ExitStack() as esy:
                gf4 = esy.enter_context(nc.sbuf_tensor("gf4", [4, D], F32))
                T.dma("pool", gf4[:], fin_g[0:1, :].broadcast_to([4, D]), writes=["gf4"])
                T.op("dve", lambda e: e.memset(s_ss[:], 0.0), writes=["s_ss"])
                T.op("act", lambda e: e.activation(out=s_jk[:], in_=xs_t[:], func=AF.Square, accum_out=s_ss[:, 0:1]), reads=["xs_t", "s_ss"], writes=["s_jk", "s_ss"])
                T.op("act", lambda e: e.activation(out=s_ss[:, 1:2], in_=s_ss[:, 0:1], func=AF.Ln, scale=1.0 / D, bias=cst[0:4, C_EPS:C_EPS + 1]), reads=["s_ss", "cst"], writes=["s_ss"])
                T.op("act", lambda e: e.activation(out=s_ss[:, 1:2], in_=s_ss[:, 1:2], func=AF.Exp, scale=-0.5), reads=["s_ss"], writes=["s_ss"])
                T.op("dve", lambda e: e.scalar_tensor_tensor(out=s_jk[:], in0=xs_t[:], scalar=s_ss[:, 1:2], in1=gf4[:], op0=ALU.mult, op1=ALU.mult),
                     reads=["xs_t", "s_ss", "gf4"], writes=["s_jk"])
                T.dma("pool", o_ys[:, :], s_jk[:], reads=["s_jk"], writes=["oys"])
                T.barrier()
      except StopBuild:
        T.barrier()
        return nc

    T.barrier()
    es.close()
    return nc


def _bias_tables(rel_bias):
    k = np.arange(128)[:, None]
    q = np.arange(128)[None, :]
    bg = np.zeros((128, 24, 2, 128), np.float32)
    bm = np.zeros((128, 24, 2, 128), np.float32)
    for h in range(24):
        d = DIL[h // 8]
        rel0 = q + 128 - k
        rel1 = q - k
        for kind, rel in ((0, rel0), (1, rel1)):
            valid = (rel >= 0) & (rel <= 128)
            bk = t5_bucket(np.maximum(rel, 0) * d)
            bg[:, h, kind, :] = rel_bias[bk, h]
            bm[:, h, kind, :] = np.where(valid, 0.0, NEG)
    return bg.reshape(128, -1), bm.reshape(128, -1)


def _sample_bias(rel_bias):
    jj = np.arange(128)
    bs = np.zeros((128, 24), np.float32)
    b0 = np.zeros((4, 24), np.float32)
    for h in range(24):
        d = DIL[h // 8]
        bs[:, h] = rel_bias[t5_bucket((128 - jj) * d), h]
        b0[:, h] = rel_bias[0, h]
    return bs, b0


def make_in_maps(inp, ncores=8):
    f = lambda a: np.ascontiguousarray(a, dtype=np.float32)
    rel_bias = np.asarray(inp["rel_bias"], np.float32)
    bg, bm = _bias_tables(rel_bias)
    bs, b0 = _sample_bias(rel_bias)
    consts = make_consts()
    c2 = np.zeros((128, 528), np.float32)
    for b in range(4):
        c2[b, 128 * b:128 * (b + 1)] = 1.0
        c2[:, 512 + 4 * b + b] = 1.0
    shared = {
        "consts2": c2, "bias_g": bg, "bias_m": bm, "bias_s": bs, "bias_0": b0, "consts": consts,
        "w_in": f(inp["w_in"]), "w_pa": f(inp["w_pa"]), "w_pb": f(inp["w_pb"]), "w_o": f(inp["w_o"]),
        "w_up": f(inp["w_up"]), "w_down": f(inp["w_down"]),
        "norm1_g": f(inp["norm1_g"]), "norm2_g": f(inp["norm2_g"]),
        "mconv_w": f(inp["mconv_w"]), "mconv_b": f(inp["mconv_b"]),
        "mgate_b": f(np.asarray(inp["mgate_b"]).reshape(2, 8)),
        "fconv_w": f(inp["fconv_w"]), "fconv_b": f(inp["fconv_b"]),
        "fin_g": f(np.asarray(inp["final_norm_g"]).reshape(1, D)),
        "cw_t": f(np.asarray(inp["mconv_w"]).reshape(2, 4, 16, 128).transpose(0, 3, 2, 1).reshape(2, 128, 64)),
        "cb_t": f(np.asarray(inp["mconv_b"]).reshape(2, 16, 128).transpose(0, 2, 1)),
        "fw_t": f(np.asarray(inp["fconv_w"]).reshape(2, 3, 2, 22, 128).transpose(0, 4, 2, 3, 1).reshape(2, 128, 132)),
        "fb_t": f(np.asarray(inp["fconv_b"]).reshape(2, 2, 22, 128).transpose(0, 3, 1, 2).reshape(2, 128, 44)),
        "gn_t": f(np.stack([np.asarray(inp["norm1_g"]), np.asarray(inp["norm2_g"])], 0).reshape(2, 2, 8, 128).transpose(3, 0, 1, 2).reshape(128, 32)),
    }
    caches = [np.asarray(inp["cache_kv_w128"]), np.asarray(inp["cache_kv_w512"]), np.asarray(inp["cache_kv_w2048"])]
    maps = []
    for c in range(ncores):
        sl = slice(4 * c, 4 * c + 4)
        m = dict(shared)
        m["xp"] = f(np.asarray(inp["x_prompt"])[c % 4])
        m["xs"] = f(np.asarray(inp["x_sample"])[sl, 0, :])
        for g in range(3):
            m["cache%d" % g] = f(caches[g][:, sl, ::DIL[g]].reshape(2, 4, 128, 1024))
        m["st_mconv"] = f(np.asarray(inp["state_mlstm_conv"])[:, sl])
        m["st_C"] = f(np.asarray(inp["state_mlstm_C"])[:, sl])
        m["st_n"] = f(np.asarray(inp["state_mlstm_n"])[:, sl])
        m["st_m"] = f(np.asarray(inp["state_mlstm_m"])[:, sl])
        m["st_fconv"] = f(np.asarray(inp["state_ffn_conv"])[:, sl])
        maps.append(m)
    return maps


def assemble(results):
    R = results
    cat_s = lambda name, axis=1: np.concatenate([R[c][name] for c in range(8)], axis=axis)
    stk_p = lambda name: np.stack([R[b][name] for b in range(4)], axis=1)
    y_prompt = np.stack([R[b]["o_yp"] for b in range(4)], axis=0)
    y_sample = np.concatenate([R[c]["o_ys"] for c in range(8)], axis=0).reshape(32, 1, D)
    outs = [y_prompt, y_sample]
    for g in range(3):
        outs.append(stk_p("o_kvp%d" % g).reshape(2, 4, WIN[g], 2, 8, 64))
        outs.append(cat_s("o_kvs%d" % g).reshape(2, 32, 1, 2, 8, 64))
    outs.append(stk_p("o_mcp").reshape(2, 4, 128, 16, 3).transpose(0, 1, 4, 3, 2).reshape(2, 4, 3, 2048))
    outs.append(cat_s("o_mcs"))
    outs.append(stk_p("o_Cp"))
    outs.append(cat_s("o_Cs"))
    outs.append(stk_p("o_np").reshape(2, 4, 128, 2, 4).transpose(0, 1, 4, 3, 2).reshape(2, 4, 4, 256))
    outs.append(cat_s("o_ns"))
    outs.append(stk_p("o_mp"))
    outs.append(cat_s("o_ms"))
    outs.append(stk_p("o_fcp").reshape(2, 4, 128, 2, 22, 2).transpose(0, 1, 5, 3, 4, 2).reshape(2, 4, 2, 2 * DFF))
    outs.append(cat_s("o_fcs"))
    return tuple(np.ascontiguousarray(o, dtype=np.float32) for o in outs)


def kernel(**inputs):
    nc = build()
    maps = make_in_maps(inputs)
    res = run_bass_kernel_spmd(nc, maps, core_ids=list(range(8)))
    return assemble(res.results)
```

```python
import contextlib
import math
import numpy as np
import concourse.bass as bass
import concourse.mybir as mybir
from concourse.bass_utils import run_bass_kernel_spmd

F32 = mybir.dt.float32
BF16 = mybir.dt.bfloat16
AF = mybir.ActivationFunctionType
ALU = mybir.AluOpType
AX = mybir.AxisListType

D = 1024
SEQ = 8192
DEPTH = 2
TT = 512
NT_FULL = SEQ // TT
DIL = (1, 4, 16)
WIN = (128, 512, 2048)
DFF = 2816
INC = 10760
SLOT = 4096
DMA_ALL_SP = False
NEG = -30000.0
EPS = 1e-6

O_QA, O_KA, O_VA, O_QB, O_KB, O_VB, O_OB, O_I, O_GA, O_GB = 0, 1536, 3072, 4608, 5632, 6656, 7680, 8704, 8712, 9736


def slab_list():
    L = []
    for i in range(3):
        L.append(("qa", i, "w_in", 8, [(O_QA + 512 * i, 512)], 1))
    for i in range(3):
        L.append(("ka", i, "w_in", 8, [(O_KA + 512 * i, 512)], 1))
    for i in range(3):
        L.append(("va", i, "w_in", 8, [(O_VA + 512 * i, 512)], 1))
    for i in range(2):
        L.append(("qb", i, "w_in", 8, [(O_QB + 512 * i, 512)], 1))
    for i in range(2):
        L.append(("kb", i, "w_in", 8, [(O_KB + 512 * i, 512)], 1))
    for i in range(2):
        L.append(("vb", i, "w_in", 8, [(O_VB + 512 * i, 512)], 1))
    for i in range(2):
        L.append(("ob", i, "w_in", 8, [(O_OB + 512 * i, 512)], 1))
    L.append(("gt", 0, "w_in", 8, [(O_I, 8)], 1))
    for i in range(2):
        L.append(("ga", i, "w_in", 8, [(O_GA + 512 * i, 512)], 1))
    for i in range(2):
        L.append(("gb", i, "w_in", 8, [(O_GB + 512 * i, 512)], 1))
    L.append(("pa", 0, "w_pa", 4, [(0, 1024)], 0))
    for i in range(2):
        L.append(("pb", i, "w_pb", 8, [(512 * i, 512)], 0))
    for i in range(2):
        L.append(("wo", i, "w_o", 8, [(512 * i, 512)], 0))
    for i in range(11):
        L.append(("up", i, "w_up", 8, [(256 * i, 256), (DFF + 256 * i, 256)], 2))
    for i in range(8):
        L.append(("dn", i, "w_down", 22, [(128 * i, 128)], 0))
    return L


SLABS = slab_list()
NSLAB = len(SLABS)


class Trk:
    NDS = 48

    def __init__(self, nc, es):
        self.nc = nc
        self.E = {"pe": nc.tensor, "act": nc.scalar, "dve": nc.vector, "pool": nc.gpsimd, "sp": nc.sync}
        self.sem = {e: es.enter_context(nc.semaphore("sem_" + e)) for e in self.E}
        self.cnt = {e: 0 for e in self.E}
        self.seen = {e: {} for e in self.E}
        self.res = {}
        self.dsems = [es.enter_context(nc.semaphore("dsem%d" % i)) for i in range(self.NDS)]
        self.dcnt = [0] * self.NDS
        self.dnext = 0

    def semh(self, key):
        return self.sem[key] if isinstance(key, str) else self.dsems[key[1]]

    def _wait(self, e, key, val):
        if e == "pe" and key == "pe":
            return
        if self.seen[e].get(key, 0) >= val:
            return
        self.seen[e][key] = val
        self.E[e].wait_ge(self.semh(key), val)

    def deps(self, e, reads, writes):
        toks = {}

        def add(k, v):
            if toks.get(k, 0) < v:
                toks[k] = v
        for r in reads:
            st = self.res.get(r)
            if st and st["w"]:
                add(*st["w"])
        for w in writes:
            st = self.res.get(w)
            if st:
                if st["w"]:
                    add(*st["w"])
                for k, v in st["r"].items():
                    add(k, v)
        for k, v in toks.items():
            self._wait(e, k, v)

    def commit(self, tok, reads, writes):
        for r in reads:
            st = self.res.setdefault(r, {"w": None, "r": {}})
            if st["r"].get(tok[0], 0) < tok[1]:
                st["r"][tok[0]] = tok[1]
        for w in writes:
            self.res[w] = {"w": tok, "r": {}}

    def op(self, e, fn, reads=(), writes=()):
        self.deps(e, reads, writes)
        ins = fn(self.E[e])
        self.cnt[e] += 1
        ins.then_inc(self.sem[e], 1)
        self.commit((e, self.cnt[e]), reads, writes)

    def dma(self, q, out, in_, reads=(), writes=(), slow=False):
        if DMA_ALL_SP:
            q = "sp"
        self.deps(q, reads, writes)
        i = self.dnext
        self.dnext = (i + 1) % self.NDS
        if self.dcnt[i] > 0:
            self._wait(q, ("d", i), 16 * self.dcnt[i])
        self.dcnt[i] += 1
        if slow:
            ins = self.E[q].dma_start(out=out, in_=in_, allow_slow_non_contiguous=True)
        else:
            ins = self.E[q].dma_start(out=out, in_=in_)
        ins.then_inc(self.dsems[i], 16)
        self.commit((("d", i), 16 * self.dcnt[i]), reads, writes)

    def alias(self, olds, news):
        for n in news:
            st = self.res.setdefault(n, {"w": None, "r": {}})
            for o in olds:
                so = self.res.get(o)
                if not so:
                    continue
                if so["w"] and st["r"].get(so["w"][0], 0) < so["w"][1]:
                    st["r"][so["w"][0]] = so["w"][1]
                for k, v in so["r"].items():
                    if st["r"].get(k, 0) < v:
                        st["r"][k] = v

    def barrier(self):
        for e in self.E:
            for f in self.E:
                if f != e and self.cnt[f] > 0:
                    self._wait(e, f, self.cnt[f])
            for i in range(self.NDS):
                if self.dcnt[i] > 0:
                    self._wait(e, ("d", i), 16 * self.dcnt[i])
        self.res = {}


def t5_bucket(dist):
    dist = np.asarray(dist, dtype=np.int64)
    df = np.maximum(dist, 1).astype(np.float32)
    large = 16 + (np.log(df / np.float32(16)) / np.float32(math.log(2048 / 16)) * np.float32(16)).astype(np.int32)
    large = np.minimum(large, 31)
    return np.where(dist < 16, dist, large).astype(np.int64)


C_ID, C_TRI, C_ONE, C_SEL63, C_CNEG, C_EPS, C_END = 0, 128, 192, 320, 448, 704, 708


def make_consts():
    c = np.zeros((128, C_END), np.float32)
    c[:, C_ID:C_ID + 128] = np.eye(128, dtype=np.float32)
    s = np.arange(64)
    c[:64, C_TRI:C_TRI + 64] = (s[:, None] <= s[None, :]).astype(np.float32)
    c[:, C_ONE:C_ONE + 128] = 1.0
    c[63, C_SEL63:C_SEL63 + 128] = 1.0
    cn = np.where(s[None, :] <= s[:, None], 0.0, -1e30).astype(np.float32)
    c[:64, C_CNEG:C_CNEG + 256] = np.tile(cn, (1, 4))
    c[:, C_EPS] = EPS
    return c


class StopBuild(Exception):
    pass


def build(NT=NT_FULL, do_sample=True, stage=99):
    def chk(n):
        if abs(stage) == n:
            raise StopBuild()

    nc = bass.Bass("TRN2", target_bir_lowering=False)
    es = contextlib.ExitStack()

    def din(name, shape):
        return nc.dram_tensor(name, list(shape), F32, kind="ExternalInput").ap()

    def dout(name, shape):
        return nc.dram_tensor(name, list(shape), F32, kind="ExternalOutput").ap()

    xp = din("xp", [SEQ, D])
    xs = din("xs", [4, D])
    cache = [din("cache%d" % g, [2, 4, 128, 1024]) for g in range(3)]
    st_mconv = din("st_mconv", [2, 4, 3, 2048])
    st_C = din("st_C", [2, 4, 4, 256, 256])
    st_n = din("st_n", [2, 4, 4, 256])
    st_m = din("st_m", [2, 4, 4])
    st_fconv = din("st_fconv", [2, 4, 2, 2 * DFF])
    bias_g = din("bias_g", [128, 24 * 2 * 128])
    bias_m = din("bias_m", [128, 24 * 2 * 128])
    bias_s = din("bias_s", [128, 24])
    bias_0 = din("bias_0", [4, 24])
    consts = din("consts", [128, C_END])
    consts2 = din("consts2", [128, 528])
    W = {
        "w_in": din("w_in", [2, D, INC]), "w_pa": din("w_pa", [2, 512, D]), "w_pb": din("w_pb", [2, D, D]),
        "w_o": din("w_o", [2, D, D]), "w_up": din("w_up", [2, D, 2 * DFF]), "w_down": din("w_down", [2, DFF, D]),
    }
    norm1_g = din("norm1_g", [2, D])
    norm2_g = din("norm2_g", [2, D])
    mconv_w = din("mconv_w", [2, 4, 2048])
    mconv_b = din("mconv_b", [2, 2048])
    mgate_b = din("mgate_b", [2, 8])
    fconv_w = din("fconv_w", [2, 3, 2 * DFF])
    fconv_b = din("fconv_b", [2, 2 * DFF])
    fin_g = din("fin_g", [1, D])
    cw_t = din("cw_t", [2, 128, 64])
    cb_t = din("cb_t", [2, 128, 16])
    fw_t = din("fw_t", [2, 128, 132])
    fb_t = din("fb_t", [2, 128, 44])
    gn_t = din("gn_t", [128, 32])

    o_yp = dout("o_yp", [SEQ, D])
    o_ys = dout("o_ys", [4, D])
    o_kvp = [dout("o_kvp%d" % g, [2, WIN[g], 1024]) for g in range(3)]
    o_kvs = [dout("o_kvs%d" % g, [2, 4, 1024]) for g in range(3)]
    o_mcp = dout("o_mcp", [2, 128, 48])
    o_mcs = dout("o_mcs", [2, 4, 3, 2048])
    o_Cp = dout("o_Cp", [2, 4, 256, 256])
    o_Cs = dout("o_Cs", [2, 4, 4, 256, 256])
    o_np = dout("o_np", [2, 128, 8])
    o_ns = dout("o_ns", [2, 4, 4, 256])
    o_mp = dout("o_mp", [2, 4])
    o_ms = dout("o_ms", [2, 4, 4])
    o_fcp = dout("o_fcp", [2, 128, 88])
    o_fcs = dout("o_fcs", [2, 4, 2, 2 * DFF])

    wbf = nc.dram_tensor("wbf", [2, NSLAB, 128, SLOT], BF16, kind="Internal").ap()
    xscr = nc.dram_tensor("xscr", [SEQ, D], F32, kind="Internal").ap()
    k3d = nc.dram_tensor("k3d", [4, 128, 16, 2, 128], BF16, kind="Internal").ap()
    v3d = nc.dram_tensor("v3d", [4, 128, 16, 2, 128], BF16, kind="Internal").ap()

    T = Trk(nc, es)

    def sb(name, shape, dt=F32):
        return es.enter_context(nc.sbuf_tensor(name, list(shape), dt))

    cst = sb("cst", [128, C_END])
    cstb = sb("cstb", [128, 320], BF16)
    T.dma("pool", cst[:], consts[:, :], writes=["cst"])
    T.op("dve", lambda e: e.tensor_copy(out=cstb[:, 0:320], in_=cst[:, 0:320]), reads=["cst"], writes=["cstb"])
    identb = cstb[:, 0:128]
    onesb = cstb[:, 192:320]
    identf = cst[:, C_ID:C_ID + 128]

    psum = [es.enter_context(nc.psum_tensor("ps%d" % i, [128, 512], F32)) for i in range(6)]
    psb = [es.enter_context(nc.psum_tensor("psb%d" % i, [128, 1024], BF16)) for i in range(2)]
    pctr = [0, 0]

    def nps():
        i = pctr[0] % 4
        pctr[0] += 1
        return psum[i], "ps%d" % i

    def npb():
        i = pctr[1] % 2
        pctr[1] += 1
        return psb[i], "psb%d" % i

    wsl = sb("wsl", [128, 3, SLOT], BF16)

    class WS:
        def __init__(self):
            self.seq = []
            self.i = 0
            self.issued = 0

        def extend(self, items):
            self.seq.extend(items)

        def _issue(self):
            if self.issued < len(self.seq):
                l, si = self.seq[self.issued]
                k = self.issued % 3
                kind, idx, src, nk, cols, gain = SLABS[si]
                n = nk * sum(c[1] for c in cols)
                T.dma("sp", wsl[:, k, 0:n], wbf[l, si, :, 0:n], reads=["wbf%d_%d" % (l, si)], writes=["wsl%d" % k])
                self.issued += 1

        def get(self, l, si):
            assert self.seq[self.i] == (l, si), (self.seq[self.i], l, si)
            while self.issued < min(self.i + 3, len(self.seq)):
                self._issue()
            k = self.i % 3
            self.i += 1
            kind, idx, src, nk, cols, gain = SLABS[si]
            nc_ = sum(c[1] for c in cols)
            return wsl[:, k, 0:nk * nc_].rearrange("p (k c) -> p k c", k=nk), "wsl%d" % k

    ws = WS()

    with contextlib.ExitStack() as es0:
        stg = [es0.enter_context(nc.sbuf_tensor("stg%d" % i, [128, SLOT], F32)) for i in range(2)]
        stb = [es0.enter_context(nc.sbuf_tensor("stb%d" % i, [128, SLOT], BF16)) for i in range(2)]
        gn = es0.enter_context(nc.sbuf_tensor("gn", [128, 2, 2, 8], F32))
        T.dma("pool", gn[:].rearrange("p a b c -> p (a b c)"), gn_t[:, :], writes=["gn"])
        j = 0
        for l in range(2):
            for si, (kind, idx, src, nk, cols, gain) in enumerate(SLABS):
                if stage < 0:
                    continue
                k = j % 2
                j += 1
                nc_ = sum(c[1] for c in cols)
                n = nk * nc_
                sview = stg[k][:, 0:n].rearrange("p (k c) -> p k c", k=nk)
                bview = stb[k][:, 0:n].rearrange("p (k c) -> p k c", k=nk)
                wsrc = W[src][l].rearrange("(k p) c -> p k c", p=128)
                off = 0
                for (c0, cn) in cols:
                    T.dma("sp" if (j % 2) else "pool", sview[:, :, off:off + cn], wsrc[:, :, c0:c0 + cn],
                          writes=["stg%d" % k], slow=(cn < 128))
                    off += cn
                if gain:
                    for kc in range(nk):
                        eng = "dve" if kc % 2 == 0 else "act"
                        if eng == "dve":
                            T.op("dve", lambda e, kc=kc: e.tensor_scalar(out=bview[:, kc, :], in0=sview[:, kc, :],
                                 scalar1=gn[:, gain - 1, l, kc:kc + 1], scalar2=None, op0=ALU.mult),
                                 reads=["stg%d" % k, "gn"], writes=["stb%d" % k])
                        else:
                            T.op("act", lambda e, kc=kc: e.activation(out=bview[:, kc, :], in_=sview[:, kc, :],
                                 func=AF.Copy, scale=gn[:, gain - 1, l, kc:kc + 1]),
                                 reads=["stg%d" % k, "gn"], writes=["stb%d" % k])
                else:
                    h = n // 2
                    T.op("dve", lambda e: e.tensor_copy(out=stb[k][:, 0:h], in_=stg[k][:, 0:h]),
                         reads=["stg%d" % k], writes=["stb%d" % k])
                    T.op("act", lambda e: e.activation(out=stb[k][:, h:n], in_=stg[k][:, h:n], func=AF.Copy),
                         reads=["stg%d" % k], writes=["stb%d" % k])
                T.dma("sp" if (j % 2) else "pool", wbf[l, si, :, 0:n], stb[k][:, 0:n], reads=["stb%d" % k],
                      writes=["wbf%d_%d" % (l, si)])
        T.barrier()

    try:
      with contextlib.ExitStack() as es1:
        if stage == 0:
            raise StopBuild()
        def sb1(name, shape, dt=F32):
            return es1.enter_context(nc.sbuf_tensor(name, list(shape), dt))

        xt = sb1("xt", [128, 4, D])
        hb = sb1("hb", [128, D], BF16)
        junk = hb
        hT = sb1("hT", [128, 8, TT], BF16)
        ss = sb1("ss", [128, 4])
        rs = sb1("rs", [128, 4])
        U = sb1("U", [128, 32, TT], BF16)
        QT, qcT, kcT = U[:, 0:12, :], U[:, 12:20, :], U[:, 20:28, :]
        boT, sgT, mT, actT = U[:, 0:8, :], U[:, 8:24, :], U[:, 24:32, :], U[:, 0:22, :]
        KT1 = sb1("KT1", [128, 4, 640], BF16)
        KT2 = sb1("KT2", [128, 4, 4, 2, 128], BF16)
        KT3 = sb1("KT3", [128, 16, 2, 128], BF16)
        k3n = sb1("k3n", [128, 16, 32], BF16)
        v3n = sb1("v3n", [128, 2, 512], BF16)
        V1 = sb1("V1", [128, 5, 512], BF16)
        V2 = sb1("V2", [128, 4, 2, 512], BF16)
        V3 = sb1("V3", [128, 16, 2, 128], BF16)
        btab = sb1("btab", [128, 24, 2, 128], BF16)
        pre = sb1("pre", [128, 2, 516])
        acc = sb1("acc", [128, 2, 512])
        fpre, fc = pre, acc
        cw = sb1("cw", [128, 16, 4])
        cb = sb1("cb", [128, 16])
        carry = sb1("carry", [128, 16, 3])
        vaug = sb1("vaug", [64, 8, 4, 258], BF16)
        ogT = sb1("ogT", [128, 8, TT], BF16)
        graw = sb1("graw", [64, 8, 8])
        gbias = sb1("gbias", [64, 8])
        igt = sb1("igt", [64, 8, 4])
        lft = sb1("lft", [64, 8, 4])
        aT = sb1("aT", [128, 4, TT], BF16)
        Cf = sb1("Cf", [128, 2, 4, 257])
        Cb = sb1("Cb", [128, 2, 4, 258], BF16)
        mprev = sb1("mprev", [128, 4])
        sm2 = sb1("sm", [128, 2, 512], BF16)
        mgA = sm2[:, 0, :]
        stmp2 = sb1("stmp", [128, 2, 256])
        rden = sb1("rden", [128, 512])
        stf = rden
        ft = rden
        g_sb = sb1("g_sb", [64, 4])
        b_sb = sb1("b_sb", [64, 4])
        dg = sb1("dg", [64, 256])
        Lm = sb1("Lm", [64, 256])
        dw = Lm
        stat = sb1("stat", [64, 12])
        small = sb1("small", [64, 16])
        bc = sb1("bc", [128, 12])
        wc = sb1("wc", [64, 4])
        Pbf = sb1("Pbf", [64, 256], BF16)
        PTs = sb1("PTs", [64, 256], BF16)
        ktok = sb1("ktok", [64, D], BF16)
        intra = sb1("intra", [64, 257])
        nd = intra
        bo = sb1("bo", [64, D], BF16)
        fw = sb1("fw", [128, 2, 22, 3])
        fb = sb1("fb", [128, 2, 22])
        fcar = sb1("fcar", [128, 2, 22, 2])

        T.op("dve", lambda e: e.memset(vaug[:], 1.0), writes=["vaug"])
        for nm, t in (("KT1", KT1), ("KT2", KT2), ("KT3", KT3), ("V1", V1), ("V2", V2), ("V3", V3)):
            T.op("dve", lambda e, t=t: e.memset(t[:], 0.0), writes=[nm])
        btf = acc[:, 0, :]
        btm = acc[:, 1, :]
        for q in range(12):
            T.dma("pool", btf, bias_g[:, q * 512:(q + 1) * 512], writes=["acc0"])
            T.dma("pool", btm, bias_m[:, q * 512:(q + 1) * 512], writes=["acc1"])
            T.op("dve", lambda e, q=q: e.tensor_tensor(out=btab[:, 2 * q:2 * q + 2, :, :].rearrange("p a b c -> p (a b c)"),
                 in0=btf, in1=btm, op=ALU.add), reads=["acc0", "acc1"], writes=["btab"])

        chk(1)

        def norm_T(srcs):
            T.op("dve", lambda e: e.memset(ss[:], 0.0), writes=["ss"])
            for s in range(4):
                T.op("act", lambda e: e.activation(out=junk[:], in_=xt[:, s, :], func=AF.Square, accum_out=ss[:, s:s + 1]),
                     reads=["xt%d" % s, "ss"], writes=["hb", "ss"])
            T.op("act", lambda e: e.activation(out=rs[:], in_=ss[:], func=AF.Ln, scale=1.0 / D, bias=cst[:, C_EPS:C_EPS + 1]), reads=["ss", "cst"], writes=["rs"])
            T.op("act", lambda e: e.activation(out=rs[:], in_=rs[:], func=AF.Exp, scale=-0.5), reads=["rs"], writes=["rs"])
            for s in range(4):
                T.op("dve", lambda e: e.tensor_scalar(out=hb[:], in0=xt[:, s, :], scalar1=rs[:, s:s + 1], scalar2=None, op0=ALU.mult),
                     reads=["xt%d" % s, "rs"], writes=["hb"])
                pb_, pn = npb()
                for kc in range(8):
                    T.op("pe", lambda e: e.transpose(out=pb_[:, kc * 128:(kc + 1) * 128], in_=hb[:, kc * 128:(kc + 1) * 128], identity=identb),
                         reads=["hb", "cstb"], writes=[pn])
                T.op("act", lambda e: e.activation(out=hT[:, :, s * 128:(s + 1) * 128], in_=pb_[:, :].rearrange("p (k t) -> p k t", k=8), func=AF.Copy),
                     reads=[pn], writes=["hT"])

        def mm(out, lhsT, rhs, start, stop, reads, writes, **kw):
            T.op("pe", lambda e: e.matmul(out, lhsT, rhs, start=start, stop=stop, skip_group_check=True, **kw), reads=reads, writes=writes)

        def fm_chunk(wv, wn, c, rhsT, rname, nk=8):
            p, pn = nps()
            for kc in range(nk):
                mm(p[:, :], wv[:, kc, c * 128:(c + 1) * 128], rhsT[:, kc, :], kc == 0, kc == nk - 1, [wn, rname], [pn])
            return p, pn

        for l in range(DEPTH):
            ws.extend([(l, si) for _ in range(NT) for si in range(NSLAB)])
        SIDX = {}
        for si, sdef in enumerate(SLABS):
            SIDX[(sdef[0], sdef[1])] = si

        for l in range(DEPTH):
            xsrc = xp if l == 0 else xscr
            T.dma("pool", cw[:].rearrange("p c k -> p (c k)"), cw_t[l], writes=["cw"])
            T.dma("pool", cb[:], cb_t[l], writes=["cb"])
            T.dma("pool", fw[:].rearrange("p h j k -> p (h j k)"), fw_t[l], writes=["fw"])
            T.dma("pool", fb[:].rearrange("p h j -> p (h j)"), fb_t[l], writes=["fb"])
            T.dma("pool", gbias[:], mgate_b[l:l + 1, :].broadcast_to([64, 8]), writes=["gbias"])
            T.op("dve", lambda e: e.memset(carry[:], 0.0), writes=["carry"])
            T.op("dve", lambda e: e.memset(fcar[:], 0.0), writes=["fcar"])
            T.op("dve", lambda e: e.memset(Cf[:], 0.0), writes=["Cf"])
            T.op("dve", lambda e: e.memset(Cb[:], 0.0), writes=["Cb"])
            T.op("dve", lambda e: e.memset(mprev[:], 0.0), writes=["mprev"])
            T.op("dve", lambda e: e.memset(KT3[:], 0.0), writes=["KT3"])
            for c in range(4):
                T.dma("pool", k3d[c], KT3[:], reads=["KT3"], writes=["k3d%d" % c])
                T.dma("pool", v3d[c], KT3[:], reads=["KT3"], writes=["v3d%d" % c])

            for ti in range(NT):
                t0 = ti * TT
                last = (ti == NT - 1)
                for s in range(4):
                    T.dma("pool", xt[:, s, :], xsrc[t0 + s * 128:t0 + (s + 1) * 128, :], reads=["xscr"] if l else [], writes=["xt%d" % s])
                chk(2)
                norm_T(None)
                chk(3)
                T.alias(["actT", "mT", "sgT", "boT"], ["QT", "qcT", "kcT"])
                par2 = ti % 2
                v3 = ti % 4
                par3 = (ti // 4) % 2

                for g in range(3):
                    wv, wn = ws.get(l, SIDX[("qa", g)])
                    d = DIL[g]
                    for c in range(4):
                        p, pn = fm_chunk(wv, wn, c, hT, "hT")
                        if d == 1:
                            T.op("act", lambda e: e.activation(out=QT[:, c, :], in_=p[:, :], func=AF.Copy), reads=[pn], writes=["QT"])
                        else:
                            T.op("act", lambda e: e.activation(out=QT[:, 4 * g + c, :].rearrange("p (r u) -> p r u", r=d),
                                 in_=p[:, :].rearrange("p (u r) -> p r u", r=d), func=AF.Copy), reads=[pn], writes=["QT"])
                chk(4)
                for g in range(3):
                    wv, wn = ws.get(l, SIDX[("ka", g)])
                    d = DIL[g]
                    for c in range(4):
                        p, pn = fm_chunk(wv, wn, c, hT, "hT")
                        if g == 0:
                            T.op("dve", lambda e: e.tensor_copy(out=KT1[:, c, 128:640], in_=p[:, :]), reads=[pn], writes=["KT1"])
                        elif g == 1:
                            T.op("dve", lambda e: e.tensor_copy(out=KT2[:, c, :, par2, :], in_=p[:, :].rearrange("p (u r) -> p r u", r=4)),
                                 reads=[pn], writes=["KT2"])
                        else:
                            T.op("dve", lambda e: e.tensor_copy(out=k3n[:], in_=p[:, :].rearrange("p (u r) -> p r u", r=16)), reads=[pn], writes=["k3n"])
                            T.dma("pool", k3d[c, :, :, par3, 32 * v3:32 * v3 + 32], k3n[:], reads=["k3n"], writes=["k3d%d" % c], slow=True)
                    row0 = t0 - (NT * TT - WIN[g])
                    for s in range(4):
                        r0 = row0 + s * 128
                        if r0 >= 0:
                            p, pn = nps()
                            for kc in range(8):
                                mm(p[:, :], hT[:, kc, s * 128:(s + 1) * 128], wv[:, kc, :], kc == 0, kc == 7, [wn, "hT"], [pn])
                            T.op("act", lambda e: e.activation(out=stf[:], in_=p[:, :], func=AF.Copy), reads=[pn], writes=["rden"])
                            T.dma("pool", o_kvp[g][l, r0:r0 + 128, 0:512], stf[:], reads=["rden"], writes=["okv"])
                chk(5)
                for g in range(3):
                    wv, wn = ws.get(l, SIDX[("va", g)])
                    row0 = t0 - (NT * TT - WIN[g])
                    if g == 0:
                        for s in range(4):
                            p, pn = nps()
                            for kc in range(8):
                                mm(p[:, :], hT[:, kc, s * 128:(s + 1) * 128], wv[:, kc, :], kc == 0, kc == 7, [wn, "hT"], [pn])
                            T.op("act", lambda e: e.activation(out=V1[:, s + 1, :], in_=p[:, :], func=AF.Copy), reads=[pn], writes=["V1"])
                            r0 = row0 + s * 128
                            if r0 >= 0:
                                T.op("dve", lambda e: e.tensor_copy(out=stf[:], in_=p[:, :]), reads=[pn], writes=["rden", pn])
                                T.dma("pool", o_kvp[g][l, r0:r0 + 128, 512:1024], stf[:], reads=["rden"], writes=["okv"])
                    else:
                        d = DIL[g]
                        hTr = hT[:, :, :].rearrange("p k (u r) -> p k r u", r=d)
                        for r in range(d):
                            p, pn = nps()
                            if g == 1:
                                for kc in range(8):
                                    mm(p[:, :], hTr[:, kc, r, :], wv[:, kc, :], kc == 0, kc == 7, [wn, "hT"], [pn])
                                T.op("act", lambda e: e.activation(out=V2[:, r, par2, :], in_=p[:, :], func=AF.Copy), reads=[pn], writes=["V2"])
                            else:
                                lo = 32 * v3
                                for kc in range(8):
                                    mm(p[lo:lo + 32, :], hTr[:, kc, r, :], wv[:, kc, :], kc == 0, kc == 7, [wn, "hT"], [pn], tile_position=(0, lo))
                                vk = r % 2
                                T.op("act", lambda e: e.activation(out=v3n[lo:lo + 32, vk, :], in_=p[lo:lo + 32, :], func=AF.Copy), reads=[pn], writes=["v3n%d" % vk])
                                T.dma("pool", v3d[:, lo:lo + 32, r, par3, :].rearrange("c p f -> p c f"), v3n[lo:lo + 32, vk, :].rearrange("p (c f) -> p c f", c=4),
                                      reads=["v3n%d" % vk], writes=["v3d0", "v3d1", "v3d2", "v3d3"], slow=True)
                        for s in range(4):
                            r0 = row0 + s * 128
                            if r0 >= 0:
                                p, pn = nps()
                                for kc in range(8):
                                    mm(p[:, :], hT[:, kc, s * 128:(s + 1) * 128], wv[:, kc, :], kc == 0, kc == 7, [wn, "hT"], [pn])
                                T.op("dve", lambda e: e.tensor_copy(out=stf[:], in_=p[:, :]), reads=[pn], writes=["rden"])
                                T.dma("pool", o_kvp[g][l, r0:r0 + 128, 512:1024], stf[:], reads=["rden"], writes=["okv"])
                chk(6)
                for which, dst in (("qb", qcT), ("kb", kcT)):
                    for i2 in range(2):
                        wv, wn = ws.get(l, SIDX[(which, i2)])
                        for c in range(4):
                            ch = (0 if which == "qb" else 8) + 4 * i2 + c
                            k2 = ch % 2
                            p, pn = fm_chunk(wv, wn, c, hT, "hT")
                            pr, ac = "pre%d" % k2, "acc%d" % k2
                            T.op("dve", lambda e: e.tensor_copy(out=pre[:, k2, 0:3], in_=carry[:, ch, :]), reads=["carry"], writes=[pr])
                            T.op("act", lambda e: e.activation(out=pre[:, k2, 3:515], in_=p[:, :], func=AF.Copy), reads=[pn], writes=[pr])
                            T.op("dve", lambda e: e.tensor_copy(out=carry[:, ch, :], in_=pre[:, k2, 512:515]), reads=[pr], writes=["carry"])
                            T.op("dve", lambda e: e.tensor_scalar(out=acc[:, k2, :], in0=pre[:, k2, 0:512], scalar1=cw[:, ch, 0:1], scalar2=cb[:, ch:ch + 1],
                                 op0=ALU.mult, op1=ALU.add), reads=[pr, "cw", "cb"], writes=[ac])
                            for tp in range(1, 4):
                                T.op("dve", lambda e: e.scalar_tensor_tensor(out=acc[:, k2, :], in0=pre[:, k2, tp:tp + 512], scalar=cw[:, ch, tp:tp + 1],
                                     in1=acc[:, k2, :], op0=ALU.mult, op1=ALU.add), reads=[pr, "cw", ac], writes=[ac])
                            cc = 4 * i2 + c
                            if which == "qb":
                                T.op("act", lambda e: e.activation(out=dst[:, cc, :], in_=acc[:, k2, :], func=AF.Silu), reads=[ac], writes=["qcT"])
                            else:
                                T.op("act", lambda e: e.activation(out=acc[:, k2, :], in_=acc[:, k2, :], func=AF.Silu), reads=[ac], writes=[ac])
                                T.op("dve", lambda e: e.tensor_scalar(out=dst[:, cc, :], in0=acc[:, k2, :], scalar1=1.0 / 16.0, scalar2=None, op0=ALU.mult),
                                     reads=[ac], writes=["kcT"])
                chk(7)
                for which in ("vb", "ob"):
                    for i2 in range(2):
                        wv, wn = ws.get(l, SIDX[(which, i2)])
                        if which == "ob":
                            for c in range(4):
                                p, pn = fm_chunk(wv, wn, c, hT, "hT")
                                T.op("act", lambda e: e.activation(out=ogT[:, 4 * i2 + c, :], in_=p[:, :], func=AF.Sigmoid), reads=[pn], writes=["ogT"])
                            continue
                        for ch in range(8):
                            p, pn = nps()
                            for kc in range(8):
                                mm(p[0:64, :], hT[:, kc, ch * 64:(ch + 1) * 64], wv[:, kc, :], kc == 0, kc == 7, [wn, "hT"], [pn])
                            if which == "vb":
                                T.op("act", lambda e: e.activation(out=vaug[:, ch, 2 * i2:2 * i2 + 2, 0:256], in_=p[0:64, :].rearrange("p (h v) -> p h v", h=2),
                                     func=AF.Copy), reads=[pn], writes=["vaug"])
                            else:
                                T.op("act", lambda e: e.activation(out=og[:, ch, 512 * i2:512 * i2 + 512], in_=p[0:64, :], func=AF.Sigmoid), reads=[pn], writes=["og"])
                wv, wn = ws.get(l, SIDX[("gt", 0)])
                p, pn = nps()
                for ch in range(8):
                    for kc in range(8):
                        mm(p[0:64, ch * 8:(ch + 1) * 8], hT[:, kc, ch * 64:(ch + 1) * 64], wv[:, kc, :], (kc == 0 and ch == 0), kc == 7, [wn, "hT"], [pn])
                for ch in range(8):
                    T.op("dve", lambda e: e.tensor_tensor(out=graw[:, ch, :], in0=p[0:64, ch * 8:(ch + 1) * 8], in1=gbias[:, :], op=ALU.add),
                         reads=[pn, "gbias"], writes=["graw"])
                T.op("dve", lambda e: e.tensor_copy(out=igt[:], in_=graw[:, :, 0:4]), reads=["graw"], writes=["igt"])
                T.op("act", lambda e: e.activation(out=lft[:], in_=graw[:, :, 4:8], func=AF.Exp, scale=-1.0), reads=["graw"], writes=["lft"])
                T.op("act", lambda e: e.activation(out=lft[:], in_=lft[:], func=AF.Ln, bias=cst[0:64, C_ONE:C_ONE + 1]), reads=["lft", "cst"], writes=["lft"])
                T.op("dve", lambda e: e.tensor_scalar(out=lft[:], in0=lft[:], scalar1=-1.0, scalar2=None, op0=ALU.mult), reads=["lft"], writes=["lft"])
                chk(8)
                for c in range(4):
                    num, den = psum[4], psum[5]
                    T.dma("pool", KT3[:], k3d[c], reads=["k3d%d" % c], writes=["KT3"])
                    T.dma("pool", V3[:], v3d[c], reads=["v3d%d" % c], writes=["V3"])
                    first = {0: True, 64: True}
                    for hh in range(2):
                        h = 2 * c + hh
                        pb = 64 * hh
                        blocks = []
                        for b in range(4):
                            kb = []
                            if not (ti == 0 and b == 0):
                                kb.append((KT1[pb:pb + 64, c, b * 128:(b + 1) * 128], V1[:, b, h * 64:(h + 1) * 64], 0))
                            kb.append((KT1[pb:pb + 64, c, (b + 1) * 128:(b + 2) * 128], V1[:, b + 1, h * 64:(h + 1) * 64], 1))
                            blocks.append((0, QT[pb:pb + 64, c, b * 128:(b + 1) * 128], 128, kb, slice(b * 128, (b + 1) * 128), slice(0, 128)))
                        for r in range(4):
                            kb = []
                            if ti > 0:
                                kb.append((KT2[pb:pb + 64, c, r, 1 - par2, :], V2[:, r, 1 - par2, h * 64:(h + 1) * 64], 0))
                            kb.append((KT2[pb:pb + 64, c, r, par2, :], V2[:, r, par2, h * 64:(h + 1) * 64], 1))
                            blocks.append((1, QT[pb:pb + 64, 4 + c, r * 128:(r + 1) * 128], 128, kb, ("str", r, 4), slice(0, 128)))
                        for r in range(16):
                            kb = []
                            if ti // 4 > 0:
                                kb.append((KT3[pb:pb + 64, r, 1 - par3, :], V3[:, r, 1 - par3, hh * 64:(hh + 1) * 64], 0))
                            kb.append((KT3[pb:pb + 64, r, par3, :], V3[:, r, par3, hh * 64:(hh + 1) * 64], 1))
                            blocks.append((2, QT[pb:pb + 64, 8 + c, r * 32:(r + 1) * 32], 32, kb, ("str", r, 16), slice(32 * v3, 32 * v3 + 32)))
                        def emit_st(blk):
                            g_, qv_, nq_, kb_, _oc, _qs = blk
                            p_, pn_ = nps()
                            for (ktv_, vv_, kind_) in kb_:
                                mm(p_[:, kind_ * nq_:(kind_ + 1) * nq_], ktv_, qv_, True, True, ["KT%d" % (g_ + 1), "QT"], [pn_])
                            return p_, pn_

                        pend = emit_st(blocks[0])
                        for bi, (g, qv, nq, kb, ocol, qsl) in enumerate(blocks):
                            p, pn = pend
                            if bi + 1 < len(blocks):
                                pend = emit_st(blocks[bi + 1])
                            k0 = kb[0][2]
                            nk_ = len(kb)
                            sm, stmp = sm2[:, bi % 2, :], stmp2[:, bi % 2, :]
                            smn, stn = "sm%d" % (bi % 2), "stmp%d" % (bi % 2)
                            pv_ = p[:, k0 * nq:(k0 + nk_) * nq].rearrange("p (k q) -> p k q", k=nk_)
                            tv = stmp[:, k0 * nq:(k0 + nk_) * nq].rearrange("p (k q) -> p k q", k=nk_)
                            sv = sm[:, k0 * nq:(k0 + nk_) * nq].rearrange("p (k q) -> p k q", k=nk_)
                            T.op("dve", lambda e: e.scalar_tensor_tensor(out=tv, in0=pv_, scalar=0.125, in1=btab[:, 8 * g + h, k0:k0 + nk_, qsl],
                                 op0=ALU.mult, op1=ALU.add), reads=[pn, "btab"], writes=[stn])
                            T.op("act", lambda e: e.activation(out=sv, in_=tv, func=AF.Exp), reads=[stn], writes=[smn])
                            if isinstance(ocol, tuple):
                                _, r_, d_ = ocol
                                no = num[pb:pb + 64, :].rearrange("p (u r) -> p r u", r=d_)[:, r_, :]
                                do = den[pb:pb + 64, :].rearrange("p (u r) -> p r u", r=d_)[:, r_, :]
                            else:
                                no = num[pb:pb + 64, ocol]
                                do = den[pb:pb + 64, ocol]
                            for (ktv, vv, kind) in kb:
                                mm(no, vv, sm[:, kind * nq:(kind + 1) * nq], first[pb], False, ["V%d" % (g + 1), smn], ["psnum"], tile_position=(0, pb))
                                mm(do, onesb[:, 0:64], sm[:, kind * nq:(kind + 1) * nq], first[pb], False, ["cstb", smn], ["psden"], tile_position=(0, pb))
                                first[pb] = False
                    T.op("dve", lambda e: e.reciprocal(out=rden[:], in_=den[:, :]), reads=["psden"], writes=["rden"])
                    T.op("dve", lambda e: e.tensor_tensor(out=aT[:, c, :], in0=num[:, :], in1=rden[:], op=ALU.mult), reads=["psnum", "rden"], writes=["aT"])
                T.op("act", lambda e: e.activation(out=KT1[:, :, 0:128], in_=KT1[:, :, 512:640], func=AF.Copy), reads=["KT1"], writes=["KT1"])
                T.op("act", lambda e: e.activation(out=V1[:, 0, :], in_=V1[:, 4, :], func=AF.Copy), reads=["V1"], writes=["V1"])

                chk(9)
                T.alias(["QT"], ["boT"])
                tri = cst[0:64, C_TRI:C_TRI + 64]
                ones64 = cst[0:64, C_ONE:C_ONE + 64]
                sel63 = cst[0:64, C_SEL63:C_SEL63 + 128]
                cneg = cst[0:64, C_CNEG:C_CNEG + 256]
                id64 = cst[0:64, C_ID:C_ID + 64]
                for ch in range(8):
                    c0 = ch * 64
                    pb_, pbn = npb()
                    for j in range(8):
                        T.op("pe", lambda e: e.transpose(out=pb_[0:64, j * 128:(j + 1) * 128], in_=kcT[:, j, c0:c0 + 64], identity=identb),
                             reads=["kcT", "cstb"], writes=[pbn])
                    T.op("act", lambda e: e.activation(out=ktok[:], in_=pb_[0:64, :], func=AF.Copy), reads=[pbn], writes=["ktok"])
                    p, pn = nps()
                    mm(p[0:64, 0:4], tri, lft[:, ch, :], True, True, ["cst", "lft"], [pn])
                    T.op("dve", lambda e: e.tensor_copy(out=b_sb[:], in_=p[0:64, 0:4]), reads=[pn], writes=["b_sb"])
                    T.op("dve", lambda e: e.tensor_tensor(out=g_sb[:], in0=igt[:, ch, :], in1=b_sb[:], op=ALU.subtract), reads=["igt", "b_sb"], writes=["g_sb"])
                    for h in range(4):
                        T.op("dve", lambda e: e.tensor_scalar(out=dg[:, h * 64:(h + 1) * 64], in0=id64, scalar1=g_sb[:, h:h + 1], scalar2=None, op0=ALU.mult),
                             reads=["cst", "g_sb"], writes=["dg"])
                    p, pn = nps()
                    mm(p[0:64, 0:256], ones64, dg[:], True, True, ["cst", "dg"], [pn])
                    T.op("dve", lambda e: e.tensor_tensor(out=Lm[:], in0=p[0:64, 0:256], in1=cneg, op=ALU.add), reads=[pn, "cst"], writes=["Lm"])
                    mmx, iw_, mt_ = stat[:, 0:4], stat[:, 4:8], stat[:, 8:12]
                    T.op("dve", lambda e: e.tensor_reduce(out=small[:, 0:4], in_=Lm[:].rearrange("p (h s) -> p h s", h=4), axis=AX.X, op=ALU.max),
                         reads=["Lm"], writes=["small"])
                    T.op("dve", lambda e: e.tensor_tensor(out=mmx, in0=small[:, 0:4], in1=mprev[0:64, :], op=ALU.max), reads=["small", "mprev"], writes=["stat"])
                    T.op("dve", lambda e: e.tensor_scalar(out=small[:, 4:8], in0=mmx, scalar1=-1.0, scalar2=None, op0=ALU.mult), reads=["stat"], writes=["small"])
                    for h in range(4):
                        T.op("act", lambda e: e.activation(out=dw[:, h * 64:(h + 1) * 64], in_=Lm[:, h * 64:(h + 1) * 64], func=AF.Exp, bias=small[:, 4 + h:5 + h]),
                             reads=["Lm", "small"], writes=["Lm"])
                    T.op("dve", lambda e: e.tensor_tensor(out=small[:, 8:12], in0=mprev[0:64, :], in1=mmx, op=ALU.subtract), reads=["mprev", "stat"], writes=["small"])
                    T.op("act", lambda e: e.activation(out=iw_, in_=small[:, 8:12], func=AF.Exp), reads=["small"], writes=["stat"])
                    T.op("dve", lambda e: e.tensor_tensor(out=mt_, in0=b_sb[:], in1=mmx, op=ALU.add), reads=["b_sb", "stat"], writes=["stat"])
                    T.op("act", lambda e: e.activation(out=small[:, 12:16], in_=mt_, func=AF.Exp, scale=-1.0), reads=["stat"], writes=["small"])
                    p, pn = nps()
                    mm(p[:, 0:12], sel63, stat[:, 0:12], True, True, ["cst", "stat"], [pn])
                    T.op("dve", lambda e: e.tensor_copy(out=bc[:], in_=p[:, 0:12]), reads=[pn], writes=["bc"])
                    T.op("dve", lambda e: e.tensor_tensor(out=wc[:], in0=g_sb[:], in1=bc[0:64, 0:4], op=ALU.subtract), reads=["g_sb", "bc"], writes=["wc"])
                    T.op("act", lambda e: e.activation(out=wc[:], in_=wc[:], func=AF.Exp), reads=["wc"], writes=["wc"])
                    T.op("dve", lambda e: e.tensor_copy(out=mprev[:], in_=bc[:, 8:12]), reads=["bc"], writes=["mprev"])
                    p, pn = nps()
                    for h in range(4):
                        for jj in range(2):
                            j = 2 * h + jj
                            mm(p[0:64, h * 64:(h + 1) * 64], qcT[:, j, c0:c0 + 64], kcT[:, j, c0:c0 + 64], (h == 0 and jj == 0), jj == 1, ["qcT", "kcT"], [pn])
                    T.op("dve", lambda e: e.tensor_tensor(out=Pbf[:], in0=p[0:64, 0:256], in1=dw[:], op=ALU.mult), reads=[pn, "Lm"], writes=["Pbf"])
                    pb_, pbn = npb()
                    for h in range(4):
                        T.op("pe", lambda e: e.transpose(out=pb_[0:64, h * 64:(h + 1) * 64], in_=Pbf[:, h * 64:(h + 1) * 64], identity=identb[0:64, 0:64]),
                             reads=["Pbf", "cstb"], writes=[pbn])
                    T.op("act", lambda e: e.activation(out=PTs[:], in_=pb_[0:64, 0:256], func=AF.Copy), reads=[pbn], writes=["PTs"])
                    for h in range(4):
                        pI, pIn = nps()
                        for jj in range(2):
                            mm(pI[0:64, 0:257], qcT[:, 2 * h + jj, c0:c0 + 64], Cb[:, jj, h, 0:257], jj == 0, jj == 1, ["qcT", "Cb"], [pIn])
                        pA, pAn = nps()
                        mm(pA[0:64, 0:257], PTs[:, h * 64:(h + 1) * 64], vaug[:, ch, h, 0:257], True, True, ["PTs", "vaug"], [pAn])
                        T.op("act", lambda e: e.activation(out=intra[:], in_=pA[0:64, 0:257], func=AF.Copy), reads=[pAn], writes=["intra"])
                        T.op("dve", lambda e: e.scalar_tensor_tensor(out=nd[:], in0=pI[0:64, 0:257], scalar=stat[:, 4 + h:5 + h], in1=intra[:],
                             op0=ALU.mult, op1=ALU.add), reads=[pIn, "stat", "intra"], writes=["intra"])
                        T.op("act", lambda e: e.activation(out=nd[:, 256:257], in_=nd[:, 256:257], func=AF.Abs), reads=["intra"], writes=["intra"])
                        T.op("dve", lambda e: e.tensor_tensor(out=nd[:, 256:257], in0=nd[:, 256:257], in1=small[:, 12 + h:13 + h], op=ALU.max),
                             reads=["intra", "small"], writes=["intra"])
                        T.op("dve", lambda e: e.reciprocal(out=nd[:, 256:257], in_=nd[:, 256:257]), reads=["intra"], writes=["intra"])
                        T.op("dve", lambda e: e.tensor_scalar(out=bo[:, h * 256:(h + 1) * 256], in0=nd[:, 0:256], scalar1=nd[:, 256:257],
                             scalar2=None, op0=ALU.mult), reads=["intra"], writes=["bo"])
                    for h in range(4):
                        T.op("dve", lambda e: e.tensor_scalar(out=ktok[:, h * 256:(h + 1) * 256], in0=ktok[:, h * 256:(h + 1) * 256], scalar1=wc[:, h:h + 1],
                             scalar2=None, op0=ALU.mult), reads=["ktok", "wc"], writes=["ktok"])
                    for h in range(4):
                        for jj in range(2):
                            pC, pCn = nps()
                            mm(pC[:, 0:257], ktok[:, h * 256 + jj * 128:h * 256 + (jj + 1) * 128], vaug[:, ch, h, 0:257], True, True, ["ktok", "vaug"], [pCn])
                            T.op("dve", lambda e: e.scalar_tensor_tensor(out=Cf[:, jj, h, :], in0=Cf[:, jj, h, :], scalar=bc[:, 4 + h:5 + h], in1=pC[:, 0:257],
                                 op0=ALU.mult, op1=ALU.add), reads=["Cf", "bc", pCn], writes=["Cf"])
                            T.op("act", lambda e: e.activation(out=Cb[:, jj, h, 0:257], in_=Cf[:, jj, h, :], func=AF.Copy), reads=["Cf"], writes=["Cb"])
                    pb_, pbn = npb()
                    for j in range(8):
                        T.op("pe", lambda e: e.transpose(out=pb_[:, j * 64:(j + 1) * 64], in_=bo[:, j * 128:(j + 1) * 128], identity=identb[0:64, 0:64]),
                             reads=["bo", "cstb"], writes=[pbn])
                    T.op("dve", lambda e: e.tensor_tensor(out=boT[:, :, c0:c0 + 64], in0=pb_[:, 0:512].rearrange("p (j t) -> p j t", j=8),
                         in1=ogT[:, :, c0:c0 + 64], op=ALU.mult), reads=[pbn, "ogT"], writes=["boT"])
                if last:
                    for h in range(4):
                        for jj in range(2):
                            T.dma("pool", o_Cp[l, h, jj * 128:(jj + 1) * 128, :], Cf[:, jj, h, 0:256], reads=["Cf"], writes=["oC"])
                    T.dma("pool", o_np[l].rearrange("p (j h) -> p j h", j=2), Cf[:, :, :, 256], reads=["Cf"], writes=["on"], slow=True)
                    T.dma("pool", o_mp[l:l + 1, :], mprev[0:1, :], reads=["mprev"], writes=["om"])
                    T.dma("pool", o_mcp[l], carry[:].rearrange("p c r -> p (c r)"), reads=["carry"], writes=["omc"])

                chk(10)
                T.alias(["QT", "qcT", "kcT"], ["sgT", "mT"])
                for gi, which in enumerate(("ga", "gb")):
                    for i2 in range(2):
                        wv, wn = ws.get(l, SIDX[(which, i2)])
                        for c in range(4):
                            p, pn = fm_chunk(wv, wn, c, hT, "hT")
                            T.op("act", lambda e: e.activation(out=sgT[:, 8 * gi + 4 * i2 + c, :], in_=p[:, :], func=AF.Sigmoid), reads=[pn], writes=["sgT"])
                wv, wn = ws.get(l, SIDX[("pa", 0)])
                for cc in range(8):
                    p, pn = nps()
                    for kc in range(4):
                        mm(p[:, :], wv[:, kc, cc * 128:(cc + 1) * 128], aT[:, kc, :], kc == 0, kc == 3, [wn, "aT"], [pn])
                    T.op("dve", lambda e: e.tensor_tensor(out=mT[:, cc, :], in0=p[:, :], in1=sgT[:, cc, :], op=ALU.mult), reads=[pn, "sgT"], writes=["mT"])
                for i2 in range(2):
                    wv, wn = ws.get(l, SIDX[("pb", i2)])
                    for c in range(4):
                        cc = 4 * i2 + c
                        p, pn = fm_chunk(wv, wn, c, boT, "boT")
                        T.op("dve", lambda e: e.tensor_tensor(out=mgA[:], in0=p[:, :], in1=sgT[:, 8 + cc, :], op=ALU.mult), reads=[pn, "sgT"], writes=["sm0"])
                        T.op("dve", lambda e: e.tensor_tensor(out=mT[:, cc, :], in0=mT[:, cc, :], in1=mgA[:], op=ALU.add), reads=["mT", "sm0"], writes=["mT"])
                for i2 in range(2):
                    wv, wn = ws.get(l, SIDX[("wo", i2)])
                    for s in range(4):
                        p, pn = nps()
                        for kc in range(8):
                            mm(p[:, :], mT[:, kc, s * 128:(s + 1) * 128], wv[:, kc, :], kc == 0, kc == 7, [wn, "mT"], [pn])
                        T.op("dve", lambda e: e.tensor_tensor(out=xt[:, s, 512 * i2:512 * i2 + 512], in0=xt[:, s, 512 * i2:512 * i2 + 512], in1=p[:, :], op=ALU.add),
                             reads=[pn, "xt%d" % s], writes=["xt%d" % s])

                chk(11)
                norm_T(None)
                T.alias(["boT", "sgT", "QT", "qcT", "kcT"], ["actT"])
                for i2 in range(11):
                    wv, wn = ws.get(l, SIDX[("up", i2)])
                    for jc in range(2):
                        j = 2 * i2 + jc
                        for hf in range(2):
                            c = 2 * hf + jc
                            p, pn = fm_chunk(wv, wn, c, hT, "hT")
                            pr, fcn = "pre%d" % hf, "acc%d" % hf
                            T.op("dve", lambda e: e.tensor_copy(out=fpre[:, hf, 0:2], in_=fcar[:, hf, j, :]), reads=["fcar"], writes=[pr])
                            T.op("act", lambda e: e.activation(out=fpre[:, hf, 2:514], in_=p[:, :], func=AF.Copy), reads=[pn], writes=[pr])
                            T.op("dve", lambda e: e.tensor_copy(out=fcar[:, hf, j, :], in_=fpre[:, hf, 512:514]), reads=[pr], writes=["fcar"])
                            T.op("dve", lambda e: e.tensor_scalar(out=fc[:, hf, :], in0=fpre[:, hf, 0:512], scalar1=fw[:, hf, j, 0:1], scalar2=fb[:, hf, j:j + 1],
                                 op0=ALU.mult, op1=ALU.add), reads=[pr, "fw", "fb"], writes=[fcn])
                            for tp in range(1, 3):
                                T.op("dve", lambda e: e.scalar_tensor_tensor(out=fc[:, hf, :], in0=fpre[:, hf, tp:tp + 512], scalar=fw[:, hf, j, tp:tp + 1],
                                     in1=fc[:, hf, :], op0=ALU.mult, op1=ALU.add), reads=[pr, "fw", fcn], writes=[fcn])
                        T.op("act", lambda e: e.activation(out=ft[:], in_=fc[:, 0, :], func=AF.Square), reads=["acc0"], writes=["rden"])
                        T.op("dve", lambda e: e.tensor_scalar(out=ft[:], in0=ft[:], scalar1=0.044715, scalar2=1.0, op0=ALU.mult, op1=ALU.add), reads=["rden"], writes=["rden"])
                        T.op("dve", lambda e: e.tensor_tensor(out=ft[:], in0=ft[:], in1=fc[:, 0, :], op=ALU.mult), reads=["rden", "acc0"], writes=["rden"])
                        T.op("act", lambda e: e.activation(out=ft[:], in_=ft[:], func=AF.Sigmoid, scale=1.5957691216057308), reads=["rden"], writes=["rden"])
                        T.op("dve", lambda e: e.tensor_tensor(out=ft[:], in0=ft[:], in1=fc[:, 0, :], op=ALU.mult), reads=["rden", "acc0"], writes=["rden"])
                        T.op("dve", lambda e: e.tensor_tensor(out=actT[:, j, :], in0=ft[:], in1=fc[:, 1, :], op=ALU.mult), reads=["rden", "acc1"], writes=["actT"])
                if last:
                    T.dma("pool", o_fcp[l], fcar[:].rearrange("p h j r -> p (h j r)"), reads=["fcar"], writes=["ofc"])
                for i2 in range(8):
                    wv, wn = ws.get(l, SIDX[("dn", i2)])
                    for s in range(4):
                        p, pn = nps()
                        for kc in range(22):
                            mm(p[:, 0:128], actT[:, kc, s * 128:(s + 1) * 128], wv[:, kc, :], kc == 0, kc == 21, [wn, "actT"], [pn])
                        T.op("dve", lambda e: e.tensor_tensor(out=xt[:, s, 128 * i2:128 * i2 + 128], in0=xt[:, s, 128 * i2:128 * i2 + 128], in1=p[:, 0:128], op=ALU.add),
                             reads=[pn, "xt%d" % s], writes=["xt%d" % s])
                chk(12)
                if l == 0:
                    for s in range(4):
                        T.dma("pool", xscr[t0 + s * 128:t0 + (s + 1) * 128, :], xt[:, s, :], reads=["xt%d" % s], writes=["xscr"])
                else:
                    T.op("dve", lambda e: e.memset(ss[:], 0.0), writes=["ss"])
                    for s in range(4):
                        T.op("act", lambda e: e.activation(out=junk[:], in_=xt[:, s, :], func=AF.Square, accum_out=ss[:, s:s + 1]),
                             reads=["xt%d" % s, "ss"], writes=["hb", "ss"])
                    T.op("act", lambda e: e.activation(out=rs[:], in_=ss[:], func=AF.Ln, scale=1.0 / D, bias=cst[:, C_EPS:C_EPS + 1]), reads=["ss", "cst"], writes=["rs"])
                    T.op("act", lambda e: e.activation(out=rs[:], in_=rs[:], func=AF.Exp, scale=-0.5), reads=["rs"], writes=["rs"])
                    for s in range(4):
                        if s == 0:
                            T.dma("pool", pre[:, :, 0:512], fin_g[0:1, :].broadcast_to([128, D]).rearrange("p (a b) -> p a b", a=2), writes=["pre0", "pre1"])
                        T.op("dve", lambda e: e.scalar_tensor_tensor(out=acc[:], in0=xt[:, s, :].rearrange("p (a b) -> p a b", a=2), scalar=rs[:, s:s + 1],
                             in1=pre[:, :, 0:512], op0=ALU.mult, op1=ALU.mult), reads=["xt%d" % s, "rs", "pre0", "pre1"], writes=["acc0", "acc1"])
                        T.dma("pool", o_yp[t0 + s * 128:t0 + (s + 1) * 128, :].rearrange("p (a b) -> p a b", a=2), acc[:], reads=["acc0", "acc1"], writes=["oy"])
        T.barrier()
    except StopBuild:
        T.barrier()
        return nc


    if do_sample:
      try:
        ws.extend([(l, si) for l in range(DEPTH) for si in range(NSLAB)])
        SIDX = {}
        for si, sdef in enumerate(SLABS):
            SIDX[(sdef[0], sdef[1])] = si

        def mm(out, lhsT, rhs, start, stop, reads, writes, **kw):
            T.op("pe", lambda e: e.matmul(out, lhsT, rhs, start=start, stop=stop, skip_group_check=True, **kw), reads=reads, writes=writes)

        sctr = [0]

        def nps2():
            i = 4 + (sctr[0] % 2)
            sctr[0] += 1
            return psum[i], "ps%d" % i

        with contextlib.ExitStack() as esx:
            xs_t = esx.enter_context(nc.sbuf_tensor("xs_t", [4, D], F32))
            cst2 = esx.enter_context(nc.sbuf_tensor("cst2", [128, 528], F32))
            s_ss = esx.enter_context(nc.sbuf_tensor("s_ss", [4, 2], F32))
            s_hb = esx.enter_context(nc.sbuf_tensor("s_hb", [4, D], BF16))
            s_hT = esx.enter_context(nc.sbuf_tensor("s_hT", [128, 8, 4], BF16))
            s_mT = esx.enter_context(nc.sbuf_tensor("s_mT", [128, 22, 4], BF16))
            s_jk = esx.enter_context(nc.sbuf_tensor("s_jk", [4, D], F32))
            T.dma("pool", xs_t[:], xs[:, :], writes=["xs_t"])
            T.dma("pool", cst2[:], consts2[:, :], writes=["cst2"])
            selrow = lambda b: cst2[0:4, 128 * b:128 * (b + 1)]
            selcol = lambda b: cst2[:, 512 + 4 * b:512 + 4 * (b + 1)]
            oh4 = cst2[0:4, 512:516]
            identf4 = cst[0:4, C_ID:C_ID + 4]

            def s_norm(dstT, nm):
                T.op("dve", lambda e: e.memset(s_ss[:], 0.0), writes=["s_ss"])
                T.op("act", lambda e: e.activation(out=s_jk[:], in_=xs_t[:], func=AF.Square, accum_out=s_ss[:, 0:1]), reads=["xs_t", "s_ss"], writes=["s_jk", "s_ss"])
                T.op("act", lambda e: e.activation(out=s_ss[:, 1:2], in_=s_ss[:, 0:1], func=AF.Ln, scale=1.0 / D, bias=cst[0:4, C_EPS:C_EPS + 1]), reads=["s_ss", "cst"], writes=["s_ss"])
                T.op("act", lambda e: e.activation(out=s_ss[:, 1:2], in_=s_ss[:, 1:2], func=AF.Exp, scale=-0.5), reads=["s_ss"], writes=["s_ss"])
                T.op("dve", lambda e: e.tensor_scalar(out=s_hb[:], in0=xs_t[:], scalar1=s_ss[:, 1:2], scalar2=None, op0=ALU.mult), reads=["xs_t", "s_ss"], writes=["s_hb"])
                tr_bf(s_hb, "s_hb", 8, dstT, nm)

            def tr_bf(src, sname, nchunk, dstT, dname):
                pb_, pbn = npb()
                for kc in range(nchunk):
                    T.op("pe", lambda e: e.transpose(out=pb_[:, kc * 4:(kc + 1) * 4], in_=src[0:4, kc * 128:(kc + 1) * 128], identity=identb[0:4, 0:4]),
                         reads=[sname, "cstb"], writes=[pbn])
                T.op("act", lambda e: e.activation(out=dstT[:, 0:nchunk, :], in_=pb_[:, 0:nchunk * 4].rearrange("p (k t) -> p k t", k=nchunk), func=AF.Copy),
                     reads=[pbn], writes=[dname])

            def proj4(wv, wn, srcT, sname, nk, ncols, c0=0):
                p, pn = nps2()
                for kc in range(nk):
                    mm(p[0:4, 0:ncols], srcT[:, kc, 0:4], wv[:, kc, c0:c0 + ncols], kc == 0, kc == nk - 1, [wn, sname], [pn])
                return p, pn

            for l in range(DEPTH):
                with contextlib.ExitStack() as esm:
                    def sbm(name, shape, dt=F32):
                        return esm.enter_context(nc.sbuf_tensor(name + "_L%d" % l, list(shape), dt))
                    qa4 = sbm("qa4", [4, 1536]); ka4 = sbm("ka4", [4, 1536]); va4 = sbm("va4", [4, 1536])
                    qkpre4 = sbm("qkpre4", [4, 2048]); qk4 = sbm("qk4", [4, 2048]); tmp4 = sbm("tmp4", [4, 2048])
                    vaug4 = sbm("vaug4", [4, 4, 257]); og4 = sbm("og4", [4, D]); sg4 = sbm("sg4", [4, 2048])
                    gt4 = sbm("gt4", [4, 8]); gb4 = sbm("gb4", [4, 8])
                    bs_t = sbm("bs_t", [128, 24]); b0_t = sbm("b0_t", [4, 24])
                    p0 = sbm("p0", [4, 24]); sm4 = sbm("sm4", [4, 64])

                    T.dma("pool", gb4[:], mgate_b[l:l + 1, :].broadcast_to([4, 8]), writes=["gb4"])
                    T.dma("pool", bs_t[:], bias_s[:, :], writes=["bs_t"])
                    T.dma("pool", b0_t[:], bias_0[:, :], writes=["b0_t"])
                    T.dma("pool", sm4[:, 0:4], st_m[l], writes=["sm4"])
                    T.op("dve", lambda e: e.memset(vaug4[:], 1.0), writes=["vaug4"])

                    s_norm(s_hT, "s_hT")
                    for kind, dst, dn_ in (("qa", qa4, "qa4"), ("ka", ka4, "ka4"), ("va", va4, "va4")):
                        for i in range(3):
                            wv, wn = ws.get(l, SIDX[(kind, i)])
                            p, pn = proj4(wv, wn, s_hT, "s_hT", 8, 512)
                            T.op("act", lambda e: e.activation(out=dst[:, 512 * i:512 * (i + 1)], in_=p[0:4, 0:512], func=AF.Copy), reads=[pn], writes=[dn_])
                    for kind, off in (("qb", 0), ("kb", 1024)):
                        for i in range(2):
                            wv, wn = ws.get(l, SIDX[(kind, i)])
                            p, pn = proj4(wv, wn, s_hT, "s_hT", 8, 512)
                            T.op("act", lambda e: e.activation(out=qkpre4[:, off + 512 * i:off + 512 * (i + 1)], in_=p[0:4, 0:512], func=AF.Copy), reads=[pn], writes=["qkpre4"])
                    for i in range(2):
                        wv, wn = ws.get(l, SIDX[("vb", i)])
                        p, pn = proj4(wv, wn, s_hT, "s_hT", 8, 512)
                        T.op("act", lambda e: e.activation(out=vaug4[:, 2 * i:2 * i + 2, 0:256], in_=p[0:4, 0:512].rearrange("p (h v) -> p h v", h=2), func=AF.Copy),
                             reads=[pn], writes=["vaug4"])
                    for i in range(2):
                        wv, wn = ws.get(l, SIDX[("ob", i)])
                        p, pn = proj4(wv, wn, s_hT, "s_hT", 8, 512)
                        T.op("act", lambda e: e.activation(out=og4[:, 512 * i:512 * (i + 1)], in_=p[0:4, 0:512], func=AF.Sigmoid), reads=[pn], writes=["og4"])
                    wv, wn = ws.get(l, SIDX[("gt", 0)])
                    p, pn = proj4(wv, wn, s_hT, "s_hT", 8, 8)
                    T.op("dve", lambda e: e.tensor_tensor(out=gt4[:], in0=p[0:4, 0:8], in1=gb4[:], op=ALU.add), reads=[pn, "gb4"], writes=["gt4"])
                    for gi, kind in enumerate(("ga", "gb")):
                        for i in range(2):
                            wv, wn = ws.get(l, SIDX[(kind, i)])
                            p, pn = proj4(wv, wn, s_hT, "s_hT", 8, 512)
                            T.op("act", lambda e: e.activation(out=sg4[:, 1024 * gi + 512 * i:1024 * gi + 512 * (i + 1)], in_=p[0:4, 0:512], func=AF.Sigmoid),
                                 reads=[pn], writes=["sg4"])
                    chk(20)
                    for g in range(3):
                        T.dma("pool", o_kvs[g][l, :, 0:512], ka4[:, 512 * g:512 * (g + 1)], reads=["ka4"], writes=["okvs"])
                        T.dma("pool", o_kvs[g][l, :, 512:1024], va4[:, 512 * g:512 * (g + 1)], reads=["va4"], writes=["okvs"])

                    chk(21)
                    with contextlib.ExitStack() as esc:
                        stm = esc.enter_context(nc.sbuf_tensor("stm_L%d" % l, [4, 3, 2048], F32))
                        cw4 = esc.enter_context(nc.sbuf_tensor("cw4_L%d" % l, [4, 4, 2048], F32))
                        cb4 = esc.enter_context(nc.sbuf_tensor("cb4_L%d" % l, [4, 2048], F32))
                        T.dma("pool", stm[:], st_mconv[l], writes=["stm"])
                        T.dma("pool", cw4[:], mconv_w[l:l + 1].broadcast_to([4, 4, 2048]), writes=["cw4"])
                        T.dma("pool", cb4[:], mconv_b[l:l + 1, :].broadcast_to([4, 2048]), writes=["cb4"])
                        T.op("dve", lambda e: e.tensor_tensor(out=qk4[:], in0=qkpre4[:], in1=cw4[:, 3, :], op=ALU.mult), reads=["qkpre4", "cw4"], writes=["qk4"])
                        T.op("dve", lambda e: e.tensor_tensor(out=qk4[:], in0=qk4[:], in1=cb4[:], op=ALU.add), reads=["qk4", "cb4"], writes=["qk4"])
                        for i in range(3):
                            T.op("dve", lambda e: e.tensor_tensor(out=tmp4[:], in0=stm[:, i, :], in1=cw4[:, i, :], op=ALU.mult), reads=["stm", "cw4"], writes=["tmp4"])
                            T.op("dve", lambda e: e.tensor_tensor(out=qk4[:], in0=qk4[:], in1=tmp4[:], op=ALU.add), reads=["qk4", "tmp4"], writes=["qk4"])
                        T.op("act", lambda e: e.activation(out=qk4[:], in_=qk4[:], func=AF.Silu), reads=["qk4"], writes=["qk4"])
                        T.op("dve", lambda e: e.tensor_scalar(out=qk4[:, 1024:2048], in0=qk4[:, 1024:2048], scalar1=1.0 / 16.0, scalar2=None, op0=ALU.mult), reads=["qk4"], writes=["qk4"])

                        T.barrier()
                    ckv = [sbm("ckv%d" % i, [128, 1024]) for i in range(2)]
                    prod = sbm("prod", [128, 512]); pv = sbm("pv", [128, 512])
                    sT = sbm("sT", [128, 8]); pT = sbm("pT", [128, 8])
                    nv = sbm("nv", [4, 1536]); numt = sbm("numt", [4, 512]); dent = sbm("dent", [4, 8])
                    a4b = sbm("a4b", [4, 512], BF16); aT4 = sbm("aT4", [128, 4, 4], BF16)
                    qT4f = sbm("qT4f", [128, 8, 4]); qTm = sbm("qTm", [128, 4, 8, 4])
                    Ct = [sbm("Ct%d" % i, [128, 2, 257]) for i in range(2)]
                    Cn = [sbm("Cn%d" % i, [128, 2, 257]) for i in range(2)]
                    vm = sbm("vm", [4, 4, 4, 257]); ibc = sbm("ibc", [128, 4, 4])
                    hq = sbm("hq", [4, 4, 257]); bo4b = sbm("bo4b", [4, D], BF16); boT4 = sbm("boT4", [128, 8, 4], BF16)
                    mg4 = sbm("mg4", [4, D]); mg4b = sbm("mg4b", [4, D], BF16)
                    T.op("dve", lambda e: e.memset(qTm[:], 0.0), writes=["qTm"])
                    chk(22)
                    psO, psD = psum[0], psum[1]
                    firstO = True
                    it = 0
                    for b in range(4):
                        for g in range(3):
                            ck = ckv[it % 2]
                            ckn = "ckv%d" % (it % 2)
                            it += 1
                            T.dma("pool", ck[:], cache[g][l, b], writes=[ckn])
                            pq, pqn = nps2()
                            mm(pq[:, 0:512], selrow(b), qa4[:, 512 * g:512 * (g + 1)], True, True, ["cst2", "qa4"], [pqn])
                            T.op("dve", lambda e: e.tensor_tensor(out=prod[:], in0=ck[:, 0:512], in1=pq[:, 0:512], op=ALU.mult), reads=[ckn, pqn], writes=["prod"])
                            T.op("dve", lambda e: e.tensor_reduce(out=sT[:], in_=prod[:].rearrange("p (h c) -> p h c", h=8), axis=AX.X, op=ALU.add),
                                 reads=["prod"], writes=["sT"])
                            T.op("dve", lambda e: e.scalar_tensor_tensor(out=sT[:], in0=sT[:], scalar=0.125, in1=bs_t[:, 8 * g:8 * (g + 1)], op0=ALU.mult, op1=ALU.add),
                                 reads=["sT", "bs_t"], writes=["sT"])
                            T.op("act", lambda e: e.activation(out=pT[:], in_=sT[:], func=AF.Exp), reads=["sT"], writes=["pT"])
                            for h in range(8):
                                T.op("dve", lambda e: e.tensor_scalar(out=pv[:, 64 * h:64 * (h + 1)], in0=ck[:, 512 + 64 * h:512 + 64 * (h + 1)], scalar1=pT[:, h:h + 1],
                                     scalar2=None, op0=ALU.mult), reads=[ckn, "pT"], writes=["pv"])
                            mm(psO[0:4, 0:512], selcol(b), pv[:], firstO, False, ["cst2", "pv"], ["ps0"])
                            mm(psD[0:4, 0:8], selcol(b), pT[:], firstO, False, ["cst2", "pT"], ["ps1"])
                            firstO = False
                    T.op("dve", lambda e: e.tensor_tensor(out=nv[:], in0=qa4[:], in1=ka4[:], op=ALU.mult), reads=["qa4", "ka4"], writes=["nv"])
                    T.op("dve", lambda e: e.tensor_reduce(out=p0[:], in_=nv[:].rearrange("p (h c) -> p h c", h=24), axis=AX.X, op=ALU.add), reads=["nv"], writes=["p0"])
                    T.op("dve", lambda e: e.scalar_tensor_tensor(out=p0[:], in0=p0[:], scalar=0.125, in1=b0_t[:], op0=ALU.mult, op1=ALU.add), reads=["p0", "b0_t"], writes=["p0"])
                    T.op("act", lambda e: e.activation(out=p0[:], in_=p0[:], func=AF.Exp), reads=["p0"], writes=["p0"])
                    for hh in range(24):
                        T.op("dve", lambda e: e.tensor_scalar(out=nv[:, 64 * hh:64 * (hh + 1)], in0=va4[:, 64 * hh:64 * (hh + 1)], scalar1=p0[:, hh:hh + 1], scalar2=None,
                             op0=ALU.mult), reads=["va4", "p0", "nv"], writes=["nv"])
                    T.op("dve", lambda e: e.tensor_tensor(out=numt[:], in0=nv[:, 0:512], in1=psO[0:4, 0:512], op=ALU.add), reads=["nv", "ps0"], writes=["numt"])
                    T.op("dve", lambda e: e.tensor_tensor(out=dent[:], in0=p0[:, 0:8], in1=psD[0:4, 0:8], op=ALU.add), reads=["p0", "ps1"], writes=["dent"])
                    for g in (1, 2):
                        T.op("dve", lambda e: e.tensor_tensor(out=numt[:], in0=numt[:], in1=nv[:, 512 * g:512 * (g + 1)], op=ALU.add), reads=["numt", "nv"], writes=["numt"])
                        T.op("dve", lambda e: e.tensor_tensor(out=dent[:], in0=dent[:], in1=p0[:, 8 * g:8 * (g + 1)], op=ALU.add), reads=["dent", "p0"], writes=["dent"])
                    T.op("dve", lambda e: e.reciprocal(out=dent[:], in_=dent[:]), reads=["dent"], writes=["dent"])
                    for h in range(8):
                        T.op("dve", lambda e: e.tensor_scalar(out=a4b[:, 64 * h:64 * (h + 1)], in0=numt[:, 64 * h:64 * (h + 1)], scalar1=dent[:, h:h + 1], scalar2=None,
                             op0=ALU.mult), reads=["numt", "dent"], writes=["a4b"])
                    tr_bf(a4b, "a4b", 4, aT4, "aT4")

                    chk(23)
                    T.dma("pool", o_mcs[l, :, 0:2, :], st_mconv[l, :, 1:3, :], writes=["omcs"])
                    T.dma("pool", o_mcs[l, :, 2, :], qkpre4[:], reads=["qkpre4"], writes=["omcs"])
                    m0, ig, bm, mt, dwv, iwv, emm, qkw, denv, tq = (sm4[:, 4 * i:4 * i + 4] for i in range(10))
                    T.op("dve", lambda e: e.tensor_copy(out=ig, in_=gt4[:, 0:4]), reads=["gt4"], writes=["sm4"])
                    T.op("act", lambda e: e.activation(out=bm, in_=gt4[:, 4:8], func=AF.Exp, scale=-1.0), reads=["gt4"], writes=["sm4"])
                    T.op("act", lambda e: e.activation(out=bm, in_=bm, func=AF.Ln, bias=cst[0:4, C_ONE:C_ONE + 1]), reads=["sm4", "cst"], writes=["sm4"])
                    T.op("dve", lambda e: e.scalar_tensor_tensor(out=bm, in0=bm, scalar=-1.0, in1=m0, op0=ALU.mult, op1=ALU.add), reads=["sm4"], writes=["sm4"])
                    T.op("dve", lambda e: e.tensor_tensor(out=mt, in0=bm, in1=ig, op=ALU.max), reads=["sm4"], writes=["sm4"])
                    T.op("dve", lambda e: e.tensor_tensor(out=dwv, in0=ig, in1=mt, op=ALU.subtract), reads=["sm4"], writes=["sm4"])
                    T.op("act", lambda e: e.activation(out=dwv, in_=dwv, func=AF.Exp), reads=["sm4"], writes=["sm4"])
                    T.op("dve", lambda e: e.tensor_tensor(out=iwv, in0=bm, in1=mt, op=ALU.subtract), reads=["sm4"], writes=["sm4"])
                    T.op("act", lambda e: e.activation(out=iwv, in_=iwv, func=AF.Exp), reads=["sm4"], writes=["sm4"])
                    T.op("act", lambda e: e.activation(out=emm, in_=mt, func=AF.Exp, scale=-1.0), reads=["sm4"], writes=["sm4"])
                    T.dma("pool", o_ms[l], mt, reads=["sm4"], writes=["oms"])
                    T.op("dve", lambda e: e.tensor_tensor(out=tmp4[:, 0:1024], in0=qk4[:, 0:1024], in1=qk4[:, 1024:2048], op=ALU.mult), reads=["qk4"], writes=["tmp4"])
                    T.op("dve", lambda e: e.tensor_reduce(out=qkw, in_=tmp4[:, 0:1024].rearrange("p (h c) -> p h c", h=4), axis=AX.X, op=ALU.add), reads=["tmp4"], writes=["sm4"])
                    T.op("dve", lambda e: e.tensor_tensor(out=qkw, in0=qkw, in1=dwv, op=ALU.mult), reads=["sm4"], writes=["sm4"])
                    pq, pqn = nps2()
                    for kc in range(8):
                        T.op("pe", lambda e: e.transpose(out=pq[:, kc * 4:(kc + 1) * 4], in_=qk4[0:4, kc * 128:(kc + 1) * 128], identity=identf4), reads=["qk4", "cst"], writes=[pqn])
                    T.op("act", lambda e: e.activation(out=qT4f[:], in_=pq[:, 0:32].rearrange("p (k t) -> p k t", k=8), func=AF.Copy), reads=[pqn], writes=["qT4f"])
                    for b in range(4):
                        T.op("dve", lambda e: e.tensor_copy(out=qTm[:, b, :, b], in_=qT4f[:, :, b]), reads=["qT4f", "qTm"], writes=["qTm"])
                    for b in range(4):
                        for h in range(4):
                            T.op("dve", lambda e: e.tensor_scalar(out=vm[:, b, h, :], in0=vaug4[:, h, :], scalar1=dwv[:, h:h + 1], scalar2=cst[0:4, C_ID + b:C_ID + b + 1],
                                 op0=ALU.mult, op1=ALU.mult), reads=["vaug4", "sm4", "cst", "vm"], writes=["vm"])
                    it = 0
                    for b in range(4):
                        pi, pin = nps2()
                        mm(pi[:, 0:4], selrow(b), iwv, True, True, ["cst2", "sm4"], [pin])
                        T.op("dve", lambda e: e.tensor_copy(out=ibc[:, b, :], in_=pi[:, 0:4]), reads=[pin], writes=["ibc%d" % b])
                        for h in range(4):
                            k2 = it % 2
                            it += 1
                            ctn, cnn = "Ct%d" % k2, "Cn%d" % k2
                            T.dma("pool", Ct[k2][:, :, 0:256], st_C[l, b, h].rearrange("(j p) v -> p j v", p=128), writes=[ctn])
                            T.dma("pool", Ct[k2][:, :, 256], st_n[l, b, h].rearrange("(j p) -> p j", p=128), writes=[ctn], slow=True)
                            for jj in range(2):
                                mm(psum[h][0:4, 0:257], qTm[:, b, 2 * h + jj, :], Ct[k2][:, jj, :], (b == 0 and jj == 0), False, ["qTm", ctn], ["ps%d" % h])
                            for jj in range(2):
                                pk, pkn = nps2()
                                mm(pk[:, 0:257], qk4[0:4, 1024 + h * 256 + jj * 128:1024 + h * 256 + (jj + 1) * 128], vm[:, b, h, :], True, True, ["qk4", "vm"], [pkn])
                                T.op("dve", lambda e: e.scalar_tensor_tensor(out=Cn[k2][:, jj, :], in0=Ct[k2][:, jj, :], scalar=ibc[:, b, h:h + 1], in1=pk[:, 0:257],
                                     op0=ALU.mult, op1=ALU.add), reads=[ctn, "ibc%d" % b, pkn], writes=[cnn])
                            T.dma("pool", o_Cs[l, b, h].rearrange("(j p) v -> p j v", p=128), Cn[k2][:, :, 0:256], reads=[cnn], writes=["oCs"])
                            T.dma("pool", o_ns[l, b, h].rearrange("(j p) -> p j", p=128), Cn[k2][:, :, 256], reads=[cnn], writes=["ons"], slow=True)
                    for h in range(4):
                        T.op("act", lambda e: e.activation(out=hq[:, h, :], in_=psum[h][0:4, 0:257], func=AF.Copy), reads=["ps%d" % h], writes=["hq"])
                    for h in range(4):
                        T.op("dve", lambda e: e.tensor_scalar(out=hq[:, h, :], in0=hq[:, h, :], scalar1=iwv[:, h:h + 1], scalar2=None, op0=ALU.mult), reads=["hq", "sm4"], writes=["hq"])
                        T.op("dve", lambda e: e.scalar_tensor_tensor(out=hq[:, h, :], in0=vaug4[:, h, :], scalar=qkw[:, h:h + 1], in1=hq[:, h, :], op0=ALU.mult, op1=ALU.add),
                             reads=["vaug4", "sm4", "hq"], writes=["hq"])
                        T.op("act", lambda e: e.activation(out=denv[:, h:h + 1], in_=hq[:, h, 256:257], func=AF.Abs), reads=["hq"], writes=["sm4"])
                    T.op("dve", lambda e: e.tensor_tensor(out=denv, in0=denv, in1=emm, op=ALU.max), reads=["sm4"], writes=["sm4"])
                    T.op("dve", lambda e: e.reciprocal(out=denv, in_=denv), reads=["sm4"], writes=["sm4"])
                    for h in range(4):
                        T.op("dve", lambda e: e.scalar_tensor_tensor(out=bo4b[:, 256 * h:256 * (h + 1)], in0=hq[:, h, 0:256], scalar=denv[:, h:h + 1], in1=og4[:, 256 * h:256 * (h + 1)],
                             op0=ALU.mult, op1=ALU.mult), reads=["hq", "sm4", "og4"], writes=["bo4b"])
                    tr_bf(bo4b, "bo4b", 8, boT4, "boT4")

                    chk(24)
                    wv, wn = ws.get(l, SIDX[("pa", 0)])
                    for i in range(2):
                        p, pn = proj4(wv, wn, aT4, "aT4", 4, 512, c0=512 * i)
                        T.op("dve", lambda e: e.tensor_tensor(out=mg4[:, 512 * i:512 * (i + 1)], in0=p[0:4, 0:512], in1=sg4[:, 512 * i:512 * (i + 1)], op=ALU.mult),
                             reads=[pn, "sg4"], writes=["mg4"])
                    for i in range(2):
                        wv, wn = ws.get(l, SIDX[("pb", i)])
                        p, pn = proj4(wv, wn, boT4, "boT4", 8, 512)
                        T.op("dve", lambda e: e.tensor_tensor(out=tmp4[:, 0:512], in0=p[0:4, 0:512], in1=sg4[:, 1024 + 512 * i:1024 + 512 * (i + 1)], op=ALU.mult),
                             reads=[pn, "sg4"], writes=["tmp4"])
                        T.op("dve", lambda e: e.tensor_tensor(out=mg4[:, 512 * i:512 * (i + 1)], in0=mg4[:, 512 * i:512 * (i + 1)], in1=tmp4[:, 0:512], op=ALU.add),
                             reads=["mg4", "tmp4"], writes=["mg4"])
                    T.op("dve", lambda e: e.tensor_copy(out=mg4b[:], in_=mg4[:]), reads=["mg4"], writes=["mg4b"])
                    tr_bf(mg4b, "mg4b", 8, s_mT, "s_mT")
                    for i in range(2):
                        wv, wn = ws.get(l, SIDX[("wo", i)])
                        p, pn = proj4(wv, wn, s_mT, "s_mT", 8, 512)
                        T.op("dve", lambda e: e.tensor_tensor(out=xs_t[:, 512 * i:512 * (i + 1)], in0=xs_t[:, 512 * i:512 * (i + 1)], in1=p[0:4, 0:512], op=ALU.add),
                             reads=[pn, "xs_t"], writes=["xs_t"])
                    T.barrier()
                chk(25)
                with contextlib.ExitStack() as esf:
                    def sbf(name, shape, dt=F32):
                        return esf.enter_context(nc.sbuf_tensor(name + "_L%d" % l, list(shape), dt))
                    u4 = sbf("u4", [4, 2 * DFF]); cv4 = sbf("cv4", [4, 2 * DFF]); tf4 = sbf("tf4", [4, DFF])
                    sfc = sbf("sfc", [4, 2, DFF]); fw4 = sbf("fw4", [4, 3, DFF]); fb4 = sbf("fb4", [4, DFF])
                    ac4b = sbf("ac4b", [4, DFF], BF16)
                    s_norm(s_hT, "s_hT")
                    for i in range(11):
                        wv, wn = ws.get(l, SIDX[("up", i)])
                        p, pn = proj4(wv, wn, s_hT, "s_hT", 8, 512)
                        T.op("act", lambda e: e.activation(out=u4[:, 256 * i:256 * (i + 1)], in_=p[0:4, 0:256], func=AF.Copy), reads=[pn], writes=["u4"])
                        T.op("act", lambda e: e.activation(out=u4[:, DFF + 256 * i:DFF + 256 * (i + 1)], in_=p[0:4, 256:512], func=AF.Copy), reads=[pn], writes=["u4"])
                    T.dma("pool", o_fcs[l, :, 0, :], st_fconv[l, :, 1, :], writes=["ofcs"])
                    T.dma("pool", o_fcs[l, :, 1, :], u4[:], reads=["u4"], writes=["ofcs"])
                    for hh in range(2):
                        hs = slice(hh * DFF, (hh + 1) * DFF)
                        T.dma("pool", sfc[:], st_fconv[l, :, :, hs], writes=["sfc"])
                        T.dma("pool", fw4[:], fconv_w[l:l + 1, :, hs].broadcast_to([4, 3, DFF]), writes=["fw4"])
                        T.dma("pool", fb4[:], fconv_b[l:l + 1, hs].broadcast_to([4, DFF]), writes=["fb4"])
                        T.op("dve", lambda e: e.tensor_tensor(out=cv4[:, hs], in0=u4[:, hs], in1=fw4[:, 2, :], op=ALU.mult), reads=["u4", "fw4"], writes=["cv4"])
                        T.op("dve", lambda e: e.tensor_tensor(out=cv4[:, hs], in0=cv4[:, hs], in1=fb4[:], op=ALU.add), reads=["cv4", "fb4"], writes=["cv4"])
                        for i in range(2):
                            T.op("dve", lambda e: e.tensor_tensor(out=tf4[:], in0=sfc[:, i, :], in1=fw4[:, i, :], op=ALU.mult), reads=["sfc", "fw4"], writes=["tf4"])
                            T.op("dve", lambda e: e.tensor_tensor(out=cv4[:, hs], in0=cv4[:, hs], in1=tf4[:], op=ALU.add), reads=["cv4", "tf4"], writes=["cv4"])
                    c1, c2, tt_ = cv4[:, 0:DFF], cv4[:, DFF:2 * DFF], tf4[:, 0:DFF]
                    T.op("act", lambda e: e.activation(out=tt_, in_=c1, func=AF.Square), reads=["cv4"], writes=["tf4"])
                    T.op("dve", lambda e: e.tensor_scalar(out=tt_, in0=tt_, scalar1=0.044715, scalar2=1.0, op0=ALU.mult, op1=ALU.add), reads=["tf4"], writes=["tf4"])
                    T.op("dve", lambda e: e.tensor_tensor(out=tt_, in0=tt_, in1=c1, op=ALU.mult), reads=["tf4", "cv4"], writes=["tf4"])
                    T.op("act", lambda e: e.activation(out=tt_, in_=tt_, func=AF.Sigmoid, scale=1.5957691216057308), reads=["tf4"], writes=["tf4"])
                    T.op("dve", lambda e: e.tensor_tensor(out=tt_, in0=tt_, in1=c1, op=ALU.mult), reads=["tf4", "cv4"], writes=["tf4"])
                    T.op("dve", lambda e: e.tensor_tensor(out=ac4b[:], in0=tt_, in1=c2, op=ALU.mult), reads=["tf4", "cv4"], writes=["ac4b"])
                    tr_bf(ac4b, "ac4b", 22, s_mT, "s_mT")
                    for i in range(8):
                        wv, wn = ws.get(l, SIDX[("dn", i)])
                        p, pn = proj4(wv, wn, s_mT, "s_mT", 22, 128)
                        T.op("dve", lambda e: e.tensor_tensor(out=xs_t[:, 128 * i:128 * (i + 1)], in0=xs_t[:, 128 * i:128 * (i + 1)], in1=p[0:4, 0:128], op=ALU.add),
                             reads=[pn, "xs_t"], writes=["xs_t"])
                    T.barrier()
            with contextlib.ExitStack() as esy:
                gf4 = esy.enter_context(nc.sbuf_tensor("gf4", [4, D], F32))
                T.dma("pool", gf4[:], fin_g[0:1, :].broadcast_to([4, D]), writes=["gf4"])
                T.op("dve", lambda e: e.memset(s_ss[:], 0.0), writes=["s_ss"])
                T.op("act", lambda e: e.activation(out=s_jk[:], in_=xs_t[:], func=AF.Square, accum_out=s_ss[:, 0:1]), reads=["xs_t", "s_ss"], writes=["s_jk", "s_ss"])
                T.op("act", lambda e: e.activation(out=s_ss[:, 1:2], in_=s_ss[:, 0:1], func=AF.Ln, scale=1.0 / D, bias=cst[0:4, C_EPS:C_EPS + 1]), reads=["s_ss", "cst"], writes=["s_ss"])
                T.op("act", lambda e: e.activation(out=s_ss[:, 1:2], in_=s_ss[:, 1:2], func=AF.Exp, scale=-0.5), reads=["s_ss"], writes=["s_ss"])
                T.op("dve", lambda e: e.scalar_tensor_tensor(out=s_jk[:], in0=xs_t[:], scalar=s_ss[:, 1:2], in1=gf4[:], op0=ALU.mult, op1=ALU.mult),
                     reads=["xs_t", "s_ss", "gf4"], writes=["s_jk"])
                T.dma("pool", o_ys[:, :], s_jk[:], reads=["s_jk"], writes=["oys"])
                T.barrier()
      except StopBuild:
        T.barrier()
        return nc

    T.barrier()
    es.close()
    return nc


def _bias_tables(rel_bias):
    k = np.arange(128)[:, None]
    q = np.arange(128)[None, :]
    bg = np.zeros((128, 24, 2, 128), np.float32)
    bm = np.zeros((128, 24, 2, 128), np.float32)
    for h in range(24):
        d = DIL[h // 8]
        rel0 = q + 128 - k
        rel1 = q - k
        for kind, rel in ((0, rel0), (1, rel1)):
            valid = (rel >= 0) & (rel <= 128)
            bk = t5_bucket(np.maximum(rel, 0) * d)
            bg[:, h, kind, :] = rel_bias[bk, h]
            bm[:, h, kind, :] = np.where(valid, 0.0, NEG)
    return bg.reshape(128, -1), bm.reshape(128, -1)


def _sample_bias(rel_bias):
    jj = np.arange(128)
    bs = np.zeros((128, 24), np.float32)
    b0 = np.zeros((4, 24), np.float32)
    for h in range(24):
        d = DIL[h // 8]
        bs[:, h] = rel_bias[t5_bucket((128 - jj) * d), h]
        b0[:, h] = rel_bias[0, h]
    return bs, b0


def make_in_maps(inp, ncores=8):
    f = lambda a: np.ascontiguousarray(a, dtype=np.float32)
    rel_bias = np.asarray(inp["rel_bias"], np.float32)
    bg, bm = _bias_tables(rel_bias)
    bs, b0 = _sample_bias(rel_bias)
    consts = make_consts()
    c2 = np.zeros((128, 528), np.float32)
    for b in range(4):
        c2[b, 128 * b:128 * (b + 1)] = 1.0
        c2[:, 512 + 4 * b + b] = 1.0
    shared = {
        "consts2": c2, "bias_g": bg, "bias_m": bm, "bias_s": bs, "bias_0": b0, "consts": consts,
        "w_in": f(inp["w_in"]), "w_pa": f(inp["w_pa"]), "w_pb": f(inp["w_pb"]), "w_o": f(inp["w_o"]),
        "w_up": f(inp["w_up"]), "w_down": f(inp["w_down"]),
        "norm1_g": f(inp["norm1_g"]), "norm2_g": f(inp["norm2_g"]),
        "mconv_w": f(inp["mconv_w"]), "mconv_b": f(inp["mconv_b"]),
        "mgate_b": f(np.asarray(inp["mgate_b"]).reshape(2, 8)),
        "fconv_w": f(inp["fconv_w"]), "fconv_b": f(inp["fconv_b"]),
        "fin_g": f(np.asarray(inp["final_norm_g"]).reshape(1, D)),
        "cw_t": f(np.asarray(inp["mconv_w"]).reshape(2, 4, 16, 128).transpose(0, 3, 2, 1).reshape(2, 128, 64)),
        "cb_t": f(np.asarray(inp["mconv_b"]).reshape(2, 16, 128).transpose(0, 2, 1)),
        "fw_t": f(np.asarray(inp["fconv_w"]).reshape(2, 3, 2, 22, 128).transpose(0, 4, 2, 3, 1).reshape(2, 128, 132)),
        "fb_t": f(np.asarray(inp["fconv_b"]).reshape(2, 2, 22, 128).transpose(0, 3, 1, 2).reshape(2, 128, 44)),
        "gn_t": f(np.stack([np.asarray(inp["norm1_g"]), np.asarray(inp["norm2_g"])], 0).reshape(2, 2, 8, 128).transpose(3, 0, 1, 2).reshape(128, 32)),
    }
    caches = [np.asarray(inp["cache_kv_w128"]), np.asarray(inp["cache_kv_w512"]), np.asarray(inp["cache_kv_w2048"])]
    maps = []
    for c in range(ncores):
        sl = slice(4 * c, 4 * c + 4)
        m = dict(shared)
        m["xp"] = f(np.asarray(inp["x_prompt"])[c % 4])
        m["xs"] = f(np.asarray(inp["x_sample"])[sl, 0, :])
        for g in range(3):
            m["cache%d" % g] = f(caches[g][:, sl, ::DIL[g]].reshape(2, 4, 128, 1024))
        m["st_mconv"] = f(np.asarray(inp["state_mlstm_conv"])[:, sl])
        m["st_C"] = f(np.asarray(inp["state_mlstm_C"])[:, sl])
        m["st_n"] = f(np.asarray(inp["state_mlstm_n"])[:, sl])
        m["st_m"] = f(np.asarray(inp["state_mlstm_m"])[:, sl])
        m["st_fconv"] = f(np.asarray(inp["state_ffn_conv"])[:, sl])
        maps.append(m)
    return maps


def assemble(results):
    R = results
    cat_s = lambda name, axis=1: np.concatenate([R[c][name] for c in range(8)], axis=axis)
    stk_p = lambda name: np.stack([R[b][name] for b in range(4)], axis=1)
    y_prompt = np.stack([R[b]["o_yp"] for b in range(4)], axis=0)
    y_sample = np.concatenate([R[c]["o_ys"] for c in range(8)], axis=0).reshape(32, 1, D)
    outs = [y_prompt, y_sample]
    for g in range(3):
        outs.append(stk_p("o_kvp%d" % g).reshape(2, 4, WIN[g], 2, 8, 64))
        outs.append(cat_s("o_kvs%d" % g).reshape(2, 32, 1, 2, 8, 64))
    outs.append(stk_p("o_mcp").reshape(2, 4, 128, 16, 3).transpose(0, 1, 4, 3, 2).reshape(2, 4, 3, 2048))
    outs.append(cat_s("o_mcs"))
    outs.append(stk_p("o_Cp"))
    outs.append(cat_s("o_Cs"))
    outs.append(stk_p("o_np").reshape(2, 4, 128, 2, 4).transpose(0, 1, 4, 3, 2).reshape(2, 4, 4, 256))
    outs.append(cat_s("o_ns"))
    outs.append(stk_p("o_mp"))
    outs.append(cat_s("o_ms"))
    outs.append(stk_p("o_fcp").reshape(2, 4, 128, 2, 22, 2).transpose(0, 1, 5, 3, 4, 2).reshape(2, 4, 2, 2 * DFF))
    outs.append(cat_s("o_fcs"))
    return tuple(np.ascontiguousarray(o, dtype=np.float32) for o in outs)


def kernel(**inputs):
    nc = build()
    maps = make_in_maps(inputs)
    res = run_bass_kernel_spmd(nc, maps, core_ids=list(range(8)))
    return assemble(res.results)
```
